# Optimizing a Trainium2 kernel written in Bass

```python
import jax, jax.numpy as jnp
from jax import lax
import numpy as np

D_MODEL = 2048
BATCH = 2
SEQ = 8192
DEPTH = 4

HEAD_DIM = 64
HALF_DIM = HEAD_DIM // 2
N_MIXERS = 4
N_HEADS_PER_MIXER = 8
MIX_WIDTH = N_MIXERS * N_HEADS_PER_MIXER * HEAD_DIM
IN_COLS = 4 * MIX_WIDTH + N_HEADS_PER_MIXER
N_QK_NORMED = 3
ROPE_THETA = 10000.0
RMS_EPS = 1e-6
Q_BLOCK = 128
DILATED_SEGMENTS = ((128, 1), (512, 4), (2048, 16))
MOBA_BLOCK = 256
MOBA_TOPK = 3
MOBA_Q_CHUNK = 64
NEG = -1e30
SCALE = HEAD_DIM ** -0.5

kernel_name = "hybrid_parallel_heads_dilated_stickbreak_moba_fox"


def rms_norm(x, g):
    xf = x.astype(jnp.float32)
    y = xf * lax.rsqrt(jnp.mean(xf * xf, axis=-1, keepdims=True) + RMS_EPS)
    return y * g.astype(jnp.float32)


def head_rms_norm(t, g):
    return t * lax.rsqrt(jnp.mean(t * t, axis=-1, keepdims=True) + RMS_EPS) * g.astype(jnp.float32)


def rope_tables(seq):
    inv = 1.0 / (ROPE_THETA ** (jnp.arange(0, HEAD_DIM, 2, dtype=jnp.float32) / HEAD_DIM))
    ang = jnp.arange(seq, dtype=jnp.float32)[:, None] * inv[None, :]
    return jnp.cos(ang), jnp.sin(ang)


def apply_rope(t, cos, sin):
    t1, t2 = t[..., :HALF_DIM], t[..., HALF_DIM:]
    return jnp.concatenate([t1 * cos - t2 * sin, t2 * cos + t1 * sin], axis=-1)


def banded_attention(q, k, v, span):
    L = q.shape[-2]
    lead = q.shape[:-2]
    nb = -(-L // Q_BLOCK)
    Lp = nb * Q_BLOCK
    pad = [(0, 0)] * len(lead) + [(0, Lp - L), (0, 0)]
    q, k, v = (jnp.pad(t, pad) for t in (q, k, v))
    qb = q.reshape(*lead, nb, Q_BLOCK, HEAD_DIM)
    kb = k.reshape(*lead, nb, Q_BLOCK, HEAD_DIM)
    vb = v.reshape(*lead, nb, Q_BLOCK, HEAD_DIM)
    zero = jnp.zeros_like(kb[..., :1, :, :])
    kwin = jnp.concatenate([jnp.concatenate([zero, kb[..., :-1, :, :]], axis=-3), kb], axis=-2)
    vwin = jnp.concatenate([jnp.concatenate([zero, vb[..., :-1, :, :]], axis=-3), vb], axis=-2)
    s = jnp.einsum('...nqd,...nkd->...nqk', qb, kwin) * SCALE
    blk = jnp.arange(nb)[:, None]
    qpos = blk * Q_BLOCK + jnp.arange(Q_BLOCK)[None, :]
    kpos = (blk - 1) * Q_BLOCK + jnp.arange(2 * Q_BLOCK)[None, :]
    dist = qpos[:, :, None] - kpos[:, None, :]
    mask = (dist >= 0) & (dist <= span) & (kpos[:, None, :] >= 0)
    s = jnp.where(mask, s, NEG)
    lse = jax.nn.logsumexp(s, axis=-1)
    p = jnp.exp(s - lse[..., None])
    o = jnp.einsum('...nqk,...nkd->...nqd', p, vwin)
    return o.reshape(*lead, Lp, HEAD_DIM)[..., :L, :], lse.reshape(*lead, Lp)[..., :L]


def dilated_attention(q, k, v):
    bsz, nh, seq, hd = q.shape
    outs, lses = [], []
    for window, dil in DILATED_SEGMENTS:
        L = seq // dil
        sub = lambda t: t.reshape(bsz, nh, L, dil, hd).swapaxes(2, 3)
        o, lse = banded_attention(sub(q), sub(k), sub(v), window // dil)
        outs.append(o.swapaxes(2, 3).reshape(bsz, nh, seq, hd))
        lses.append(lse.swapaxes(2, 3).reshape(bsz, nh, seq))
    w = jax.nn.softmax(jnp.stack(lses, axis=0), axis=0)
    return sum(w[i][..., None] * outs[i] for i in range(len(outs)))


def stick_breaking_attention(q, k, v):
    bsz, nh, seq, hd = q.shape
    kpos = jnp.arange(seq)

    def block(i):
        qi = lax.dynamic_slice_in_dim(q, i * Q_BLOCK, Q_BLOCK, axis=2)
        z = jnp.einsum('bhqd,bhkd->bhqk', qi, k) * SCALE
        qpos = i * Q_BLOCK + jnp.arange(Q_BLOCK)
        causal = kpos[None, :] < qpos[:, None]
        log_1mb = jnp.where(causal, jax.nn.log_sigmoid(-z), 0.0)
        after = lax.cumsum(log_1mb, axis=3, reverse=True) - log_1mb
        log_a = jnp.where(causal, jax.nn.log_sigmoid(z) + after, NEG)
        return jnp.einsum('bhqk,bhkd->bhqd', jnp.exp(log_a), v)

    out = lax.map(block, jnp.arange(seq // Q_BLOCK))
    return out.transpose(1, 2, 0, 3, 4).reshape(bsz, nh, seq, hd)


def moba_attention(q, k, v):
    bsz, nh, seq, hd = q.shape
    nblk = -(-seq // MOBA_BLOCK)
    sp = nblk * MOBA_BLOCK
    pad = [(0, 0), (0, 0), (0, sp - seq), (0, 0)]
    kb = jnp.pad(k, pad).reshape(bsz, nh, nblk, MOBA_BLOCK, hd)
    vb = jnp.pad(v, pad).reshape(bsz, nh, nblk, MOBA_BLOCK, hd)
    kmean = jnp.mean(kb, axis=3)
    topk = min(MOBA_TOPK, nblk)
    b_idx = jnp.arange(bsz)[:, None, None, None]
    h_idx = jnp.arange(nh)[None, :, None, None]
    blk_ids = jnp.arange(nblk)

    def chunk(i):
        start = i * MOBA_Q_CHUNK
        qi = lax.dynamic_slice_in_dim(q, start, MOBA_Q_CHUNK, axis=2)
        qpos = start + jnp.arange(MOBA_Q_CHUNK)
        qblk = qpos // MOBA_BLOCK
        gate = jnp.einsum('bhqd,bhnd->bhqn', qi, kmean)
        gate = jnp.where(blk_ids[None, :] < qblk[:, None], gate, NEG)
        _, sel = lax.top_k(gate, topk)
        valid = sel < qblk[None, None, :, None]
        ks = kb[b_idx, h_idx, sel]
        vs = vb[b_idx, h_idx, sel]
        s_sel = jnp.einsum('bhqd,bhqnkd->bhqnk', qi, ks) * SCALE
        s_sel = jnp.where(valid[..., None], s_sel, NEG).reshape(bsz, nh, MOBA_Q_CHUNK, topk * MOBA_BLOCK)
        own = start // MOBA_BLOCK
        k_own = lax.dynamic_index_in_dim(kb, own, axis=2, keepdims=False)
        v_own = lax.dynamic_index_in_dim(vb, own, axis=2, keepdims=False)
        s_own = jnp.einsum('bhqd,bhkd->bhqk', qi, k_own) * SCALE
        own_pos = own * MOBA_BLOCK + jnp.arange(MOBA_BLOCK)
        s_own = jnp.where(own_pos[None, :] <= qpos[:, None], s_own, NEG)
        p = jax.nn.softmax(jnp.concatenate([s_sel, s_own], axis=-1), axis=-1)
        p_sel = p[..., :topk * MOBA_BLOCK].reshape(bsz, nh, MOBA_Q_CHUNK, topk, MOBA_BLOCK)
        p_own = p[..., topk * MOBA_BLOCK:]
        return (jnp.einsum('bhqnk,bhqnkd->bhqd', p_sel, vs)
                + jnp.einsum('bhqk,bhkd->bhqd', p_own, v_own))

    out = lax.map(chunk, jnp.arange(seq // MOBA_Q_CHUNK))
    return out.transpose(1, 2, 0, 3, 4).reshape(bsz, nh, seq, hd)


def forgetting_attention(q, k, v, log_f):
    bsz, nh, seq, hd = q.shape
    cum = lax.cumsum(log_f, axis=2)
    kpos = jnp.arange(seq)

    def block(i):
        qi = lax.dynamic_slice_in_dim(q, i * Q_BLOCK, Q_BLOCK, axis=2)
        fi = lax.dynamic_slice_in_dim(cum, i * Q_BLOCK, Q_BLOCK, axis=2)
        s = jnp.einsum('bhqd,bhkd->bhqk', qi, k) * SCALE + fi[..., :, None] - cum[..., None, :]
        qpos = i * Q_BLOCK + jnp.arange(Q_BLOCK)
        s = jnp.where(kpos[None, :] <= qpos[:, None], s, NEG)
        return jnp.einsum('bhqk,bhkd->bhqd', jax.nn.softmax(s, axis=-1), v)

    out = lax.map(block, jnp.arange(seq // Q_BLOCK))
    return out.transpose(1, 2, 0, 3, 4).reshape(bsz, nh, seq, hd)


def setup_inputs(seed: int = 0) -> dict:
    key = jax.random.key(seed)
    ks = jax.random.split(key, 7)
    x = jax.random.normal(ks[0], (BATCH, SEQ, D_MODEL), jnp.float32)
    norm_gain = 1.0 + 0.02 * jax.random.normal(ks[1], (DEPTH, D_MODEL), jnp.float32)
    w_in = jax.random.normal(ks[2], (DEPTH, D_MODEL, IN_COLS), jnp.float32) * D_MODEL ** -0.5
    q_norm_gain = 1.0 + 0.02 * jax.random.normal(ks[3], (DEPTH, N_QK_NORMED, HEAD_DIM), jnp.float32)
    k_norm_gain = 1.0 + 0.02 * jax.random.normal(ks[4], (DEPTH, N_QK_NORMED, HEAD_DIM), jnp.float32)
    forget_bias = jax.random.uniform(ks[5], (DEPTH, N_HEADS_PER_MIXER), jnp.float32, 1.0, 5.0)
    w_out = jax.random.normal(ks[6], (DEPTH, MIX_WIDTH, D_MODEL), jnp.float32) * MIX_WIDTH ** -0.5
    return {"x": x, "norm_gain": norm_gain, "w_in": w_in, "q_norm_gain": q_norm_gain,
            "k_norm_gain": k_norm_gain, "forget_bias": forget_bias, "w_out": w_out}


def reference(x, norm_gain, w_in, q_norm_gain, k_norm_gain, forget_bias, w_out):
    bsz, seq, _ = x.shape
    cos, sin = rope_tables(seq)
    for layer in range(DEPTH):
        h = rms_norm(x, norm_gain[layer])
        proj = jnp.einsum('bsd,dc->bsc', h, w_in[layer].astype(jnp.float32))
        qkv = proj[..., :3 * MIX_WIDTH].reshape(bsz, seq, 3, N_MIXERS, N_HEADS_PER_MIXER, HEAD_DIM)
        qkv = qkv.transpose(2, 3, 0, 4, 1, 5)
        q, k, v = qkv[0], qkv[1], qkv[2]
        gate = proj[..., 3 * MIX_WIDTH:4 * MIX_WIDTH]
        log_f = jax.nn.log_sigmoid(proj[..., 4 * MIX_WIDTH:]
                                   + forget_bias[layer].astype(jnp.float32)).transpose(0, 2, 1)
        qn, kn = q_norm_gain[layer], k_norm_gain[layer]
        o_a = dilated_attention(apply_rope(head_rms_norm(q[0], qn[0]), cos, sin),
                                apply_rope(head_rms_norm(k[0], kn[0]), cos, sin), v[0])
        o_b = stick_breaking_attention(q[1], k[1], v[1])
        o_c = moba_attention(apply_rope(head_rms_norm(q[2], qn[1]), cos, sin),
                             apply_rope(head_rms_norm(k[2], kn[1]), cos, sin), v[2])
        o_d = forgetting_attention(head_rms_norm(q[3], qn[2]), head_rms_norm(k[3], kn[2]), v[3], log_f)
        y = jnp.stack([o_a, o_b, o_c, o_d], axis=1)
        y = y.transpose(0, 3, 1, 2, 4).reshape(bsz, seq, MIX_WIDTH)
        y = y * jax.nn.silu(gate)
        x = x + jnp.einsum('bsc,cd->bsd', y, w_out[layer].astype(jnp.float32)).astype(x.dtype)
    return x
```

```python
import contextlib
import numpy as np
import ml_dtypes
import concourse.bass as bass
import concourse.mybir as mybir
from concourse.bass_utils import run_bass_kernel_spmd

F32 = mybir.dt.float32
BF16 = mybir.dt.bfloat16
AF = mybir.ActivationFunctionType
ALU = mybir.AluOpType
AX = mybir.AxisListType

D_MODEL = 2048
NCH = 16
WC = 2050
HD = 64
NEGM = -30000.0
SCALE = 0.125
EPS = 1e-6
NAUG = (0, 0, 32, 3)
ROWS = tuple(a + 64 for a in NAUG)


class Res:
    __slots__ = ("name", "w", "r", "uid", "pre")
    _n = [0]

    def __init__(self, name):
        Res._n[0] += 1
        self.uid = Res._n[0]
        self.name = name
        self.w = []
        self.r = []
        self.pre = []


class Op:
    __slots__ = ("eng", "fn", "deps", "has_dependents", "semkey", "value", "is_dma", "seq", "inc")

    def __init__(self, eng, fn, is_dma, semkey):
        self.seq = 0
        self.inc = 16 if is_dma else 1
        self.eng = eng
        self.fn = fn
        self.deps = []
        self.has_dependents = bool(is_dma)
        self.semkey = semkey
        self.value = None
        self.is_dma = is_dma


class Prog:
    ENGS = ("pe", "act", "dve", "pool", "sp")

    def __init__(self, nc):
        self.nc = nc
        self.ops = {e: [] for e in self.ENGS}
        self.nres = 0
        self.final_ops = []
        self.pending_barrier = {e: [] for e in self.ENGS}
        self.dma_since_barrier = []

    def res(self, name=None):
        self.nres += 1
        return Res(name or f"r{self.nres}")

    def _add(self, eng, fn, reads, writes, acc, is_dma, semkey, after=()):
        op = Op(eng, fn, is_dma, semkey)
        deps = list(self.pending_barrier[eng]) + list(after)
        self.pending_barrier[eng] = []
        for r in reads:
            deps.extend(r.w)
        for w in writes:
            deps.extend(w.r)
            deps.extend(w.w)
        for w in acc:
            deps.extend(w.r)
            deps.extend(w.pre)
        seen = set()
        for d in deps:
            if id(d) in seen:
                continue
            seen.add(id(d))
            if d.eng == eng and not d.is_dma and not is_dma and eng == "pe":
                continue
            d.has_dependents = True
            op.deps.append(d)
        for r in reads:
            r.r.append(op)
        for w in writes:
            w.pre = [d for d in (w.r + w.w) if d.eng != eng or d.is_dma]
            w.w = [op]
            w.r = []
        for w in acc:
            w.w.append(op)
        self.ops[eng].append(op)
        self.seqn = getattr(self, "seqn", 0) + 1
        op.seq = self.seqn
        if is_dma:
            self.dma_since_barrier.append(op)
            self.dma_by_key = getattr(self, "dma_by_key", {})
            self.dma_by_key.setdefault(semkey, []).append(op)
        return op

    def op(self, eng, fn, reads=(), writes=(), acc=(), after=()):
        return self._add(eng, fn, reads, writes, acc, False, ("eng", eng), after)

    def dma(self, eng, out_ap, in_ap, reads=(), writes=(), acc=(), tile=None, **kw):
        fn = lambda e: e.dma_start(out=out_ap, in_=in_ap, **kw)
        return self._add(eng, fn, reads, writes, acc, True, ("res", tile.name))

    def collective(self, fn, reads, writes, res):
        op = self._add("pool", fn, reads, writes, (), True, ("res", res.name))
        op.inc = 1
        return op

    def finish(self, op):
        op.has_dependents = True
        self.final_ops.append(op)

    def barrier(self):
        deps = []
        for e in self.ENGS:
            for op in reversed(self.ops[e]):
                if not op.is_dma:
                    deps.append(op)
                    break
        deps.extend(self.dma_since_barrier)
        self.dma_since_barrier = []
        for e in self.ENGS:
            self.pending_barrier[e] = list(self.pending_barrier[e]) + deps

    def emit(self):
        nc = self.nc
        semvals = {}
        semkeys = []
        for e in self.ENGS:
            for op in self.ops[e]:
                if op.has_dependents:
                    k = op.semkey
                    if k not in semvals:
                        semvals[k] = 0
                        semkeys.append(k)
                    semvals[k] += op.inc
                    op.value = semvals[k]
        stack = contextlib.ExitStack()
        sems = {}
        for i, k in enumerate(semkeys):
            sems[k] = stack.enter_context(nc.semaphore(f"s{i}"))
        self.nsems = len(semkeys)
        ops = self.ops
        final_ops = self.final_ops

        import bisect
        dma_by_key = getattr(self, "dma_by_key", {})
        dma_seqs = {k: [o.seq for o in v] for k, v in dma_by_key.items()}

        def need(op_seq, deps):
            req = {}
            for d in deps:
                k = d.semkey
                v = d.value
                if d.is_dma:
                    lst = dma_by_key[k]
                    i = bisect.bisect_left(dma_seqs[k], op_seq) - 1
                    if i >= 0 and lst[i].value > v:
                        v = lst[i].value
                if req.get(k, 0) < v:
                    req[k] = v
            return req

        def run(ename, e):
            seen = {}
            for op in ops[ename]:
                for k, v in need(op.seq, op.deps).items():
                    if seen.get(k, 0) >= v:
                        continue
                    e.wait_ge(sems[k], v)
                    seen[k] = v
                ins = op.fn(e)
                if op.has_dependents:
                    ins.then_inc(sems[op.semkey], op.inc)
            if ename == "sp":
                for k, v in need(10 ** 12, final_ops).items():
                    if seen.get(k, 0) >= v:
                        continue
                    e.wait_ge(sems[k], v)
                    seen[k] = v

        with stack:
            with nc.Block() as block:
                @block.tensor
                def _(e):
                    run("pe", e)

                @block.scalar
                def _(e):
                    run("act", e)

                @block.vector
                def _(e):
                    run("dve", e)

                @block.gpsimd
                def _(e):
                    run("pool", e)

                @block.sync
                def _(e):
                    run("sp", e)


class Arena:
    def __init__(self, nc, st, P, name, nbytes):
        self.t = st.enter_context(nc.sbuf_tensor(name, [128, nbytes // 2], BF16))
        self.cap = nbytes
        self.off = 0
        self.P = P
        self.name = name
        self.n = 0

    def alloc(self, cols, dt, name=None):
        esz = 4 if dt == F32 else 2
        off = (self.off + 63) // 64 * 64
        size = cols * esz
        assert off + size <= self.cap, (self.name, name, off, size, self.cap)
        ap = self.t[:, off // 2:(off + size) // 2]
        if dt == F32:
            ap = ap.bitcast(F32)
        self.off = off + size
        self.n += 1
        return ap, self.P.res(f"{self.name}.{name or self.n}")

    def mark(self):
        return self.off

    def reset(self, mark=0):
        self.off = mark


C_ID = 0
C_TRI = 128
C_ONES = 256
C_NTRI = 384
C_NONES = 512
C_CM = 640
C_BM = 640 + 2048
C_TOT = C_BM + 256


def make_consts():
    c = np.zeros((128, C_TOT), np.float32)
    i = np.arange(128)
    c[:, C_ID:C_ID + 128] = np.eye(128)
    c[:, C_TRI:C_TRI + 128] = (i[:, None] <= i[None, :])
    c[:, C_ONES:C_ONES + 128] = 1.0
    c[:, C_NTRI:C_NTRI + 128] = -(i[:, None] >= i[None, :]).astype(np.float32)
    c[:, C_NONES:C_NONES + 128] = -1.0
    q = np.arange(512)
    for d in range(4):
        c[:, C_CM + d * 512:C_CM + (d + 1) * 512] = np.where(q[None, :] >= 128 * d + i[:, None], 0.0, NEGM)
    c[:, C_BM:C_BM + 128] = np.where(i[:, None] >= i[None, :], 0.0, NEGM)
    c[:, C_BM + 128:C_BM + 256] = np.where(i[:, None] <= i[None, :], 0.0, NEGM)
    return c


def rope_table(S):
    inv = (1.0 / (np.float32(10000.0) ** (np.arange(0, HD, 2, dtype=np.float32) / np.float32(HD)))).astype(np.float32)
    ang = (np.arange(S, dtype=np.float32)[:, None] * inv[None, :]).astype(np.float32)
    return np.concatenate([np.cos(ang), np.sin(ang)], axis=1).astype(np.float32)


class LayerBuilder:
    def __init__(self, nc, P, st, S):
        self.nc, self.P, self.st, self.S = nc, P, st, S
        self.NT = S // 128
        self.NQ = S // 512
        assert S % 2048 == 0
        self.ar = Arena(nc, st, P, "arena", 180 * 1024)
        self.cst = Arena(nc, st, P, "cst", 24 * 1024)
        self.banks = []
        for i in range(8):
            t = st.enter_context(nc.psum_tensor(f"bank{i}", [128, 512], F32))
            self.banks.append((t[:, :], P.res(f"bank{i}")))
        self.QA = nc.dram_tensor("QA", [8, 96, S], BF16).ap()
        self.KA = nc.dram_tensor("KA", [8, 96, S], BF16).ap()
        self.V = nc.dram_tensor("Vs", [S, 512], BF16).ap()
        self.GT = nc.dram_tensor("GT", [4, 128, S], BF16).ap()
        self.QA_r = [P.res(f"QA{h}") for h in range(8)]
        self.KA_r = [P.res(f"KA{h}") for h in range(8)]
        self.V_r = P.res("Vd")
        self.GT_r = P.res("GTd")
        self.yT_r = P.res("yTd")
        self.yT_external = True

    def load_consts(self, consts_ap):
        P, cst = self.P, self.cst
        cf, cf_r = cst.alloc(C_TOT, F32, "cf")
        self.cf, self.cf_r = cf, cf_r
        P.dma("sp", cf, consts_ap, writes=[cf_r], tile=cf_r)
        self.idb, self.idb_r = cst.alloc(128, BF16, "idb")
        self.cmb, self.cmb_r = cst.alloc(2048, BF16, "cmb")
        self.bmb, self.bmb_r = cst.alloc(256, BF16, "bmb")
        self.oneb, self.oneb_r = cst.alloc(128, BF16, "oneb")
        P.op("dve", lambda e: e.tensor_copy(out=self.idb, in_=cf[:, C_ID:C_ID + 128]), reads=[cf_r], writes=[self.idb_r])
        P.op("dve", lambda e: e.tensor_copy(out=self.cmb, in_=cf[:, C_CM:C_CM + 2048]), reads=[cf_r], writes=[self.cmb_r])
        P.op("dve", lambda e: e.tensor_copy(out=self.bmb, in_=cf[:, C_BM:C_BM + 256]), reads=[cf_r], writes=[self.bmb_r])
        P.op("dve", lambda e: e.tensor_copy(out=self.oneb, in_=cf[:, C_ONES:C_ONES + 128]), reads=[cf_r], writes=[self.oneb_r])
        self.negF, self.negF_r = cst.alloc(self.NT * 2, F32, "negF")

    def phase1(self, x_ap, wc_ap, ng_ap, gqk_ap, fb_ap, cs_ap):
        P, ar, S, NT = self.P, self.ar, self.S, self.NT
        cf, cf_r = self.cf, self.cf_r
        idb, idb_r = self.idb, self.idb_r
        ar.reset()
        Wb, Wb_r = ar.alloc(NCH * WC, BF16, "Wb")
        Wb3 = Wb.rearrange("p (c n) -> p c n", c=NCH)
        ng, ng_r = ar.alloc(NCH, F32, "ng")
        Gq, Gq_r = ar.alloc(512, F32, "Gq")
        Gk, Gk_r = ar.alloc(512, F32, "Gk")
        nfb, nfb_r = ar.alloc(2, F32, "nfb")
        kmT = [ar.alloc(32, F32, f"kmT{j}") for j in range(2)]
        carry, carry_r = ar.alloc(2, F32, "carry")
        P.dma("sp", ng, ng_ap, writes=[ng_r], tile=ng_r)
        P.dma("sp", Gq, gqk_ap[0].partition_broadcast(128), writes=[Gq_r], tile=Gq_r)
        P.dma("sp", Gk, gqk_ap[1].partition_broadcast(128), writes=[Gk_r], tile=Gk_r)
        P.dma("sp", nfb, fb_ap[0].partition_broadcast(128), writes=[nfb_r], tile=nfb_r)
        P.op("dve", lambda e: e.tensor_scalar_mul(out=Gq, in0=Gq, scalar1=SCALE), reads=[Gq_r], writes=[Gq_r])
        P.op("dve", lambda e: e.tensor_scalar_mul(out=nfb, in0=nfb, scalar1=-1.0), reads=[nfb_r], writes=[nfb_r])
        P.op("dve", lambda e: e.memset(carry, 0.0), writes=[carry_r])
        for j in range(2):
            P.op("dve", lambda e, j=j: e.memset(kmT[j][0], 0.0), writes=[kmT[j][1]])
        wst = [ar.alloc(WC, F32, f"wst{i}") for i in range(2)]
        for c in range(NCH):
            ws, ws_r = wst[c % 2]
            P.dma("sp", ws, wc_ap[c * 128:(c + 1) * 128, :], writes=[ws_r], tile=ws_r)
            eng = "dve" if c % 2 == 0 else "pool"
            P.op(eng, lambda e, ws=ws, c=c: e.tensor_scalar(out=Wb3[:, c, :], in0=ws, scalar1=ng[:, c:c + 1], scalar2=None, op0=ALU.mult),
                 reads=[ws_r, ng_r], acc=[Wb_r])

        xs = [ar.alloc(D_MODEL, F32, f"xs{i}") for i in range(2)]
        xn = [ar.alloc(D_MODEL, BF16, f"xn{i}") for i in range(2)]
        xT = [ar.alloc(D_MODEL, BF16, f"xT{i}") for i in range(2)]
        junk, junk_r = ar.alloc(D_MODEL, BF16, "junk")
        ss = [ar.alloc(1, F32, f"ss{i}") for i in range(2)]
        rr = [ar.alloc(1, F32, f"rr{i}") for i in range(2)]
        cs = [ar.alloc(64, F32, f"cs{i}") for i in range(3)]
        qsb = [ar.alloc(512, F32, f"qsb{i}") for i in range(2)]
        ksb = [ar.alloc(512, F32, f"ksb{i}") for i in range(2)]
        pfs = [ar.alloc(2, F32, f"pfs{i}") for i in range(2)]
        tmp, tmp_r = ar.alloc(512, F32, "tmp")
        ssh, ssh_r = ar.alloc(8, F32, "ssh")
        rt = [ar.alloc(128, F32, f"rt{i}") for i in range(4)]
        QAtm = [ar.alloc(8 * 96, BF16, f"QAtm{i}") for i in range(2)]
        KAtm = [ar.alloc(8 * 96, BF16, f"KAtm{i}") for i in range(2)]
        QTst = [ar.alloc(8 * 128, BF16, f"QTst{i}") for i in range(2)]
        KTst = [ar.alloc(8 * 128, BF16, f"KTst{i}") for i in range(2)]
        vst = [ar.alloc(512, BF16, f"vst{i}") for i in range(2)]
        gst = [ar.alloc(512, BF16, f"gst{i}") for i in range(2)]
        qTf = [ar.alloc(128, F32, f"qTf{j}") for j in range(2)]
        gm = [ar.alloc(32, F32, f"gm{j}") for j in range(2)]
        top8 = [ar.alloc(8, F32, f"top8{j}") for j in range(2)]
        sel = [ar.alloc(32, F32, f"sel{j}") for j in range(2)]
        ef, ef_r = ar.alloc(2, F32, "ef")
        lf, lf_r = ar.alloc(2, F32, "lf")
        Ft, Ft_r = ar.alloc(2, F32, "Ft")
        r1, r1_r = ar.alloc(2, F32, "r1")
        r2, r2_r = ar.alloc(2, F32, "r2")
        hif, hif_r = ar.alloc(2, F32, "hif")
        hb = [ar.alloc(2, BF16, f"hb{i}") for i in range(3)]

        (pT, pT_r) = self.banks[0][0], self.banks[0][1]
        pTb0 = self.banks[0][0][:, :].bitcast(BF16)
        pTb1 = self.banks[1][0][:, :].bitcast(BF16)
        pT1_r = self.banks[1][1]
        pq, pq_r = self.banks[2]
        pk, pk_r = self.banks[3]
        pv, pv_r = self.banks[4]
        pg, pg_r = self.banks[5]
        pm, pm_r = self.banks[6]
        pm2, pm2_r = self.banks[7]
        import os as _os2
        OLDB = bool(_os2.environ.get("OLD_BANKS"))
        pT7 = pTb0 if OLDB else pm2.bitcast(BF16)
        pT7_r = pT_r if OLDB else pm2_r
        qtb, qtb_r, qtoff = (pm2, pm2_r, 0) if OLDB else (pm, pm_r, 128)
        pf_r = pF_r = pkm_r = pm_r
        pgt_r = [pm_r, pm_r]
        pqT_r = [pm2_r, pm2_r]

        def load_x(t):
            x_t, x_r = xs[t % 2]
            xsr = getattr(self, "x_src_r", None)
            P.dma("sp", x_t, x_ap[t * 128:(t + 1) * 128, :], reads=[xsr] if xsr is not None else (), writes=[x_r], tile=x_r)
            c_t, c_r = cs[t % 3]
            P.dma("sp", c_t, cs_ap[t * 128:(t + 1) * 128, :], writes=[c_r], tile=c_r)

        def post_qk(t, which, src_bank, src_r, G, G_r, ATM, TST, DR, DR_r):
            w_t, w_r = (qsb if which == "q" else ksb)[t % 2]
            c_t, c_r = cs[t % 3]
            atm, atm_r = ATM[t % 2]
            atm3 = atm.rearrange("p (h r) -> p h r", h=8)
            tst, tst_r = TST[t % 2]
            b = t // 2
            pf_t, pfs_r = pfs[t % 2]
            P.op("dve", lambda e: e.tensor_tensor(out=tmp, in0=w_t, in1=w_t, op=ALU.mult), reads=[w_r], writes=[tmp_r])
            P.op("dve", lambda e: e.reduce_sum(out=ssh, in_=tmp.rearrange("p (h d) -> p h d", h=8), axis=AX.X), reads=[tmp_r], writes=[ssh_r])
            P.op("act", lambda e: e.activation(out=ssh, in_=ssh, func=AF.Sqrt, bias=EPS, scale=1.0 / HD), reads=[ssh_r], writes=[ssh_r])
            P.op("dve", lambda e: e.reciprocal(out=ssh, in_=ssh), reads=[ssh_r], writes=[ssh_r])
            P.op("dve", lambda e: e.memset(ssh[:, 2:4], 1.0), reads=[ssh_r], writes=[ssh_r])
            w3 = w_t.rearrange("p (h d) -> p h d", h=8)
            P.op("dve", lambda e: e.tensor_tensor(out=w3, in0=w3, in1=ssh.unsqueeze(2).to_broadcast([128, 8, HD]), op=ALU.mult), reads=[w_r, ssh_r], writes=[w_r])
            P.op("dve", lambda e: e.tensor_tensor(out=w_t, in0=w_t, in1=G, op=ALU.mult), reads=[w_r, G_r], writes=[w_r])
            w4 = w_t.rearrange("p (a b d) -> p a b d", a=2, b=4)
            q1 = w4[:, :, 0:2, 0:32]
            q2 = w4[:, :, 0:2, 32:64]
            cosb = c_t[:, 0:32].unsqueeze(1).unsqueeze(1).to_broadcast([128, 2, 2, 32])
            sinb = c_t[:, 32:64].unsqueeze(1).unsqueeze(1).to_broadcast([128, 2, 2, 32])
            rts = [(a.rearrange("p (a b d) -> p a b d", a=2, b=2), r) for a, r in rt]
            P.op("dve", lambda e: e.tensor_tensor(out=rts[0][0], in0=q1, in1=cosb, op=ALU.mult), reads=[w_r, c_r], writes=[rts[0][1]])
            P.op("dve", lambda e: e.tensor_tensor(out=rts[1][0], in0=q2, in1=sinb, op=ALU.mult), reads=[w_r, c_r], writes=[rts[1][1]])
            P.op("pool", lambda e: e.tensor_tensor(out=rts[2][0], in0=q2, in1=cosb, op=ALU.mult), reads=[w_r, c_r], writes=[rts[2][1]])
            P.op("pool", lambda e: e.tensor_tensor(out=rts[3][0], in0=q1, in1=sinb, op=ALU.mult), reads=[w_r, c_r], writes=[rts[3][1]])
            P.op("dve", lambda e: e.tensor_tensor(out=q1, in0=rts[0][0], in1=rts[1][0], op=ALU.subtract), reads=[rts[0][1], rts[1][1]], writes=[w_r])
            P.op("dve", lambda e: e.tensor_tensor(out=q2, in0=rts[2][0], in1=rts[3][0], op=ALU.add), reads=[rts[2][1], rts[3][1], w_r], writes=[w_r])
            P.op("dve", lambda e: e.tensor_copy(out=atm3[:, 0:4, 0:64], in_=w3[:, 0:4, :]), reads=[w_r], acc=[atm_r])
            P.op("pool", lambda e: e.tensor_copy(out=atm3[:, 4:6, 32:96], in_=w3[:, 4:6, :]), reads=[w_r], acc=[atm_r])
            P.op("pool", lambda e: e.tensor_copy(out=atm3[:, 6:8, 3:67], in_=w3[:, 6:8, :]), reads=[w_r], acc=[atm_r])
            if which == "q":
                for j in range(2):
                    h = 4 + j
                    qT_t, qT_r = qTf[j]
                    g_t, g_r = gm[j]
                    t8, t8_r = top8[j]
                    s_t, s_r = sel[j]
                    if b > 0:
                        P.op("pe", lambda e, h=h, j=j: e.transpose(out=qtb[0:64, qtoff + j * 128:qtoff + (j + 1) * 128], in_=w_t[:, h * 64:(h + 1) * 64], identity=cf[:, C_ID:C_ID + 128]),
                             reads=[w_r, cf_r], acc=[qtb_r])
                        P.op("act", lambda e, j=j, qT_t=qT_t: e.copy(out=qT_t[0:64, :], in_=qtb[0:64, qtoff + j * 128:qtoff + (j + 1) * 128]), reads=[qtb_r], writes=[qT_r])
                        P.op("pe", lambda e, j=j, qT_t=qT_t: e.matmul(pm[:, 16 + 32 * j:48 + 32 * j], lhsT=qT_t[0:64, :], rhs=kmT[j][0][0:64, :], start=True, stop=True),
                             reads=[qT_r, kmT[j][1]], acc=[pgt_r[j]])
                        P.op("dve", lambda e, g_t=g_t: e.memset(g_t, -1e30), writes=[g_r])
                        P.op("dve", lambda e, j=j, g_t=g_t: e.tensor_copy(out=g_t[:, 0:b], in_=pm[:, 16 + 32 * j:16 + 32 * j + b]), reads=[pgt_r[j]], writes=[g_r])
                        P.op("dve", lambda e, g_t=g_t, t8=t8: e.max(out=t8, in_=g_t), reads=[g_r], writes=[t8_r])
                        P.op("dve", lambda e, g_t=g_t, t8=t8, s_t=s_t: e.tensor_scalar(out=s_t, in0=g_t, scalar1=t8[:, 2:3], scalar2=-NEGM, op0=ALU.is_ge, op1=ALU.mult),
                             reads=[g_r, t8_r], writes=[s_r])
                        o1 = P.op("dve", lambda e, h=h, s_t=s_t: e.tensor_scalar_add(out=atm3[:, h, 0:32], in0=s_t, scalar1=NEGM), reads=[s_r], acc=[atm_r])
                    else:
                        o1 = P.op("dve", lambda e, h=h: e.memset(atm3[:, h, 0:32], NEGM), acc=[atm_r])
                    P.op("dve", lambda e, h=h: e.memset(atm3[:, h, b:b + 1], 0.0), acc=[atm_r], after=[o1])
                for j in range(2):
                    P.op("act", lambda e, j=j: e.activation(out=ef[:, j:j + 1], in_=pf_t[:, j:j + 1], func=AF.Exp, bias=nfb[:, j:j + 1], scale=-1.0),
                         reads=[pfs_r, nfb_r], acc=[ef_r])
                P.op("act", lambda e: e.activation(out=lf, in_=ef, func=AF.Ln, bias=1.0, scale=1.0), reads=[ef_r], writes=[lf_r])
                P.op("dve", lambda e: e.tensor_scalar_mul(out=lf, in0=lf, scalar1=-1.0), reads=[lf_r], writes=[lf_r])
                P.op("pe", lambda e: e.matmul(pm[:, 8:10], lhsT=cf[:, C_TRI:C_TRI + 128], rhs=lf, start=True, stop=True), reads=[cf_r, lf_r], acc=[pF_r])
                P.op("pe", lambda e: e.matmul(pm[:, 10:12], lhsT=cf[:, C_ONES:C_ONES + 128], rhs=lf, start=True, stop=True), reads=[cf_r, lf_r], acc=[pF_r])
                P.op("dve", lambda e: e.tensor_tensor(out=Ft, in0=pm[:, 8:10], in1=carry, op=ALU.add), reads=[pF_r, carry_r], writes=[Ft_r])
                P.op("dve", lambda e: e.tensor_tensor(out=carry, in0=pm[:, 10:12], in1=carry, op=ALU.add), reads=[pF_r, carry_r], writes=[carry_r])
                P.op("dve", lambda e: e.tensor_scalar_mul(out=self.negF[:, 2 * t:2 * t + 2], in0=Ft, scalar1=-1.0), reads=[Ft_r], acc=[self.negF_r])
                P.op("dve", lambda e: e.tensor_copy(out=hb[0][0], in_=Ft), reads=[Ft_r], writes=[hb[0][1]])
                P.op("dve", lambda e: e.tensor_copy(out=hif, in_=hb[0][0]), reads=[hb[0][1]], writes=[hif_r])
                P.op("dve", lambda e: e.tensor_tensor(out=r1, in0=Ft, in1=hif, op=ALU.subtract), reads=[Ft_r, hif_r], writes=[r1_r])
                P.op("dve", lambda e: e.tensor_copy(out=hb[1][0], in_=r1), reads=[r1_r], writes=[hb[1][1]])
                P.op("dve", lambda e: e.tensor_copy(out=hif, in_=hb[1][0]), reads=[hb[1][1]], writes=[hif_r])
                P.op("dve", lambda e: e.tensor_tensor(out=r2, in0=r1, in1=hif, op=ALU.subtract), reads=[r1_r, hif_r], writes=[r2_r])
                for i3 in range(2):
                    P.op("dve", lambda e, i3=i3: e.tensor_copy(out=atm3[:, 6:8, i3:i3 + 1], in_=hb[i3][0].unsqueeze(2)), reads=[hb[i3][1]], acc=[atm_r])
                P.op("dve", lambda e: e.tensor_copy(out=atm3[:, 6:8, 2:3], in_=r2.unsqueeze(2)), reads=[r2_r], acc=[atm_r])
            else:
                o1 = P.op("pool", lambda e: e.memset(atm3[:, 4:6, 0:32], 0.0), acc=[atm_r])
                P.op("pool", lambda e: e.memset(atm3[:, 4:6, b:b + 1], 1.0), acc=[atm_r], after=[o1])
                P.op("pool", lambda e: e.memset(atm3[:, 6:8, 0:3], 1.0), acc=[atm_r])
                for j in range(2):
                    h = 4 + j
                    P.op("pe", lambda e, h=h, j=j: e.matmul(pm[0:64, 96 + j:97 + j], lhsT=w_t[:, h * 64:(h + 1) * 64], rhs=cf[:, C_ONES:C_ONES + 1],
                                                         start=True, stop=True),
                         reads=[w_r, cf_r], acc=[pkm_r])
                for j in range(2):
                    P.op("dve", lambda e, j=j: e.scalar_tensor_tensor(out=kmT[j][0][0:64, b:b + 1], in0=pm[0:64, 96 + j:97 + j], scalar=1.0 / 256.0, in1=kmT[j][0][0:64, b:b + 1],
                                                                   op0=ALU.mult, op1=ALU.add),
                         reads=[pkm_r, kmT[j][1]], writes=[kmT[j][1]])
            for h in range(8):
                R = ROWS[h // 2]
                dst = (pTb0 if h < 8 else None)
                P.op("pe", lambda e, h=h, R=R: e.transpose(out=pT7[0:R, h * 128:(h + 1) * 128], in_=atm3[:, h, 0:R], identity=idb),
                     reads=[atm_r, idb_r], writes=[pT7_r] if h == 0 else (), acc=() if h == 0 else [pT7_r])
            tst3 = tst.rearrange("p (h s) -> p h s", h=8)
            pT3 = pT7.rearrange("p (h s) -> p h s", h=8)
            P.op("act", lambda e: e.copy(out=tst3[0:64, 0:4, :], in_=pT3[0:64, 0:4, :]), reads=[pT7_r], writes=[tst_r])
            if True:
                P.op("act", lambda e: e.copy(out=tst3[0:96, 4:6, :], in_=pT3[0:96, 4:6, :]), reads=[pT7_r], acc=[tst_r])
            else:
                P.op("dve", lambda e: e.tensor_copy(out=tst3[0:96, 4:6, :], in_=pT3[0:96, 4:6, :]), reads=[pT7_r], acc=[tst_r])
            P.op("act", lambda e: e.copy(out=tst3[0:67, 6:8, :], in_=pT3[0:67, 6:8, :]), reads=[pT7_r], acc=[tst_r])
            sl = slice(t * 128, (t + 1) * 128)
            P.dma("pool", DR[0:4, 0:64, sl].rearrange("h r s -> r h s"), tst3[0:64, 0:4, :], reads=[tst_r], acc=DR_r[0:4], tile=tst_r)
            P.dma("pool", DR[4:6, 0:96, sl].rearrange("h r s -> r h s"), tst3[0:96, 4:6, :], reads=[tst_r], acc=DR_r[4:6], tile=tst_r)
            P.dma("pool", DR[6:8, 0:67, sl].rearrange("h r s -> r h s"), tst3[0:67, 6:8, :], reads=[tst_r], acc=DR_r[6:8], tile=tst_r)

        def main(t):
            x_t, x_r = xs[t % 2]
            n_t, n_r = xn[t % 2]
            T_t, T_r = xT[t % 2]
            s_t, s_r = ss[t % 2]
            r_t, r_r = rr[t % 2]
            P.op("act", lambda e: e.activation(out=junk, in_=x_t, func=AF.Square, accum_out=s_t), reads=[x_r], writes=[junk_r, s_r])
            P.op("act", lambda e: e.activation(out=r_t, in_=s_t, func=AF.Sqrt, bias=EPS, scale=1.0 / D_MODEL), reads=[s_r], writes=[r_r])
            P.op("dve", lambda e: e.reciprocal(out=r_t, in_=r_t), reads=[r_r], writes=[r_r])
            P.op("dve", lambda e: e.tensor_scalar(out=n_t, in0=x_t, scalar1=r_t, scalar2=None, op0=ALU.mult), reads=[x_r, r_r], writes=[n_r])
            for c in range(NCH):
                dst = pTb0 if c < 8 else pTb1
                dr = pT_r if c < 8 else pT1_r
                cc = c % 8
                P.op("pe", lambda e, dst=dst, cc=cc, c=c: e.transpose(out=dst[:, cc * 128:(cc + 1) * 128], in_=n_t[:, c * 128:(c + 1) * 128], identity=idb),
                     reads=[n_r, idb_r], writes=[dr] if cc == 0 else (), acc=() if cc == 0 else [dr])
            P.op("act", lambda e: e.copy(out=T_t[:, 0:1024], in_=pTb0), reads=[pT_r], writes=[T_r])
            P.op("dve", lambda e: e.tensor_copy(out=T_t[:, 1024:2048], in_=pTb1), reads=[pT1_r], acc=[T_r])
            for (bank, b_r, c0) in ((pk, pk_r, 512), (pq, pq_r, 0), (pv, pv_r, 1024)):
                for c in range(NCH):
                    P.op("pe", lambda e, bank=bank, c=c, c0=c0: e.matmul(bank, lhsT=T_t[:, c * 128:(c + 1) * 128], rhs=Wb3[:, c, c0:c0 + 512], start=(c == 0), stop=(c == NCH - 1)),
                         reads=[T_r, Wb_r], writes=[b_r] if c == 0 else (), acc=() if c == 0 else [b_r])
            for m in range(4):
                for c in range(NCH):
                    first = (m == 0 and c == 0)
                    P.op("pe", lambda e, m=m, c=c: e.matmul(pg[:, m * 128:(m + 1) * 128], lhsT=Wb3[:, c, 1536 + m * 128:1536 + (m + 1) * 128], rhs=T_t[:, c * 128:(c + 1) * 128], start=(c == 0), stop=(c == NCH - 1)),
                         reads=[T_r, Wb_r], writes=[pg_r] if first else (), acc=() if first else [pg_r])
            for c in range(NCH):
                P.op("pe", lambda e, c=c: e.matmul(pm[:, 0:2], lhsT=T_t[:, c * 128:(c + 1) * 128], rhs=Wb3[:, c, 2048:2050], start=(c == 0), stop=(c == NCH - 1)),
                     reads=[T_r, Wb_r], writes=[pf_r] if c == 0 else (), acc=() if c == 0 else [pf_r])

        def post_a(t):
            v_t, v_r = vst[t % 2]
            g_t, g_r = gst[t % 2]
            k_t, k_r = ksb[t % 2]
            q_t, q_r = qsb[t % 2]
            pf_t, pfs_r = pfs[t % 2]
            P.op("act", lambda e: e.copy(out=k_t, in_=pk), reads=[pk_r], writes=[k_r])
            if True:
                P.op("act", lambda e: e.copy(out=q_t, in_=pq), reads=[pq_r], writes=[q_r])
            else:
                P.op("dve", lambda e: e.tensor_copy(out=q_t, in_=pq), reads=[pq_r], writes=[q_r])
            P.op("act", lambda e: e.copy(out=v_t, in_=pv), reads=[pv_r], writes=[v_r])
            P.dma("pool", self.V[t * 128:(t + 1) * 128, :], v_t, reads=[v_r], acc=[self.V_r], tile=v_r)
            if True:
                P.op("act", lambda e: e.copy(out=pf_t, in_=pm[:, 0:2]), reads=[pf_r], writes=[pfs_r])
            else:
                P.op("dve", lambda e: e.tensor_copy(out=pf_t, in_=pm[:, 0:2]), reads=[pf_r], writes=[pfs_r])
            P.op("act", lambda e: e.activation(out=g_t, in_=pg, func=AF.Silu), reads=[pg_r], writes=[g_r])
            P.dma("pool", self.GT[:, :, t * 128:(t + 1) * 128].rearrange("m c s -> c m s"), g_t.rearrange("p (m s) -> p m s", m=4), reads=[g_r], acc=[self.GT_r], tile=g_r)

        def post_b(t):
            post_qk(t, "k", pk, pk_r, Gk, Gk_r, KAtm, KTst, self.KA, self.KA_r)
            post_qk(t, "q", pq, pq_r, Gq, Gq_r, QAtm, QTst, self.QA, self.QA_r)

        import os as _os
        if _os.environ.get("PH1_SEQ"):
            load_x(0)
            if NT > 1:
                load_x(1)
            for t in range(NT):
                if t + 2 < NT:
                    load_x(t + 2)
                main(t)
                post_a(t)
                post_b(t)
        else:
            load_x(0)
            if NT > 1:
                load_x(1)
            main(0)
            post_a(0)
            for t in range(NT):
                if t + 2 < NT:
                    load_x(t + 2)
                if t + 1 < NT:
                    main(t + 1)
                    post_a(t + 1)
                post_b(t)
        P.barrier()

    def load_qkv(self, m, need_q=True):
        P, ar, S, NT = self.P, self.ar, self.S, self.NT
        R = ROWS[m]
        QT, KT = [], []
        for j in range(2):
            h = 2 * m + j
            q_t, q_r = ar.alloc(S, BF16, f"QT{m}{j}")
            k_t, k_r = ar.alloc(S, BF16, f"KT{m}{j}")
            nsp = 4
            for i in range(nsp):
                sl = slice(i * S // nsp, (i + 1) * S // nsp)
                P.dma("sp", q_t[0:R, sl], self.QA[h, 0:R, sl], reads=[self.QA_r[h]], acc=[q_r], tile=q_r)
                P.dma("sp", k_t[0:R, sl], self.KA[h, 0:R, sl], reads=[self.KA_r[h]], acc=[k_r], tile=k_r)
            QT.append((q_t, q_r))
            KT.append((k_t, k_r))
        v_t, v_r = ar.alloc(NT * 128, BF16, f"V{m}")
        v3 = v_t.rearrange("p (n c) -> p n c", n=NT)
        nsp = 4
        for i in range(nsp):
            n0, n1 = i * NT // nsp, (i + 1) * NT // nsp
            P.dma("sp", v3[:, n0:n1, :], self.V[n0 * 128:n1 * 128, m * 128:(m + 1) * 128].rearrange("(n p) c -> p n c", p=128),
                  reads=[self.V_r], acc=[v_r], tile=v_r)
        return QT, KT, (v3, v_r)

    def finish_qtile(self, m, qt, num_ap, num_r, den_ap, den_r, yT_ap, bufs, has_den=True):
        P = self.P
        i = bufs["i"]
        bufs["i"] += 1
        gt_t, gt_r = bufs["gt"][i % 2]
        y32, y32_r = bufs["y32"][i % 2]
        yb, yb_r = bufs["yb"][i % 2]
        sl = slice(qt * 512, (qt + 1) * 512)
        P.dma("sp", gt_t, self.GT[m, :, sl], reads=[self.GT_r], writes=[gt_r], tile=gt_r)
        if has_den:
            P.op("dve", lambda e: e.reciprocal(out=y32, in_=den_ap), reads=[den_r], writes=[y32_r])
            P.op("dve", lambda e: e.tensor_tensor(out=y32, in0=num_ap, in1=y32, op=ALU.mult), reads=[num_r, y32_r], writes=[y32_r])
            P.op("pool", lambda e: e.tensor_tensor(out=yb, in0=y32, in1=gt_t, op=ALU.mult), reads=[y32_r, gt_r], writes=[yb_r])
        else:
            P.op("dve", lambda e: e.tensor_tensor(out=yb, in0=num_ap, in1=gt_t, op=ALU.mult), reads=[num_r, gt_r], writes=[yb_r])
        dst_ap = self.y_dst(m, qt) if getattr(self, "y_dst", None) is not None else yT_ap[m * 128:(m + 1) * 128, sl]
        o = P.dma("pool", dst_ap, yb, reads=[yb_r], acc=[self.yT_r], tile=yb_r)
        if self.yT_external:
            P.finish(o)

    def out_bufs(self):
        ar = self.ar
        return {"i": 0,
                "gt": [ar.alloc(512, BF16, f"gt{i}") for i in range(2)],
                "y32": [ar.alloc(512, F32, f"y32{i}") for i in range(2)],
                "yb": [ar.alloc(512, BF16, f"yb{i}") for i in range(2)]}

    def load_vaug(self, m):
        P, ar, S, NT = self.P, self.ar, self.S, self.NT
        va, va_r = ar.alloc(NT * 256, BF16, f"VA{m}")
        va4 = va.rearrange("p (n j c) -> p n j c", n=NT, j=2)
        o1 = P.op("pool", lambda e: e.memset(va4[:, :, :, 64:128], 1.0), writes=[va_r])
        nsp = 4
        for i in range(nsp):
            n0, n1 = i * NT // nsp, (i + 1) * NT // nsp
            for j in range(2):
                P.dma("sp", va4[:, n0:n1, j, 0:64], self.V[n0 * 128:n1 * 128, m * 128 + j * 64:m * 128 + (j + 1) * 64].rearrange("(n p) c -> p n c", p=128),
                      reads=[self.V_r], acc=[va_r], tile=va_r)
        return va4, va_r

    def load_qk(self, m):
        P, ar, S = self.P, self.ar, self.S
        R = ROWS[m]
        QT, KT = [], []
        for j in range(2):
            h = 2 * m + j
            q_t, q_r = ar.alloc(S, BF16, f"QT{m}{j}")
            k_t, k_r = ar.alloc(S, BF16, f"KT{m}{j}")
            nsp = 4
            for i in range(nsp):
                sl = slice(i * S // nsp, (i + 1) * S // nsp)
                P.dma("sp", k_t[0:R, sl], self.KA[h, 0:R, sl], reads=[self.KA_r[h]], acc=[k_r], tile=k_r)
                P.dma("sp", q_t[0:R, sl], self.QA[h, 0:R, sl], reads=[self.QA_r[h]], acc=[q_r], tile=q_r)
            QT.append((q_t, q_r))
            KT.append((k_t, k_r))
        return QT, KT

    def phase_dense(self, m, yT_ap):
        P, ar, S, NT, NQ = self.P, self.ar, self.S, self.NT, self.NQ
        ar.reset()
        R = ROWS[m]
        QT, KT = self.load_qk(m)
        va4, va_r = self.load_vaug(m)
        NA = 4
        AT = [ar.alloc(512, BF16, f"AT{i}") for i in range(NA)]
        gts = [ar.alloc(512, BF16, f"gt{i}") for i in range(2)]
        ybs = [ar.alloc(512, BF16, f"yb{i}") for i in range(2)]
        y32s = [ar.alloc(512, F32, f"y32{i}") for i in range(2)]
        rdn = [ar.alloc(512, F32, f"rdn{i}") for i in range(2)]
        sb = [self.banks[i] for i in range(3)]
        pod = [[self.banks[3], self.banks[4]], [self.banks[5], self.banks[6]]]
        seq = []
        for qt in range(NQ):
            nkb = 4 * qt + 4
            for j in range(2):
                for kb in range(nkb):
                    seq.append((qt, j, kb, nkb))
        LOOK = 2
        n = len(seq)

        def stage_S(i):
            qt, j, kb, nkb = seq[i]
            s_t, s_r = sb[i % 3]
            a_t, a_r = AT[i % NA]
            q_t, q_r = QT[j]
            k_t, k_r = KT[j]
            d = kb - 4 * qt
            P.op("pe", lambda e: e.matmul(s_t, lhsT=k_t[0:R, kb * 128:(kb + 1) * 128], rhs=q_t[0:R, qt * 512:(qt + 1) * 512], start=True, stop=(d < 0)),
                 reads=[k_r, q_r], writes=[s_r])
            if d >= 0:
                P.op("pe", lambda e: e.matmul(s_t, lhsT=self.idb, rhs=self.cmb[:, d * 512:(d + 1) * 512], start=False, stop=True),
                     reads=[self.idb_r, self.cmb_r], acc=[s_r])
            if m == 3:
                P.op("act", lambda e: e.activation(out=a_t, in_=s_t, func=AF.Exp, bias=self.negF[:, 2 * kb + j:2 * kb + j + 1], scale=1.0),
                     reads=[s_r, self.negF_r], writes=[a_r])
            else:
                P.op("act", lambda e: e.activation(out=a_t, in_=s_t, func=AF.Exp), reads=[s_r], writes=[a_r])

        def stage_PV(i):
            qt, j, kb, nkb = seq[i]
            a_t, a_r = AT[i % NA]
            pb, pb_r = pod[qt % 2][j]
            P.op("pe", lambda e: e.matmul(pb, lhsT=va4[:, kb, j, :], rhs=a_t, start=(kb == 0), stop=(kb == nkb - 1)),
                 reads=[a_r, va_r], writes=[pb_r] if kb == 0 else (), acc=() if kb == 0 else [pb_r])
            if kb == nkb - 1:
                par = qt % 2
                gt_t, gt_r = gts[par]
                yb, yb_r = ybs[par]
                y32, y32_r = y32s[par]
                rd, rd_r = rdn[j]
                sl = slice(qt * 512, (qt + 1) * 512)
                if j == 0:
                    P.dma("sp", gt_t, self.GT[m, :, sl], reads=[self.GT_r], writes=[gt_r], tile=gt_r)
                P.op("dve", lambda e: e.reciprocal(out=rd[0:64, :], in_=pb[64:128, :]), reads=[pb_r], writes=[rd_r])
                P.op("dve", lambda e: e.tensor_tensor(out=y32[64 * j:64 * j + 64, :], in0=pb[0:64, :], in1=rd[0:64, :], op=ALU.mult),
                     reads=[pb_r, rd_r], writes=[y32_r] if j == 0 else (), acc=() if j == 0 else [y32_r])
                P.op("pool", lambda e: e.tensor_tensor(out=yb[64 * j:64 * j + 64, :], in0=y32[64 * j:64 * j + 64, :], in1=gt_t[64 * j:64 * j + 64, :], op=ALU.mult),
                     reads=[y32_r, gt_r], writes=[yb_r] if j == 0 else (), acc=() if j == 0 else [yb_r])
                if j == 1:
                    dst_ap = self.y_dst(m, qt) if getattr(self, "y_dst", None) is not None else yT_ap[m * 128:(m + 1) * 128, sl]
                    o = P.dma("pool", dst_ap, yb, reads=[yb_r], acc=[self.yT_r], tile=yb_r)
                    if self.yT_external:
                        P.finish(o)

        for i in range(n + LOOK):
            if i < n:
                stage_S(i)
            if i - LOOK >= 0:
                stage_PV(i - LOOK)
        P.barrier()

    def phase_B(self, yT_ap):
        P, ar, S, NT, NQ = self.P, self.ar, self.S, self.NT, self.NQ
        m = 1
        ar.reset()
        QT, KT = self.load_qk(m)
        v_t, v_r = ar.alloc(NT * 128, BF16, f"V{m}")
        v3 = v_t.rearrange("p (n c) -> p n c", n=NT)
        for i in range(4):
            n0, n1 = i * NT // 4, (i + 1) * NT // 4
            P.dma("sp", v3[:, n0:n1, :], self.V[n0 * 128:n1 * 128, m * 128:(m + 1) * 128].rearrange("(n p) c -> p n c", p=128),
                  reads=[self.V_r], acc=[v_r], tile=v_r)
        NB = 4
        AT = [ar.alloc(512, BF16, f"AT{i}") for i in range(NB)]
        E = [ar.alloc(512, F32, f"E{i}") for i in range(NB)]
        SP = [ar.alloc(512, F32, f"SP{i}") for i in range(NB)]
        SS = [[ar.alloc(512, F32, f"SS{j}{i}") for i in range(2)] for j in range(2)]
        gts = [ar.alloc(512, BF16, f"gt{i}") for i in range(2)]
        ybs = [ar.alloc(512, BF16, f"yb{i}") for i in range(2)]
        sb = [self.banks[i] for i in range(NB)]
        pos = [self.banks[4], self.banks[5]]
        cf, cf_r = self.cf, self.cf_r
        seq = []
        for qt in range(NQ):
            nkb = 4 * qt + 4
            for idx, kb in enumerate(range(nkb - 1, -1, -1)):
                for j in range(2):
                    seq.append((qt, j, kb, idx, nkb))
        n = len(seq)

        def st1(i):
            qt, j, kb, idx, nkb = seq[i]
            s_t, s_r = sb[i % NB]
            e_t, e_r = E[i % NB]
            p_t, p_r = SP[i % NB]
            q_t, q_r = QT[j]
            k_t, k_r = KT[j]
            d = kb - 4 * qt
            P.op("pe", lambda e: e.matmul(s_t, lhsT=k_t[0:64, kb * 128:(kb + 1) * 128], rhs=q_t[0:64, qt * 512:(qt + 1) * 512], start=True, stop=(d < 0)),
                 reads=[k_r, q_r], writes=[s_r])
            if d >= 0:
                P.op("pe", lambda e: e.matmul(s_t, lhsT=self.idb, rhs=self.cmbs[:, d * 512:(d + 1) * 512], start=False, stop=True),
                     reads=[self.idb_r, self.cmbs_r], acc=[s_r])
            P.op("act", lambda e: e.activation(out=e_t, in_=s_t, func=AF.Exp), reads=[s_r], writes=[e_r])
            P.op("act", lambda e: e.activation(out=p_t, in_=e_t, func=AF.Ln, bias=1.0, scale=1.0), reads=[e_r], writes=[p_r])

        def st2(i):
            qt, j, kb, idx, nkb = seq[i]
            s_t, s_r = sb[i % NB]
            p_t, p_r = SP[i % NB]
            a_t, a_r = AT[i % NB]
            so_t, so_r = SS[j][(idx + 1) % 2]
            sn_t, sn_r = SS[j][idx % 2]
            P.op("pe", lambda e: e.matmul(s_t, lhsT=cf[:, C_NTRI:C_NTRI + 128], rhs=p_t, start=False, stop=(idx == 0), skip_group_check=True),
                 reads=[cf_r, p_r], acc=[s_r])
            if idx > 0:
                P.op("pe", lambda e: e.matmul(s_t, lhsT=cf[:, C_NONES:C_NONES + 128], rhs=so_t, start=False, stop=True, skip_group_check=True),
                     reads=[cf_r, so_r], acc=[s_r])
            if kb > 0:
                if idx == 0:
                    P.op("pool", lambda e: e.tensor_copy(out=sn_t, in_=p_t), reads=[p_r], writes=[sn_r])
                else:
                    P.op("pool", lambda e: e.tensor_tensor(out=sn_t, in0=so_t, in1=p_t, op=ALU.add), reads=[so_r, p_r], writes=[sn_r])
            P.op("act", lambda e: e.activation(out=a_t, in_=s_t, func=AF.Exp), reads=[s_r], writes=[a_r])

        def st3(i):
            qt, j, kb, idx, nkb = seq[i]
            a_t, a_r = AT[i % NB]
            pO, pO_r = pos[qt % 2]
            first = (idx == 0 and j == 0)
            P.op("pe", lambda e: e.matmul(pO[64 * j:64 * j + 64, :], lhsT=v3[:, kb, 64 * j:64 * j + 64], rhs=a_t, start=(idx == 0), stop=(idx == nkb - 1)),
                 reads=[a_r, v_r], writes=[pO_r] if first else (), acc=() if first else [pO_r])
            if idx == nkb - 1 and j == 1:
                par = qt % 2
                gt_t, gt_r = gts[par]
                yb, yb_r = ybs[par]
                sl = slice(qt * 512, (qt + 1) * 512)
                P.dma("sp", gt_t, self.GT[m, :, sl], reads=[self.GT_r], writes=[gt_r], tile=gt_r)
                P.op("dve", lambda e: e.tensor_tensor(out=yb, in0=pO, in1=gt_t, op=ALU.mult), reads=[pO_r, gt_r], writes=[yb_r])
                dst_ap = self.y_dst(m, qt) if getattr(self, "y_dst", None) is not None else yT_ap[m * 128:(m + 1) * 128, sl]
                o = P.dma("pool", dst_ap, yb, reads=[yb_r], acc=[self.yT_r], tile=yb_r)
                if self.yT_external:
                    P.finish(o)

        for i in range(n + 2):
            if i < n:
                st1(i)
            if 0 <= i - 1 < n:
                st2(i - 1)
            if 0 <= i - 2 < n:
                st3(i - 2)
        P.barrier()

    def make_strict_mask(self):
        P, cst = self.P, self.cst
        self.cmbs, self.cmbs_r = cst.alloc(2048, BF16, "cmbs")
        cm3 = self.cmb.rearrange("p (d q) -> p d q", d=4)
        cs3 = self.cmbs.rearrange("p (d q) -> p d q", d=4)
        P.op("dve", lambda e: e.memset(self.cmbs, NEGM), writes=[self.cmbs_r])
        P.op("dve", lambda e: e.tensor_copy(out=cs3[:, :, 1:512], in_=cm3[:, :, 0:511]), reads=[self.cmb_r], writes=[self.cmbs_r])

    def phase_A(self, yT_ap):
        P, ar, S, NT, NQ = self.P, self.ar, self.S, self.NT, self.NQ
        m = 0
        ar.reset()
        QT, KT, (v3, v_r) = self.load_qkv(m)
        bufs = self.out_bufs()
        num, num_r = ar.alloc(S, F32, "numA")
        den, den_r = ar.alloc(S, F32, "denA")
        v_t2 = v3
        AT = [ar.alloc(256, BF16, f"ATa{i}") for i in range(3)]
        sb = [self.banks[i] for i in range(3)]
        pOs = [self.banks[3], self.banks[4]]
        pDs = [self.banks[5], self.banks[6]]
        it = 0
        grp = 0
        for dil in (1, 4, 16):
            L = S // dil
            nb = L // 128
            vt3, vt_r = v3, v_r
            if dil > 1:
                src = self.V[:, m * 128:(m + 1) * 128].rearrange("(n i r) c -> r i n c", i=128, r=dil)
                for r in range(dil):
                    P.dma("sp", v3[:, r * nb:(r + 1) * nb, :], src[r], reads=[self.V_r],
                          writes=[v_r] if r == 0 else (), acc=() if r == 0 else [v_r], tile=v_r)
            for r in range(dil):
                gs = min(4, nb)
                for n0 in range(0, nb, gs):
                    pO, pO_r = pOs[grp % 2]
                    pD, pD_r = pDs[grp % 2]
                    grp += 1
                    for j in range(2):
                        q_t, q_r = QT[j]
                        k_t, k_r = KT[j]
                        for n in range(n0, n0 + gs):
                            s_t, s_r = sb[it % 3]
                            a_t, a_r = AT[it % 3]
                            it += 1
                            def tok(bi, dil=dil, r=r):
                                return slice(r + dil * 128 * bi, r + dil * 128 * bi + dil * 127 + 1, dil)
                            c0 = 0 if n > 0 else 128
                            if n > 0:
                                P.op("pe", lambda e, s_t=s_t, k_t=k_t, q_t=q_t, n=n, tok=tok: e.matmul(s_t[:, 0:128], lhsT=k_t[0:64, tok(n - 1)], rhs=q_t[0:64, tok(n)], start=True, stop=False),
                                     reads=[k_r, q_r], writes=[s_r])
                            P.op("pe", lambda e, s_t=s_t, k_t=k_t, q_t=q_t, n=n, tok=tok: e.matmul(s_t[:, 128:256], lhsT=k_t[0:64, tok(n)], rhs=q_t[0:64, tok(n)], start=(n == 0), stop=False),
                                 reads=[k_r, q_r], writes=[s_r] if n == 0 else (), acc=() if n == 0 else [s_r])
                            P.op("pe", lambda e, s_t=s_t, c0=c0: e.matmul(s_t[:, c0:256], lhsT=self.idb, rhs=self.bmb[:, c0:256], start=False, stop=True),
                                 reads=[self.idb_r, self.bmb_r], acc=[s_r])
                            P.op("act", lambda e, s_t=s_t, a_t=a_t, c0=c0: e.activation(out=a_t[:, c0:256], in_=s_t[:, c0:256], func=AF.Exp), reads=[s_r], writes=[a_r])
                            cs_ = (n - n0) * 128
                            first = (j == 0 and n == n0)
                            kbs = ([(n - 1, 0)] if n > 0 else []) + [(n, 128)]
                            for ii, (kbi, ac) in enumerate(kbs):
                                P.op("pe", lambda e, a_t=a_t, kbi=kbi, ac=ac, j=j, cs_=cs_, ii=ii, nk=len(kbs), r=r, nb=nb, pO=pO, vt3=vt3: e.matmul(
                                        pO[64 * j:64 * j + 64, cs_:cs_ + 128], lhsT=vt3[:, r * nb + kbi, 64 * j:64 * j + 64], rhs=a_t[:, ac:ac + 128], start=(ii == 0), stop=(ii == nk - 1)),
                                     reads=[a_r, vt_r], writes=[pO_r] if (first and ii == 0) else (), acc=() if (first and ii == 0) else [pO_r])
                                P.op("pe", lambda e, a_t=a_t, ac=ac, j=j, cs_=cs_, ii=ii, nk=len(kbs), pD=pD: e.matmul(
                                        pD[64 * j:64 * j + 64, cs_:cs_ + 128], lhsT=self.oneb[:, 0:64], rhs=a_t[:, ac:ac + 128], start=(ii == 0), stop=(ii == nk - 1)),
                                     reads=[a_r, self.oneb_r], writes=[pD_r] if (first and ii == 0) else (), acc=() if (first and ii == 0) else [pD_r])
                    t0 = r + dil * 128 * n0
                    tsl = slice(t0, t0 + dil * (gs * 128 - 1) + 1, dil)
                    gw = gs * 128
                    if dil == 1:
                        P.op("dve", lambda e, pO=pO, tsl=tsl, gw=gw: e.tensor_copy(out=num[:, tsl], in_=pO[:, 0:gw]), reads=[pO_r], acc=[num_r])
                        P.op("act", lambda e, pD=pD, tsl=tsl, gw=gw: e.copy(out=den[:, tsl], in_=pD[:, 0:gw]), reads=[pD_r], acc=[den_r])
                    else:
                        P.op("dve", lambda e, pO=pO, tsl=tsl, gw=gw: e.tensor_tensor(out=num[:, tsl], in0=num[:, tsl], in1=pO[:, 0:gw], op=ALU.add), reads=[pO_r, num_r], acc=[num_r])
                        P.op("dve", lambda e, pD=pD, tsl=tsl, gw=gw: e.tensor_tensor(out=den[:, tsl], in0=den[:, tsl], in1=pD[:, 0:gw], op=ALU.add), reads=[pD_r, den_r], acc=[den_r])
        for qt in range(NQ):
            sl = slice(qt * 512, (qt + 1) * 512)
            self.finish_qtile(m, qt, num[:, sl], num_r, den[:, sl], den_r, yT_ap, bufs)
        P.barrier()


def _phase_O(lb, YG_ap, YG_r, wo_ap, x_ap, x_r, out_ap, out_r, tok0, ntok, out_row0, external_out):
    P, ar = lb.P, lb.ar
    ar.reset()
    Wb, Wb_r = ar.alloc(NCH * D_MODEL, BF16, "Wob")
    Wb3 = Wb.rearrange("p (c n) -> p c n", c=NCH)
    wst = [ar.alloc(D_MODEL, F32, f"wost{i}") for i in range(2)]
    for q in range(NCH):
        r, m = q // 4, q % 4
        wrow = (m * 4 + r) * 128
        ws, ws_r = wst[q % 2]
        P.dma("sp", ws, wo_ap[wrow:wrow + 128, :], writes=[ws_r], tile=ws_r)
        eng = "dve" if q % 2 == 0 else "pool"
        P.op(eng, lambda e, ws=ws, q=q: e.tensor_copy(out=Wb3[:, q, :], in_=ws), reads=[ws_r], acc=[Wb_r])
    yt = [ar.alloc(NCH * 128, BF16, f"oyt{i}") for i in range(2)]
    xs = [ar.alloc(D_MODEL, F32, f"oxs{i}") for i in range(2)]
    xos = [ar.alloc(D_MODEL, F32, f"oxo{i}") for i in range(2)]
    bi = 0
    for ti in range(ntok // 128):
        t0 = tok0 + ti * 128
        y_t, y_r = yt[ti % 2]
        x_t, xr_ = xs[ti % 2]
        o_t, o_r = xos[ti % 2]
        y3 = y_t.rearrange("p (c s) -> p c s", c=NCH)
        P.dma("sp", y3, YG_ap(t0).rearrange("(c p) s -> p c s", p=128), reads=[YG_r], writes=[y_r], tile=y_r)
        P.dma("sp", x_t, x_ap[t0:t0 + 128, :], reads=[x_r] if x_r is not None else (), writes=[xr_], tile=xr_)
        for cg in range(4):
            bk, bk_r = lb.banks[bi % 8]
            bi += 1
            for c in range(NCH):
                P.op("pe", lambda e, bk=bk, c=c, cg=cg, y3=y3: e.matmul(bk, lhsT=y3[:, c, :], rhs=Wb3[:, c, cg * 512:(cg + 1) * 512], start=(c == 0), stop=(c == NCH - 1)),
                     reads=[y_r, Wb_r], writes=[bk_r] if c == 0 else (), acc=() if c == 0 else [bk_r])
            P.op("dve", lambda e, bk=bk, cg=cg, o_t=o_t, x_t=x_t: e.tensor_tensor(out=o_t[:, cg * 512:(cg + 1) * 512], in0=x_t[:, cg * 512:(cg + 1) * 512], in1=bk, op=ALU.add),
                 reads=[bk_r, xr_], acc=[o_r])
        orow = out_row0 + ti * 128
        o = P.dma("pool", out_ap[orow:orow + 128, :], o_t, reads=[o_r], acc=[out_r] if out_r is not None else (), tile=o_r)
        if external_out:
            P.finish(o)
    P.barrier()


def build_fused_program(S, depth, groups=((0, 1, 2, 3), (4, 5, 6, 7))):
    nc = bass.Bass("TRN2", target_bir_lowering=False)
    x = nc.dram_tensor("x", [S, D_MODEL], F32, kind="ExternalInput").ap()
    wc = nc.dram_tensor("wc", [depth, D_MODEL, WC], F32, kind="ExternalInput").ap()
    ng = nc.dram_tensor("ng", [depth, 128, NCH], F32, kind="ExternalInput").ap()
    gqk = nc.dram_tensor("gqk", [depth, 2, 512], F32, kind="ExternalInput").ap()
    fb = nc.dram_tensor("fb", [depth, 1, 2], F32, kind="ExternalInput").ap()
    cs = nc.dram_tensor("cs", [S, 64], F32, kind="ExternalInput").ap()
    consts = nc.dram_tensor("consts", [128, C_TOT], F32, kind="ExternalInput").ap()
    wo = nc.dram_tensor("wo", [depth, D_MODEL, D_MODEL], F32, kind="ExternalInput").ap()
    xo = nc.dram_tensor("xo", [S, D_MODEL], F32, kind="ExternalOutput").ap()
    Xb = [nc.dram_tensor(f"Xbuf{i}", [S, D_MODEL], F32).ap() for i in range(2)]
    PART = 1024
    NP = S // PART
    YL = [[nc.dram_tensor(f"YL{i}_{p}", [512, PART], BF16).ap() for p in range(NP)] for i in range(2)]
    YG = [[nc.dram_tensor(f"YG{i}_{p}", [4 * 512, PART], BF16).ap() for p in range(NP)] for i in range(2)]
    P = Prog(nc)
    st = contextlib.ExitStack()
    with st:
        lb = LayerBuilder(nc, P, st, S)
        lb.yT_external = False
        lb.load_consts(consts)
        lb.make_strict_mask()
        Xb_r = [P.res("Xb0"), P.res("Xb1")]
        YL_r = [P.res("YL0"), P.res("YL1")]
        YG_r = [P.res("YG0"), P.res("YG1")]
        cc_r = P.res("cc")
        xo_r = P.res("xo")
        grp = [list(g_) for g_ in groups]
        import os as _os
        for l in range(depth):
            src, src_r = (x, None) if l == 0 else (Xb[(l - 1) % 2], Xb_r[(l - 1) % 2])
            yl, yl_r = YL[l % 2], YL_r[l % 2]
            yg, yg_r = YG[l % 2], YG_r[l % 2]
            lb.yT_r = yl_r
            lb.x_src_r = src_r
            lb.y_dst = lambda m, qt, yl=yl: yl[(qt * 512) // PART][m * 128:(m + 1) * 128, (qt * 512) % PART:(qt * 512) % PART + 512]
            lb.phase1(src, wc[l], ng[l], gqk[l], fb[l], cs)
            lb.phase_A(None)
            lb.phase_B(None)
            lb.phase_dense(2, None)
            lb.phase_dense(3, None)
            if not _os.environ.get("FUSED_NOCC"):
                for p in range(NP):
                    P.collective(lambda e, a=yl[p], b_=yg[p]: e.collective_compute("AllGather", ALU.bypass, replica_groups=grp, ins=[a], outs=[b_]),
                                 [yl_r], [yg_r] if p == 0 else [], cc_r)
                    if p > 0:
                        yg_r.w.append(P.ops["pool"][-1])
            last = (l == depth - 1)
            yg_fn = lambda t0, yg=yg: yg[t0 // PART][:, t0 % PART:t0 % PART + 128]
            if last:
                _phase_O(lb, yg_fn, yg_r, wo[l], src, src_r, xo, xo_r, 0, S, 0, True)
            else:
                _phase_O(lb, yg_fn, yg_r, wo[l], src, src_r, Xb[l % 2], Xb_r[l % 2], 0, S, 0, False)
        P.emit()
    return nc, P


def build_layer_program(S, phases="1ABCD"):
    nc = bass.Bass("TRN2", target_bir_lowering=False)
    x = nc.dram_tensor("x", [S, D_MODEL], F32, kind="ExternalInput").ap()
    wc = nc.dram_tensor("wc", [D_MODEL, WC], F32, kind="ExternalInput").ap()
    ng = nc.dram_tensor("ng", [128, NCH], F32, kind="ExternalInput").ap()
    gqk = nc.dram_tensor("gqk", [2, 512], F32, kind="ExternalInput").ap()
    fb = nc.dram_tensor("fb", [1, 2], F32, kind="ExternalInput").ap()
    cs = nc.dram_tensor("cs", [S, 64], F32, kind="ExternalInput").ap()
    consts = nc.dram_tensor("consts", [128, C_TOT], F32, kind="ExternalInput").ap()
    yT = nc.dram_tensor("yT", [512, S], BF16, kind="ExternalOutput").ap()
    P = Prog(nc)
    st = contextlib.ExitStack()
    with st:
        lb = LayerBuilder(nc, P, st, S)
        lb.load_consts(consts)
        lb.make_strict_mask()
        if "1" in phases:
            lb.phase1(x, wc, ng, gqk, fb, cs)
        if "A" in phases:
            lb.phase_A(yT)
        if "B" in phases:
            lb.phase_B(yT)
        if "C" in phases:
            lb.phase_dense(2, yT)
        if "D" in phases:
            lb.phase_dense(3, yT)
        P.emit()
    return nc, P


def build_out_program(T):
    nc = bass.Bass("TRN2", target_bir_lowering=False)
    yT = nc.dram_tensor("yT", [D_MODEL, T], BF16, kind="ExternalInput").ap()
    wo = nc.dram_tensor("wo", [D_MODEL, D_MODEL], F32, kind="ExternalInput").ap()
    x = nc.dram_tensor("x", [T, D_MODEL], F32, kind="ExternalInput").ap()
    xo = nc.dram_tensor("xo", [T, D_MODEL], F32, kind="ExternalOutput").ap()
    P = Prog(nc)
    st = contextlib.ExitStack()
    with st:
        ar = Arena(nc, st, P, "arena", 160 * 1024)
        banks = []
        for i in range(8):
            t = st.enter_context(nc.psum_tensor(f"bank{i}", [128, 512], F32))
            banks.append((t[:, :], P.res(f"bank{i}")))
        Wb, Wb_r = ar.alloc(NCH * D_MODEL, BF16, "Wob")
        Wb3 = Wb.rearrange("p (c n) -> p c n", c=NCH)
        wst = [ar.alloc(D_MODEL, F32, f"wst{i}") for i in range(2)]
        for c in range(NCH):
            ws, ws_r = wst[c % 2]
            P.dma("sp", ws, wo[c * 128:(c + 1) * 128, :], writes=[ws_r], tile=ws_r)
            eng = "dve" if c % 2 == 0 else "pool"
            P.op(eng, lambda e, ws=ws, c=c: e.tensor_copy(out=Wb3[:, c, :], in_=ws), reads=[ws_r], acc=[Wb_r])
        yt = [ar.alloc(NCH * 128, BF16, f"yt{i}") for i in range(2)]
        xs = [ar.alloc(D_MODEL, F32, f"xs{i}") for i in range(2)]
        xos = [ar.alloc(D_MODEL, F32, f"xo{i}") for i in range(2)]
        NT = T // 128
        bi = 0
        for t in range(NT):
            y_t, y_r = yt[t % 2]
            x_t, x_r = xs[t % 2]
            o_t, o_r = xos[t % 2]
            y3 = y_t.rearrange("p (c s) -> p c s", c=NCH)
            P.dma("sp", y3, yT[:, t * 128:(t + 1) * 128].rearrange("(c p) s -> p c s", p=128), writes=[y_r], tile=y_r)
            P.dma("sp", x_t, x[t * 128:(t + 1) * 128, :], writes=[x_r], tile=x_r)
            for cg in range(4):
                bk, bk_r = banks[bi % 8]
                bi += 1
                for c in range(NCH):
                    P.op("pe", lambda e, bk=bk, c=c, cg=cg, y3=y3: e.matmul(bk, lhsT=y3[:, c, :], rhs=Wb3[:, c, cg * 512:(cg + 1) * 512], start=(c == 0), stop=(c == NCH - 1)),
                         reads=[y_r, Wb_r], writes=[bk_r] if c == 0 else (), acc=() if c == 0 else [bk_r])
                P.op("dve", lambda e, bk=bk, cg=cg, o_t=o_t, x_t=x_t: e.tensor_tensor(out=o_t[:, cg * 512:(cg + 1) * 512], in0=x_t[:, cg * 512:(cg + 1) * 512], in1=bk, op=ALU.add),
                     reads=[bk_r, x_r], acc=[o_r])
            o = P.dma("pool", xo[t * 128:(t + 1) * 128, :], o_t, reads=[o_r], tile=o_r)
            P.finish(o)
        P.emit()
    return nc, P


_CACHE = {}


def _layer_prog(S):
    if ("L", S) not in _CACHE:
        _CACHE[("L", S)] = build_layer_program(S)[0]
    return _CACHE[("L", S)]


def _out_prog(T):
    if ("O", T) not in _CACHE:
        _CACHE[("O", T)] = build_out_program(T)[0]
    return _CACHE[("O", T)]


def pack_layer_inputs(x_b, norm_gain_l, w_in_l, qn_l, kn_l, fb_l, g, cs, consts):
    cols = []
    for t in range(4):
        for m in range(4):
            c0 = t * 2048 + m * 512 + g * 128
            cols.append(np.arange(c0, c0 + 128))
    cols.append(np.array([8192 + 2 * g, 8192 + 2 * g + 1]))
    cols = np.concatenate(cols)
    wc = np.ascontiguousarray(w_in_l[:, cols])
    ng = np.ascontiguousarray(norm_gain_l.reshape(NCH, 128).T)
    one = np.ones(64, np.float32)
    gq = np.concatenate([qn_l[0], qn_l[0], one, one, qn_l[1], qn_l[1], qn_l[2], qn_l[2]])
    gk = np.concatenate([kn_l[0], kn_l[0], one, one, kn_l[1], kn_l[1], kn_l[2], kn_l[2]])
    gqk = np.ascontiguousarray(np.stack([gq, gk]).astype(np.float32))
    fb = np.ascontiguousarray(fb_l[2 * g:2 * g + 2].reshape(1, 2).astype(np.float32))
    return {"x": x_b, "wc": wc, "ng": ng, "gqk": gqk, "fb": fb, "cs": cs, "consts": consts}


def pack_fused_inputs(x_b, norm_gain, w_in, qn, kn, fbias, w_out, g, cs, consts):
    depth = norm_gain.shape[0]
    per = [pack_layer_inputs(None, norm_gain[l], w_in[l], qn[l], kn[l], fbias[l], g, cs, consts) for l in range(depth)]
    return {"x": x_b,
            "wc": np.ascontiguousarray(np.stack([p["wc"] for p in per])),
            "ng": np.ascontiguousarray(np.stack([p["ng"] for p in per])),
            "gqk": np.ascontiguousarray(np.stack([p["gqk"] for p in per])),
            "fb": np.ascontiguousarray(np.stack([p["fb"] for p in per])),
            "cs": cs, "consts": consts, "wo": w_out}


def _fused_prog(S, depth):
    if ("F", S, depth) not in _CACHE:
        _CACHE[("F", S, depth)] = build_fused_program(S, depth)[0]
    return _CACHE[("F", S, depth)]


def kernel(x, norm_gain, w_in, q_norm_gain, k_norm_gain, forget_bias, w_out):
    x = np.ascontiguousarray(np.asarray(x, dtype=np.float32))
    B, S, D = x.shape
    depth = norm_gain.shape[0]
    norm_gain = np.asarray(norm_gain, np.float32)
    w_in = np.asarray(w_in, np.float32)
    q_norm_gain = np.asarray(q_norm_gain, np.float32)
    k_norm_gain = np.asarray(k_norm_gain, np.float32)
    forget_bias = np.asarray(forget_bias, np.float32)
    w_out = np.ascontiguousarray(np.asarray(w_out, np.float32))
    cs = rope_table(S)
    consts = make_consts()
    assert B == 2
    nc = _fused_prog(S, depth)
    in_maps = []
    for c in range(8):
        b, g = c // 4, c % 4
        in_maps.append(pack_fused_inputs(x[b], norm_gain, w_in, q_norm_gain, k_norm_gain, forget_bias, w_out, g, cs, consts))
    res = run_bass_kernel_spmd(nc, in_maps, core_ids=list(range(8)))
    out = np.stack([np.asarray(res.results[0]["xo"]), np.asarray(res.results[4]["xo"])], axis=0)
    return out.astype(np.float32)


def kernel_unfused(x, norm_gain, w_in, q_norm_gain, k_norm_gain, forget_bias, w_out):
    x = np.ascontiguousarray(np.asarray(x, dtype=np.float32))
    B, S, D = x.shape
    depth = norm_gain.shape[0]
    norm_gain = np.asarray(norm_gain, np.float32)
    w_in = np.asarray(w_in, np.float32)
    q_norm_gain = np.asarray(q_norm_gain, np.float32)
    k_norm_gain = np.asarray(k_norm_gain, np.float32)
    forget_bias = np.asarray(forget_bias, np.float32)
    w_out = np.asarray(w_out, np.float32)
    cs = rope_table(S)
    consts = make_consts()
    ncores = 8
    T = B * S // ncores
    cur = x
    for l in range(depth):
        ncL = _layer_prog(S)
        in_maps = []
        for c in range(ncores):
            b, g = c // 4, c % 4
            in_maps.append(pack_layer_inputs(cur[b], norm_gain[l], w_in[l], q_norm_gain[l], k_norm_gain[l], forget_bias[l], g, cs, consts))
        res = run_bass_kernel_spmd(ncL, in_maps, core_ids=list(range(ncores)))
        yT_full = []
        for b in range(B):
            yb = np.zeros((4, 8, 64, S), dtype=ml_dtypes.bfloat16)
            for g in range(4):
                y = np.asarray(res.results[b * 4 + g]["yT"]).reshape(4, 2, 64, S)
                yb[:, 2 * g:2 * g + 2] = y
            yT_full.append(yb.reshape(2048, S))
        yT_cat = np.concatenate(yT_full, axis=1)
        xf = cur.reshape(B * S, D)
        ncO = _out_prog(T)
        in_maps = []
        for c in range(ncores):
            in_maps.append({"yT": np.ascontiguousarray(yT_cat[:, c * T:(c + 1) * T]), "wo": w_out[l],
                            "x": np.ascontiguousarray(xf[c * T:(c + 1) * T])})
        res = run_bass_kernel_spmd(ncO, in_maps, core_ids=list(range(ncores)))
        cur = np.concatenate([np.asarray(r["xo"]) for r in res.results], axis=0).reshape(B, S, D)
    return cur.astype(np.float32)
```

```python
import contextlib
import numpy as np
import ml_dtypes
import concourse.bass as bass
import concourse.mybir as mybir
from concourse.bass_utils import run_bass_kernel_spmd

F32 = mybir.dt.float32
BF16 = mybir.dt.bfloat16
AF = mybir.ActivationFunctionType
ALU = mybir.AluOpType
AX = mybir.AxisListType

D_MODEL = 2048
NCH = 16
WC = 2050
HD = 64
NEGM = -30000.0
SCALE = 0.125
EPS = 1e-6
NAUG = (0, 0, 32, 3)
ROWS = tuple(a + 64 for a in NAUG)


class Res:
    __slots__ = ("name", "w", "r", "uid", "pre")
    _n = [0]

    def __init__(self, name):
        Res._n[0] += 1
        self.uid = Res._n[0]
        self.name = name
        self.w = []
        self.r = []
        self.pre = []


class Op:
    __slots__ = ("eng", "fn", "deps", "has_dependents", "semkey", "value", "is_dma", "seq", "inc")

    def __init__(self, eng, fn, is_dma, semkey):
        self.seq = 0
        self.inc = 16 if is_dma else 1
        self.eng = eng
        self.fn = fn
        self.deps = []
        self.has_dependents = bool(is_dma)
        self.semkey = semkey
        self.value = None
        self.is_dma = is_dma


class Prog:
    ENGS = ("pe", "act", "dve", "pool", "sp")

    def __init__(self, nc):
        self.nc = nc
        self.ops = {e: [] for e in self.ENGS}
        self.nres = 0
        self.final_ops = []
        self.pending_barrier = {e: [] for e in self.ENGS}
        self.dma_since_barrier = []

    def res(self, name=None):
        self.nres += 1
        return Res(name or f"r{self.nres}")

    def _add(self, eng, fn, reads, writes, acc, is_dma, semkey, after=()):
        op = Op(eng, fn, is_dma, semkey)
        deps = list(self.pending_barrier[eng]) + list(after)
        self.pending_barrier[eng] = []
        for r in reads:
            deps.extend(r.w)
        for w in writes:
            deps.extend(w.r)
            deps.extend(w.w)
        for w in acc:
            deps.extend(w.r)
            deps.extend(w.pre)
        seen = set()
        for d in deps:
            if id(d) in seen:
                continue
            seen.add(id(d))
            if d.eng == eng and not d.is_dma and not is_dma and eng == "pe":
                continue
            d.has_dependents = True
            op.deps.append(d)
        for r in reads:
            r.r.append(op)
        for w in writes:
            w.pre = [d for d in (w.r + w.w) if d.eng != eng or d.is_dma]
            w.w = [op]
            w.r = []
        for w in acc:
            w.w.append(op)
        self.ops[eng].append(op)
        self.seqn = getattr(self, "seqn", 0) + 1
        op.seq = self.seqn
        if is_dma:
            self.dma_since_barrier.append(op)
            self.dma_by_key = getattr(self, "dma_by_key", {})
            self.dma_by_key.setdefault(semkey, []).append(op)
        return op

    def op(self, eng, fn, reads=(), writes=(), acc=(), after=()):
        return self._add(eng, fn, reads, writes, acc, False, ("eng", eng), after)

    def dma(self, eng, out_ap, in_ap, reads=(), writes=(), acc=(), tile=None, **kw):
        fn = lambda e: e.dma_start(out=out_ap, in_=in_ap, **kw)
        return self._add(eng, fn, reads, writes, acc, True, ("res", tile.name))

    def collective(self, fn, reads, writes, res):
        op = self._add("pool", fn, reads, writes, (), True, ("res", res.name))
        op.inc = 1
        return op

    def finish(self, op):
        op.has_dependents = True
        self.final_ops.append(op)

    def barrier(self):
        deps = []
        for e in self.ENGS:
            for op in reversed(self.ops[e]):
                if not op.is_dma:
                    deps.append(op)
                    break
        deps.extend(self.dma_since_barrier)
        self.dma_since_barrier = []
        for e in self.ENGS:
            self.pending_barrier[e] = list(self.pending_barrier[e]) + deps

    def emit(self):
        nc = self.nc
        semvals = {}
        semkeys = []
        for e in self.ENGS:
            for op in self.ops[e]:
                if op.has_dependents:
                    k = op.semkey
                    if k not in semvals:
                        semvals[k] = 0
                        semkeys.append(k)
                    semvals[k] += op.inc
                    op.value = semvals[k]
        stack = contextlib.ExitStack()
        sems = {}
        for i, k in enumerate(semkeys):
            sems[k] = stack.enter_context(nc.semaphore(f"s{i}"))
        self.nsems = len(semkeys)
        ops = self.ops
        final_ops = self.final_ops

        import bisect
        dma_by_key = getattr(self, "dma_by_key", {})
        dma_seqs = {k: [o.seq for o in v] for k, v in dma_by_key.items()}

        def need(op_seq, deps):
            req = {}
            for d in deps:
                k = d.semkey
                v = d.value
                if d.is_dma:
                    lst = dma_by_key[k]
                    i = bisect.bisect_left(dma_seqs[k], op_seq) - 1
                    if i >= 0 and lst[i].value > v:
                        v = lst[i].value
                if req.get(k, 0) < v:
                    req[k] = v
            return req

        def run(ename, e):
            seen = {}
            for op in ops[ename]:
                for k, v in need(op.seq, op.deps).items():
                    if seen.get(k, 0) >= v:
                        continue
                    e.wait_ge(sems[k], v)
                    seen[k] = v
                ins = op.fn(e)
                if op.has_dependents:
                    ins.then_inc(sems[op.semkey], op.inc)
            if ename == "sp":
                for k, v in need(10 ** 12, final_ops).items():
                    if seen.get(k, 0) >= v:
                        continue
                    e.wait_ge(sems[k], v)
                    seen[k] = v

        with stack:
            with nc.Block() as block:
                @block.tensor
                def _(e):
                    run("pe", e)

                @block.scalar
                def _(e):
                    run("act", e)

                @block.vector
                def _(e):
                    run("dve", e)

                @block.gpsimd
                def _(e):
                    run("pool", e)

                @block.sync
                def _(e):
                    run("sp", e)


class _PH:
    __slots__ = ("real",)

    def __init__(self):
        self.real = None


class Deferred:
    def __init__(self):
        self.calls = []

    def op(self, *a, **kw):
        ph = _PH()
        self.calls.append(("op", a, kw, ph))
        return ph

    def dma(self, *a, **kw):
        ph = _PH()
        self.calls.append(("dma", a, kw, ph))
        return ph


def replay_interleaved(P, streams):
    idx = [0] * len(streams)
    live = True
    while live:
        live = False
        for si, st_ in enumerate(streams):
            if idx[si] < len(st_.calls):
                kind, a, kw, ph = st_.calls[idx[si]]
                idx[si] += 1
                if "after" in kw:
                    kw = dict(kw)
                    kw["after"] = [x.real if isinstance(x, _PH) else x for x in kw["after"]]
                ph.real = getattr(P, kind)(*a, **kw)
                live = True


class Arena:
    def __init__(self, nc, st, P, name, nbytes):
        self.t = st.enter_context(nc.sbuf_tensor(name, [128, nbytes // 2], BF16))
        self.cap = nbytes
        self.off = 0
        self.P = P
        self.name = name
        self.n = 0

    def alloc(self, cols, dt, name=None):
        esz = 4 if dt == F32 else 2
        off = (self.off + 63) // 64 * 64
        size = cols * esz
        assert off + size <= self.cap, (self.name, name, off, size, self.cap)
        ap = self.t[:, off // 2:(off + size) // 2]
        if dt == F32:
            ap = ap.bitcast(F32)
        self.off = off + size
        self.n += 1
        return ap, self.P.res(f"{self.name}.{name or self.n}")

    def mark(self):
        return self.off

    def reset(self, mark=0):
        self.off = mark


C_ID = 0
C_TRI = 128
C_ONES = 256
C_NTRI = 384
C_NONES = 512
C_CM = 640
C_BM = 640 + 2048
C_TOT = C_BM + 256


def make_consts():
    c = np.zeros((128, C_TOT), np.float32)
    i = np.arange(128)
    c[:, C_ID:C_ID + 128] = np.eye(128)
    c[:, C_TRI:C_TRI + 128] = (i[:, None] <= i[None, :])
    c[:, C_ONES:C_ONES + 128] = 1.0
    c[:, C_NTRI:C_NTRI + 128] = -(i[:, None] >= i[None, :]).astype(np.float32)
    c[:, C_NONES:C_NONES + 128] = -1.0
    q = np.arange(512)
    for d in range(4):
        c[:, C_CM + d * 512:C_CM + (d + 1) * 512] = np.where(q[None, :] >= 128 * d + i[:, None], 0.0, NEGM)
    c[:, C_BM:C_BM + 128] = np.where(i[:, None] >= i[None, :], 0.0, NEGM)
    c[:, C_BM + 128:C_BM + 256] = np.where(i[:, None] <= i[None, :], 0.0, NEGM)
    return c


def rope_table(S):
    inv = (1.0 / (np.float32(10000.0) ** (np.arange(0, HD, 2, dtype=np.float32) / np.float32(HD)))).astype(np.float32)
    ang = (np.arange(S, dtype=np.float32)[:, None] * inv[None, :]).astype(np.float32)
    return np.concatenate([np.cos(ang), np.sin(ang)], axis=1).astype(np.float32)


class LayerBuilder:
    def __init__(self, nc, P, st, S):
        self.nc, self.P, self.st, self.S = nc, P, st, S
        self.NT = S // 128
        self.NQ = S // 512
        assert S % 2048 == 0
        self.ar = Arena(nc, st, P, "arena", 180 * 1024)
        self.cst = Arena(nc, st, P, "cst", 24 * 1024)
        self.banks = []
        for i in range(8):
            t = st.enter_context(nc.psum_tensor(f"bank{i}", [128, 512], F32))
            self.banks.append((t[:, :], P.res(f"bank{i}")))
        self.QA = nc.dram_tensor("QA", [8, 96, S], BF16).ap()
        self.KA = nc.dram_tensor("KA", [8, 96, S], BF16).ap()
        self.V = nc.dram_tensor("Vs", [S, 512], BF16).ap()
        self.GT = nc.dram_tensor("GT", [4, 128, S], BF16).ap()
        self.QA_r = [P.res(f"QA{h}") for h in range(8)]
        self.KA_r = [P.res(f"KA{h}") for h in range(8)]
        self.V_r = P.res("Vd")
        self.GT_r = P.res("GTd")
        self.yT_r = P.res("yTd")
        self.yT_external = True

    def load_consts(self, consts_ap):
        P, cst = self.P, self.cst
        cf, cf_r = cst.alloc(C_TOT, F32, "cf")
        self.cf, self.cf_r = cf, cf_r
        P.dma("sp", cf, consts_ap, writes=[cf_r], tile=cf_r)
        self.idb, self.idb_r = cst.alloc(128, BF16, "idb")
        self.cmb, self.cmb_r = cst.alloc(2048, BF16, "cmb")
        self.bmb, self.bmb_r = cst.alloc(256, BF16, "bmb")
        self.oneb, self.oneb_r = cst.alloc(128, BF16, "oneb")
        P.op("dve", lambda e: e.tensor_copy(out=self.idb, in_=cf[:, C_ID:C_ID + 128]), reads=[cf_r], writes=[self.idb_r])
        P.op("dve", lambda e: e.tensor_copy(out=self.cmb, in_=cf[:, C_CM:C_CM + 2048]), reads=[cf_r], writes=[self.cmb_r])
        P.op("dve", lambda e: e.tensor_copy(out=self.bmb, in_=cf[:, C_BM:C_BM + 256]), reads=[cf_r], writes=[self.bmb_r])
        P.op("dve", lambda e: e.tensor_copy(out=self.oneb, in_=cf[:, C_ONES:C_ONES + 128]), reads=[cf_r], writes=[self.oneb_r])
        self.negF, self.negF_r = cst.alloc(self.NT * 2, F32, "negF")

    def phase1(self, x_ap, wc_ap, ng_ap, gqk_ap, fb_ap, cs_ap):
        P, ar, S, NT = self.P, self.ar, self.S, self.NT
        cf, cf_r = self.cf, self.cf_r
        idb, idb_r = self.idb, self.idb_r
        ar.reset()
        Wb, Wb_r = ar.alloc(NCH * WC, BF16, "Wb")
        Wb3 = Wb.rearrange("p (c n) -> p c n", c=NCH)
        ng, ng_r = ar.alloc(NCH, F32, "ng")
        Gq, Gq_r = ar.alloc(512, F32, "Gq")
        Gk, Gk_r = ar.alloc(512, F32, "Gk")
        nfb, nfb_r = ar.alloc(2, F32, "nfb")
        kmT = [ar.alloc(32, F32, f"kmT{j}") for j in range(2)]
        carry, carry_r = ar.alloc(2, F32, "carry")
        P.dma("sp", ng, ng_ap, writes=[ng_r], tile=ng_r)
        P.dma("sp", Gq, gqk_ap[0].partition_broadcast(128), writes=[Gq_r], tile=Gq_r)
        P.dma("sp", Gk, gqk_ap[1].partition_broadcast(128), writes=[Gk_r], tile=Gk_r)
        P.dma("sp", nfb, fb_ap[0].partition_broadcast(128), writes=[nfb_r], tile=nfb_r)
        P.op("dve", lambda e: e.tensor_scalar_mul(out=Gq, in0=Gq, scalar1=SCALE), reads=[Gq_r], writes=[Gq_r])
        P.op("dve", lambda e: e.tensor_scalar_mul(out=nfb, in0=nfb, scalar1=-1.0), reads=[nfb_r], writes=[nfb_r])
        P.op("dve", lambda e: e.memset(carry, 0.0), writes=[carry_r])
        for j in range(2):
            P.op("dve", lambda e, j=j: e.memset(kmT[j][0], 0.0), writes=[kmT[j][1]])
        wst = [ar.alloc(WC, F32, f"wst{i}") for i in range(2)]
        for c in range(NCH):
            ws, ws_r = wst[c % 2]
            P.dma("sp", ws, wc_ap[c * 128:(c + 1) * 128, :], writes=[ws_r], tile=ws_r)
            eng = "dve" if c % 2 == 0 else "pool"
            P.op(eng, lambda e, ws=ws, c=c: e.tensor_scalar(out=Wb3[:, c, :], in0=ws, scalar1=ng[:, c:c + 1], scalar2=None, op0=ALU.mult),
                 reads=[ws_r, ng_r], acc=[Wb_r])

        xs = [ar.alloc(D_MODEL, F32, f"xs{i}") for i in range(2)]
        xn = [ar.alloc(D_MODEL, BF16, f"xn{i}") for i in range(2)]
        xT = [ar.alloc(D_MODEL, BF16, f"xT{i}") for i in range(2)]
        junk, junk_r = ar.alloc(D_MODEL, BF16, "junk")
        ss = [ar.alloc(1, F32, f"ss{i}") for i in range(2)]
        rr = [ar.alloc(1, F32, f"rr{i}") for i in range(2)]
        cs = [ar.alloc(64, F32, f"cs{i}") for i in range(3)]
        qsb = [ar.alloc(512, F32, f"qsb{i}") for i in range(2)]
        ksb = [ar.alloc(512, F32, f"ksb{i}") for i in range(2)]
        pfs = [ar.alloc(2, F32, f"pfs{i}") for i in range(2)]
        tmps = {w: ar.alloc(512, F32, f"tmp{w}") for w in "qk"}
        sshs = {w: ar.alloc(8, F32, f"ssh{w}") for w in "qk"}
        rtsd = {w: [ar.alloc(128, F32, f"rt{w}{i}") for i in range(4)] for w in "qk"}
        QAtm = [ar.alloc(8 * 96, BF16, f"QAtm{i}") for i in range(2)]
        KAtm = [ar.alloc(8 * 96, BF16, f"KAtm{i}") for i in range(2)]
        QTst = [ar.alloc(8 * 128, BF16, f"QTst{i}") for i in range(2)]
        KTst = [ar.alloc(8 * 128, BF16, f"KTst{i}") for i in range(2)]
        vst = [ar.alloc(512, BF16, f"vst{i}") for i in range(2)]
        gst = [ar.alloc(512, BF16, f"gst{i}") for i in range(2)]
        qTf = [ar.alloc(128, F32, f"qTf{j}") for j in range(2)]
        gm = [ar.alloc(32, F32, f"gm{j}") for j in range(2)]
        top8 = [ar.alloc(8, F32, f"top8{j}") for j in range(2)]
        sel = [ar.alloc(32, F32, f"sel{j}") for j in range(2)]
        ef, ef_r = ar.alloc(2, F32, "ef")
        lf, lf_r = ar.alloc(2, F32, "lf")
        Ft, Ft_r = ar.alloc(2, F32, "Ft")
        r1, r1_r = ar.alloc(2, F32, "r1")
        r2, r2_r = ar.alloc(2, F32, "r2")
        hif, hif_r = ar.alloc(2, F32, "hif")
        hb = [ar.alloc(2, BF16, f"hb{i}") for i in range(3)]

        (pT, pT_r) = self.banks[0][0], self.banks[0][1]
        pTb0 = self.banks[0][0][:, :].bitcast(BF16)
        pTb1 = self.banks[1][0][:, :].bitcast(BF16)
        pT1_r = self.banks[1][1]
        pq, pq_r = self.banks[2]
        pk, pk_r = self.banks[3]
        pv, pv_r = self.banks[4]
        pg, pg_r = self.banks[5]
        pm, pm_r = self.banks[6]
        pm2, pm2_r = self.banks[7]
        import os as _os2
        OLDB = bool(_os2.environ.get("OLD_BANKS"))
        pT7 = pTb0 if OLDB else pm2.bitcast(BF16)
        pT7_r = pT_r if OLDB else pm2_r
        qtb, qtb_r, qtoff = (pm2, pm2_r, 0) if OLDB else (pm, pm_r, 128)
        pf_r = pF_r = pkm_r = pm_r
        pgt_r = [pm_r, pm_r]
        pqT_r = [pm2_r, pm2_r]

        def load_x(t):
            x_t, x_r = xs[t % 2]
            xsr = getattr(self, "x_src_r", None)
            P.dma("sp", x_t, x_ap[t * 128:(t + 1) * 128, :], reads=[xsr] if xsr is not None else (), writes=[x_r], tile=x_r)
            c_t, c_r = cs[t % 3]
            P.dma("sp", c_t, cs_ap[t * 128:(t + 1) * 128, :], writes=[c_r], tile=c_r)

        def post_qk(P, t, which, src_bank, src_r, G, G_r, ATM, TST, DR, DR_r):
            tmp, tmp_r = tmps[which]
            ssh, ssh_r = sshs[which]
            rt = rtsd[which]
            w_t, w_r = (qsb if which == "q" else ksb)[t % 2]
            c_t, c_r = cs[t % 3]
            atm, atm_r = ATM[t % 2]
            atm3 = atm.rearrange("p (h r) -> p h r", h=8)
            tst, tst_r = TST[t % 2]
            b = t // 2
            pf_t, pfs_r = pfs[t % 2]
            P.op("dve", lambda e: e.tensor_tensor(out=tmp, in0=w_t, in1=w_t, op=ALU.mult), reads=[w_r], writes=[tmp_r])
            P.op("dve", lambda e: e.reduce_sum(out=ssh, in_=tmp.rearrange("p (h d) -> p h d", h=8), axis=AX.X), reads=[tmp_r], writes=[ssh_r])
            P.op("act", lambda e: e.activation(out=ssh, in_=ssh, func=AF.Sqrt, bias=EPS, scale=1.0 / HD), reads=[ssh_r], writes=[ssh_r])
            P.op("dve", lambda e: e.reciprocal(out=ssh, in_=ssh), reads=[ssh_r], writes=[ssh_r])
            P.op("dve", lambda e: e.memset(ssh[:, 2:4], 1.0), reads=[ssh_r], writes=[ssh_r])
            w3 = w_t.rearrange("p (h d) -> p h d", h=8)
            P.op("dve", lambda e: e.tensor_tensor(out=w3, in0=w3, in1=ssh.unsqueeze(2).to_broadcast([128, 8, HD]), op=ALU.mult), reads=[w_r, ssh_r], writes=[w_r])
            P.op("dve", lambda e: e.tensor_tensor(out=w_t, in0=w_t, in1=G, op=ALU.mult), reads=[w_r, G_r], writes=[w_r])
            w4 = w_t.rearrange("p (a b d) -> p a b d", a=2, b=4)
            q1 = w4[:, :, 0:2, 0:32]
            q2 = w4[:, :, 0:2, 32:64]
            cosb = c_t[:, 0:32].unsqueeze(1).unsqueeze(1).to_broadcast([128, 2, 2, 32])
            sinb = c_t[:, 32:64].unsqueeze(1).unsqueeze(1).to_broadcast([128, 2, 2, 32])
            rts = [(a.rearrange("p (a b d) -> p a b d", a=2, b=2), r) for a, r in rt]
            P.op("dve", lambda e: e.tensor_tensor(out=rts[0][0], in0=q1, in1=cosb, op=ALU.mult), reads=[w_r, c_r], writes=[rts[0][1]])
            P.op("dve", lambda e: e.tensor_tensor(out=rts[1][0], in0=q2, in1=sinb, op=ALU.mult), reads=[w_r, c_r], writes=[rts[1][1]])
            P.op("pool", lambda e: e.tensor_tensor(out=rts[2][0], in0=q2, in1=cosb, op=ALU.mult), reads=[w_r, c_r], writes=[rts[2][1]])
            P.op("pool", lambda e: e.tensor_tensor(out=rts[3][0], in0=q1, in1=sinb, op=ALU.mult), reads=[w_r, c_r], writes=[rts[3][1]])
            P.op("dve", lambda e: e.tensor_tensor(out=q1, in0=rts[0][0], in1=rts[1][0], op=ALU.subtract), reads=[rts[0][1], rts[1][1]], writes=[w_r])
            P.op("dve", lambda e: e.tensor_tensor(out=q2, in0=rts[2][0], in1=rts[3][0], op=ALU.add), reads=[rts[2][1], rts[3][1], w_r], writes=[w_r])
            P.op("dve", lambda e: e.tensor_copy(out=atm3[:, 0:4, 0:64], in_=w3[:, 0:4, :]), reads=[w_r], acc=[atm_r])
            P.op("pool", lambda e: e.tensor_copy(out=atm3[:, 4:6, 32:96], in_=w3[:, 4:6, :]), reads=[w_r], acc=[atm_r])
            P.op("pool", lambda e: e.tensor_copy(out=atm3[:, 6:8, 3:67], in_=w3[:, 6:8, :]), reads=[w_r], acc=[atm_r])
            if which == "q":
                for j in range(2):
                    h = 4 + j
                    qT_t, qT_r = qTf[j]
                    g_t, g_r = gm[j]
                    t8, t8_r = top8[j]
                    s_t, s_r = sel[j]
                    if b > 0:
                        P.op("pe", lambda e, h=h, j=j: e.transpose(out=qtb[0:64, qtoff + j * 128:qtoff + (j + 1) * 128], in_=w_t[:, h * 64:(h + 1) * 64], identity=cf[:, C_ID:C_ID + 128]),
                             reads=[w_r, cf_r], acc=[qtb_r])
                        P.op("act", lambda e, j=j, qT_t=qT_t: e.copy(out=qT_t[0:64, :], in_=qtb[0:64, qtoff + j * 128:qtoff + (j + 1) * 128]), reads=[qtb_r], writes=[qT_r])
                        P.op("pe", lambda e, j=j, qT_t=qT_t: e.matmul(pm[:, 16 + 32 * j:48 + 32 * j], lhsT=qT_t[0:64, :], rhs=kmT[j][0][0:64, :], start=True, stop=True),
                             reads=[qT_r, kmT[j][1]], acc=[pgt_r[j]])
                        P.op("dve", lambda e, g_t=g_t: e.memset(g_t, -1e30), writes=[g_r])
                        P.op("dve", lambda e, j=j, g_t=g_t: e.tensor_copy(out=g_t[:, 0:b], in_=pm[:, 16 + 32 * j:16 + 32 * j + b]), reads=[pgt_r[j]], writes=[g_r])
                        P.op("dve", lambda e, g_t=g_t, t8=t8: e.max(out=t8, in_=g_t), reads=[g_r], writes=[t8_r])
                        P.op("dve", lambda e, g_t=g_t, t8=t8, s_t=s_t: e.tensor_scalar(out=s_t, in0=g_t, scalar1=t8[:, 2:3], scalar2=-NEGM, op0=ALU.is_ge, op1=ALU.mult),
                             reads=[g_r, t8_r], writes=[s_r])
                        o1 = P.op("dve", lambda e, h=h, s_t=s_t: e.tensor_scalar_add(out=atm3[:, h, 0:32], in0=s_t, scalar1=NEGM), reads=[s_r], acc=[atm_r])
                    else:
                        o1 = P.op("dve", lambda e, h=h: e.memset(atm3[:, h, 0:32], NEGM), acc=[atm_r])
                    P.op("dve", lambda e, h=h: e.memset(atm3[:, h, b:b + 1], 0.0), acc=[atm_r], after=[o1])
                for j in range(2):
                    P.op("act", lambda e, j=j: e.activation(out=ef[:, j:j + 1], in_=pf_t[:, j:j + 1], func=AF.Exp, bias=nfb[:, j:j + 1], scale=-1.0),
                         reads=[pfs_r, nfb_r], acc=[ef_r])
                P.op("act", lambda e: e.activation(out=lf, in_=ef, func=AF.Ln, bias=1.0, scale=1.0), reads=[ef_r], writes=[lf_r])
                P.op("dve", lambda e: e.tensor_scalar_mul(out=lf, in0=lf, scalar1=-1.0), reads=[lf_r], writes=[lf_r])
                P.op("pe", lambda e: e.matmul(pm[:, 8:10], lhsT=cf[:, C_TRI:C_TRI + 128], rhs=lf, start=True, stop=True), reads=[cf_r, lf_r], acc=[pF_r])
                P.op("pe", lambda e: e.matmul(pm[:, 10:12], lhsT=cf[:, C_ONES:C_ONES + 128], rhs=lf, start=True, stop=True), reads=[cf_r, lf_r], acc=[pF_r])
                P.op("dve", lambda e: e.tensor_tensor(out=Ft, in0=pm[:, 8:10], in1=carry, op=ALU.add), reads=[pF_r, carry_r], writes=[Ft_r])
                P.op("dve", lambda e: e.tensor_tensor(out=carry, in0=pm[:, 10:12], in1=carry, op=ALU.add), reads=[pF_r, carry_r], writes=[carry_r])
                P.op("dve", lambda e: e.tensor_scalar_mul(out=self.negF[:, 2 * t:2 * t + 2], in0=Ft, scalar1=-1.0), reads=[Ft_r], acc=[self.negF_r])
                P.op("dve", lambda e: e.tensor_copy(out=hb[0][0], in_=Ft), reads=[Ft_r], writes=[hb[0][1]])
                P.op("dve", lambda e: e.tensor_copy(out=hif, in_=hb[0][0]), reads=[hb[0][1]], writes=[hif_r])
                P.op("dve", lambda e: e.tensor_tensor(out=r1, in0=Ft, in1=hif, op=ALU.subtract), reads=[Ft_r, hif_r], writes=[r1_r])
                P.op("dve", lambda e: e.tensor_copy(out=hb[1][0], in_=r1), reads=[r1_r], writes=[hb[1][1]])
                P.op("dve", lambda e: e.tensor_copy(out=hif, in_=hb[1][0]), reads=[hb[1][1]], writes=[hif_r])
                P.op("dve", lambda e: e.tensor_tensor(out=r2, in0=r1, in1=hif, op=ALU.subtract), reads=[r1_r, hif_r], writes=[r2_r])
                for i3 in range(2):
                    P.op("dve", lambda e, i3=i3: e.tensor_copy(out=atm3[:, 6:8, i3:i3 + 1], in_=hb[i3][0].unsqueeze(2)), reads=[hb[i3][1]], acc=[atm_r])
                P.op("dve", lambda e: e.tensor_copy(out=atm3[:, 6:8, 2:3], in_=r2.unsqueeze(2)), reads=[r2_r], acc=[atm_r])
            else:
                o1 = P.op("pool", lambda e: e.memset(atm3[:, 4:6, 0:32], 0.0), acc=[atm_r])
                P.op("pool", lambda e: e.memset(atm3[:, 4:6, b:b + 1], 1.0), acc=[atm_r], after=[o1])
                P.op("pool", lambda e: e.memset(atm3[:, 6:8, 0:3], 1.0), acc=[atm_r])
                for j in range(2):
                    h = 4 + j
                    P.op("pe", lambda e, h=h, j=j: e.matmul(pm[0:64, 96 + j:97 + j], lhsT=w_t[:, h * 64:(h + 1) * 64], rhs=cf[:, C_ONES:C_ONES + 1],
                                                         start=True, stop=True),
                         reads=[w_r, cf_r], acc=[pkm_r])
                for j in range(2):
                    P.op("dve", lambda e, j=j: e.scalar_tensor_tensor(out=kmT[j][0][0:64, b:b + 1], in0=pm[0:64, 96 + j:97 + j], scalar=1.0 / 256.0, in1=kmT[j][0][0:64, b:b + 1],
                                                                   op0=ALU.mult, op1=ALU.add),
                         reads=[pkm_r, kmT[j][1]], writes=[kmT[j][1]])
            for h in range(8):
                R = ROWS[h // 2]
                dst = (pTb0 if h < 8 else None)
                P.op("pe", lambda e, h=h, R=R: e.transpose(out=pT7[0:R, h * 128:(h + 1) * 128], in_=atm3[:, h, 0:R], identity=idb),
                     reads=[atm_r, idb_r], writes=[pT7_r] if h == 0 else (), acc=() if h == 0 else [pT7_r])
            tst3 = tst.rearrange("p (h s) -> p h s", h=8)
            pT3 = pT7.rearrange("p (h s) -> p h s", h=8)
            P.op("act", lambda e: e.copy(out=tst3[0:64, 0:4, :], in_=pT3[0:64, 0:4, :]), reads=[pT7_r], writes=[tst_r])
            if True:
                P.op("act", lambda e: e.copy(out=tst3[0:96, 4:6, :], in_=pT3[0:96, 4:6, :]), reads=[pT7_r], acc=[tst_r])
            else:
                P.op("dve", lambda e: e.tensor_copy(out=tst3[0:96, 4:6, :], in_=pT3[0:96, 4:6, :]), reads=[pT7_r], acc=[tst_r])
            P.op("act", lambda e: e.copy(out=tst3[0:67, 6:8, :], in_=pT3[0:67, 6:8, :]), reads=[pT7_r], acc=[tst_r])
            sl = slice(t * 128, (t + 1) * 128)
            P.dma("pool", DR[0:4, 0:64, sl].rearrange("h r s -> r h s"), tst3[0:64, 0:4, :], reads=[tst_r], acc=DR_r[0:4], tile=tst_r)
            P.dma("pool", DR[4:6, 0:96, sl].rearrange("h r s -> r h s"), tst3[0:96, 4:6, :], reads=[tst_r], acc=DR_r[4:6], tile=tst_r)
            P.dma("pool", DR[6:8, 0:67, sl].rearrange("h r s -> r h s"), tst3[0:67, 6:8, :], reads=[tst_r], acc=DR_r[6:8], tile=tst_r)

        def main(t):
            x_t, x_r = xs[t % 2]
            n_t, n_r = xn[t % 2]
            T_t, T_r = xT[t % 2]
            s_t, s_r = ss[t % 2]
            r_t, r_r = rr[t % 2]
            P.op("act", lambda e: e.activation(out=junk, in_=x_t, func=AF.Square, accum_out=s_t), reads=[x_r], writes=[junk_r, s_r])
            P.op("act", lambda e: e.activation(out=r_t, in_=s_t, func=AF.Sqrt, bias=EPS, scale=1.0 / D_MODEL), reads=[s_r], writes=[r_r])
            P.op("dve", lambda e: e.reciprocal(out=r_t, in_=r_t), reads=[r_r], writes=[r_r])
            P.op("dve", lambda e: e.tensor_scalar(out=n_t, in0=x_t, scalar1=r_t, scalar2=None, op0=ALU.mult), reads=[x_r, r_r], writes=[n_r])
            for c in range(NCH):
                dst = pTb0 if c < 8 else pTb1
                dr = pT_r if c < 8 else pT1_r
                cc = c % 8
                P.op("pe", lambda e, dst=dst, cc=cc, c=c: e.transpose(out=dst[:, cc * 128:(cc + 1) * 128], in_=n_t[:, c * 128:(c + 1) * 128], identity=idb),
                     reads=[n_r, idb_r], writes=[dr] if cc == 0 else (), acc=() if cc == 0 else [dr])
            P.op("act", lambda e: e.copy(out=T_t[:, 0:1024], in_=pTb0), reads=[pT_r], writes=[T_r])
            P.op("dve", lambda e: e.tensor_copy(out=T_t[:, 1024:2048], in_=pTb1), reads=[pT1_r], acc=[T_r])
            for (bank, b_r, c0) in ((pk, pk_r, 512), (pq, pq_r, 0), (pv, pv_r, 1024)):
                for c in range(NCH):
                    P.op("pe", lambda e, bank=bank, c=c, c0=c0: e.matmul(bank, lhsT=T_t[:, c * 128:(c + 1) * 128], rhs=Wb3[:, c, c0:c0 + 512], start=(c == 0), stop=(c == NCH - 1)),
                         reads=[T_r, Wb_r], writes=[b_r] if c == 0 else (), acc=() if c == 0 else [b_r])
            for m in range(4):
                for c in range(NCH):
                    first = (m == 0 and c == 0)
                    P.op("pe", lambda e, m=m, c=c: e.matmul(pg[:, m * 128:(m + 1) * 128], lhsT=Wb3[:, c, 1536 + m * 128:1536 + (m + 1) * 128], rhs=T_t[:, c * 128:(c + 1) * 128], start=(c == 0), stop=(c == NCH - 1)),
                         reads=[T_r, Wb_r], writes=[pg_r] if first else (), acc=() if first else [pg_r])
            for c in range(NCH):
                P.op("pe", lambda e, c=c: e.matmul(pm[:, 0:2], lhsT=T_t[:, c * 128:(c + 1) * 128], rhs=Wb3[:, c, 2048:2050], start=(c == 0), stop=(c == NCH - 1)),
                     reads=[T_r, Wb_r], writes=[pf_r] if c == 0 else (), acc=() if c == 0 else [pf_r])

        def post_a(t):
            v_t, v_r = vst[t % 2]
            g_t, g_r = gst[t % 2]
            k_t, k_r = ksb[t % 2]
            q_t, q_r = qsb[t % 2]
            pf_t, pfs_r = pfs[t % 2]
            P.op("act", lambda e: e.copy(out=k_t, in_=pk), reads=[pk_r], writes=[k_r])
            if True:
                P.op("act", lambda e: e.copy(out=q_t, in_=pq), reads=[pq_r], writes=[q_r])
            else:
                P.op("dve", lambda e: e.tensor_copy(out=q_t, in_=pq), reads=[pq_r], writes=[q_r])
            P.op("act", lambda e: e.copy(out=v_t, in_=pv), reads=[pv_r], writes=[v_r])
            P.dma("pool", self.V[t * 128:(t + 1) * 128, :], v_t, reads=[v_r], acc=[self.V_r], tile=v_r)
            if True:
                P.op("act", lambda e: e.copy(out=pf_t, in_=pm[:, 0:2]), reads=[pf_r], writes=[pfs_r])
            else:
                P.op("dve", lambda e: e.tensor_copy(out=pf_t, in_=pm[:, 0:2]), reads=[pf_r], writes=[pfs_r])
            P.op("act", lambda e: e.activation(out=g_t, in_=pg, func=AF.Silu), reads=[pg_r], writes=[g_r])
            P.dma("pool", self.GT[:, :, t * 128:(t + 1) * 128].rearrange("m c s -> c m s"), g_t.rearrange("p (m s) -> p m s", m=4), reads=[g_r], acc=[self.GT_r], tile=g_r)

        def post_b(t):
            dk, dq = Deferred(), Deferred()
            post_qk(dk, t, "k", pk, pk_r, Gk, Gk_r, KAtm, KTst, self.KA, self.KA_r)
            post_qk(dq, t, "q", pq, pq_r, Gq, Gq_r, QAtm, QTst, self.QA, self.QA_r)
            replay_interleaved(P, [dk, dq])

        import os as _os
        if _os.environ.get("PH1_SEQ"):
            load_x(0)
            if NT > 1:
                load_x(1)
            for t in range(NT):
                if t + 2 < NT:
                    load_x(t + 2)
                main(t)
                post_a(t)
                post_b(t)
        else:
            load_x(0)
            if NT > 1:
                load_x(1)
            main(0)
            post_a(0)
            for t in range(NT):
                if t + 2 < NT:
                    load_x(t + 2)
                if t + 1 < NT:
                    main(t + 1)
                    post_a(t + 1)
                post_b(t)
        P.barrier()

    def load_qkv(self, m, need_q=True):
        P, ar, S, NT = self.P, self.ar, self.S, self.NT
        R = ROWS[m]
        QT, KT = [], []
        for j in range(2):
            h = 2 * m + j
            q_t, q_r = ar.alloc(S, BF16, f"QT{m}{j}")
            k_t, k_r = ar.alloc(S, BF16, f"KT{m}{j}")
            nsp = 4
            for i in range(nsp):
                sl = slice(i * S // nsp, (i + 1) * S // nsp)
                P.dma("sp", q_t[0:R, sl], self.QA[h, 0:R, sl], reads=[self.QA_r[h]], acc=[q_r], tile=q_r)
                P.dma("sp", k_t[0:R, sl], self.KA[h, 0:R, sl], reads=[self.KA_r[h]], acc=[k_r], tile=k_r)
            QT.append((q_t, q_r))
            KT.append((k_t, k_r))
        v_t, v_r = ar.alloc(NT * 128, BF16, f"V{m}")
        v3 = v_t.rearrange("p (n c) -> p n c", n=NT)
        nsp = 4
        for i in range(nsp):
            n0, n1 = i * NT // nsp, (i + 1) * NT // nsp
            P.dma("sp", v3[:, n0:n1, :], self.V[n0 * 128:n1 * 128, m * 128:(m + 1) * 128].rearrange("(n p) c -> p n c", p=128),
                  reads=[self.V_r], acc=[v_r], tile=v_r)
        return QT, KT, (v3, v_r)

    def finish_qtile(self, m, qt, num_ap, num_r, den_ap, den_r, yT_ap, bufs, has_den=True):
        P = self.P
        i = bufs["i"]
        bufs["i"] += 1
        gt_t, gt_r = bufs["gt"][i % 2]
        y32, y32_r = bufs["y32"][i % 2]
        yb, yb_r = bufs["yb"][i % 2]
        sl = slice(qt * 512, (qt + 1) * 512)
        P.dma("sp", gt_t, self.GT[m, :, sl], reads=[self.GT_r], writes=[gt_r], tile=gt_r)
        if has_den:
            P.op("dve", lambda e: e.reciprocal(out=y32, in_=den_ap), reads=[den_r], writes=[y32_r])
            P.op("dve", lambda e: e.tensor_tensor(out=y32, in0=num_ap, in1=y32, op=ALU.mult), reads=[num_r, y32_r], writes=[y32_r])
            P.op("pool", lambda e: e.tensor_tensor(out=yb, in0=y32, in1=gt_t, op=ALU.mult), reads=[y32_r, gt_r], writes=[yb_r])
        else:
            P.op("dve", lambda e: e.tensor_tensor(out=yb, in0=num_ap, in1=gt_t, op=ALU.mult), reads=[num_r, gt_r], writes=[yb_r])
        dst_ap = self.y_dst(m, qt) if getattr(self, "y_dst", None) is not None else yT_ap[m * 128:(m + 1) * 128, sl]
        o = P.dma("pool", dst_ap, yb, reads=[yb_r], acc=[self.yT_r], tile=yb_r)
        if self.yT_external:
            P.finish(o)

    def out_bufs(self):
        ar = self.ar
        return {"i": 0,
                "gt": [ar.alloc(512, BF16, f"gt{i}") for i in range(2)],
                "y32": [ar.alloc(512, F32, f"y32{i}") for i in range(2)],
                "yb": [ar.alloc(512, BF16, f"yb{i}") for i in range(2)]}

    def load_vaug(self, m):
        P, ar, S, NT = self.P, self.ar, self.S, self.NT
        va, va_r = ar.alloc(NT * 256, BF16, f"VA{m}")
        va4 = va.rearrange("p (n j c) -> p n j c", n=NT, j=2)
        o1 = P.op("pool", lambda e: e.memset(va4[:, :, :, 64:128], 1.0), writes=[va_r])
        nsp = 4
        for i in range(nsp):
            n0, n1 = i * NT // nsp, (i + 1) * NT // nsp
            for j in range(2):
                P.dma("sp", va4[:, n0:n1, j, 0:64], self.V[n0 * 128:n1 * 128, m * 128 + j * 64:m * 128 + (j + 1) * 64].rearrange("(n p) c -> p n c", p=128),
                      reads=[self.V_r], acc=[va_r], tile=va_r)
        return va4, va_r

    def load_qk(self, m):
        P, ar, S = self.P, self.ar, self.S
        R = ROWS[m]
        QT, KT = [], []
        for j in range(2):
            h = 2 * m + j
            q_t, q_r = ar.alloc(S, BF16, f"QT{m}{j}")
            k_t, k_r = ar.alloc(S, BF16, f"KT{m}{j}")
            nsp = 4
            for i in range(nsp):
                sl = slice(i * S // nsp, (i + 1) * S // nsp)
                P.dma("sp", k_t[0:R, sl], self.KA[h, 0:R, sl], reads=[self.KA_r[h]], acc=[k_r], tile=k_r)
                P.dma("sp", q_t[0:R, sl], self.QA[h, 0:R, sl], reads=[self.QA_r[h]], acc=[q_r], tile=q_r)
            QT.append((q_t, q_r))
            KT.append((k_t, k_r))
        return QT, KT

    def phase_dense(self, m, yT_ap):
        P, ar, S, NT, NQ = self.P, self.ar, self.S, self.NT, self.NQ
        ar.reset()
        R = ROWS[m]
        QT, KT = self.load_qk(m)
        va4, va_r = self.load_vaug(m)
        NA = 4
        AT = [ar.alloc(512, BF16, f"AT{i}") for i in range(NA)]
        gts = [ar.alloc(512, BF16, f"gt{i}") for i in range(2)]
        ybs = [ar.alloc(512, BF16, f"yb{i}") for i in range(2)]
        y32s = [ar.alloc(512, F32, f"y32{i}") for i in range(2)]
        rdn = [ar.alloc(512, F32, f"rdn{i}") for i in range(2)]
        sb = [self.banks[i] for i in range(3)]
        pod = [[self.banks[3], self.banks[4]], [self.banks[5], self.banks[6]]]
        seq = []
        for qt in range(NQ):
            nkb = 4 * qt + 4
            for j in range(2):
                for kb in range(nkb):
                    seq.append((qt, j, kb, nkb))
        LOOK = 2
        n = len(seq)

        def stage_S(i):
            qt, j, kb, nkb = seq[i]
            s_t, s_r = sb[i % 3]
            a_t, a_r = AT[i % NA]
            q_t, q_r = QT[j]
            k_t, k_r = KT[j]
            d = kb - 4 * qt
            P.op("pe", lambda e: e.matmul(s_t, lhsT=k_t[0:R, kb * 128:(kb + 1) * 128], rhs=q_t[0:R, qt * 512:(qt + 1) * 512], start=True, stop=(d < 0)),
                 reads=[k_r, q_r], writes=[s_r])
            if d >= 0:
                P.op("pe", lambda e: e.matmul(s_t, lhsT=self.idb, rhs=self.cmb[:, d * 512:(d + 1) * 512], start=False, stop=True),
                     reads=[self.idb_r, self.cmb_r], acc=[s_r])
            if m == 3:
                P.op("act", lambda e: e.activation(out=a_t, in_=s_t, func=AF.Exp, bias=self.negF[:, 2 * kb + j:2 * kb + j + 1], scale=1.0),
                     reads=[s_r, self.negF_r], writes=[a_r])
            else:
                P.op("act", lambda e: e.activation(out=a_t, in_=s_t, func=AF.Exp), reads=[s_r], writes=[a_r])

        def stage_PV(i):
            qt, j, kb, nkb = seq[i]
            a_t, a_r = AT[i % NA]
            pb, pb_r = pod[qt % 2][j]
            P.op("pe", lambda e: e.matmul(pb, lhsT=va4[:, kb, j, :], rhs=a_t, start=(kb == 0), stop=(kb == nkb - 1)),
                 reads=[a_r, va_r], writes=[pb_r] if kb == 0 else (), acc=() if kb == 0 else [pb_r])
            if kb == nkb - 1:
                par = qt % 2
                gt_t, gt_r = gts[par]
                yb, yb_r = ybs[par]
                y32, y32_r = y32s[par]
                rd, rd_r = rdn[j]
                sl = slice(qt * 512, (qt + 1) * 512)
                if j == 0:
                    P.dma("sp", gt_t, self.GT[m, :, sl], reads=[self.GT_r], writes=[gt_r], tile=gt_r)
                P.op("dve", lambda e: e.reciprocal(out=rd[0:64, :], in_=pb[64:128, :]), reads=[pb_r], writes=[rd_r])
                P.op("dve", lambda e: e.tensor_tensor(out=y32[64 * j:64 * j + 64, :], in0=pb[0:64, :], in1=rd[0:64, :], op=ALU.mult),
                     reads=[pb_r, rd_r], writes=[y32_r] if j == 0 else (), acc=() if j == 0 else [y32_r])
                P.op("pool", lambda e: e.tensor_tensor(out=yb[64 * j:64 * j + 64, :], in0=y32[64 * j:64 * j + 64, :], in1=gt_t[64 * j:64 * j + 64, :], op=ALU.mult),
                     reads=[y32_r, gt_r], writes=[yb_r] if j == 0 else (), acc=() if j == 0 else [yb_r])
                if j == 1:
                    dst_ap = self.y_dst(m, qt) if getattr(self, "y_dst", None) is not None else yT_ap[m * 128:(m + 1) * 128, sl]
                    o = P.dma("pool", dst_ap, yb, reads=[yb_r], acc=[self.yT_r], tile=yb_r)
                    if self.yT_external:
                        P.finish(o)

        for i in range(n + LOOK):
            if i < n:
                stage_S(i)
            if i - LOOK >= 0:
                stage_PV(i - LOOK)
        P.barrier()

    def phase_B(self, yT_ap):
        P, ar, S, NT, NQ = self.P, self.ar, self.S, self.NT, self.NQ
        m = 1
        ar.reset()
        QT, KT = self.load_qk(m)
        v_t, v_r = ar.alloc(NT * 128, BF16, f"V{m}")
        v3 = v_t.rearrange("p (n c) -> p n c", n=NT)
        for i in range(4):
            n0, n1 = i * NT // 4, (i + 1) * NT // 4
            P.dma("sp", v3[:, n0:n1, :], self.V[n0 * 128:n1 * 128, m * 128:(m + 1) * 128].rearrange("(n p) c -> p n c", p=128),
                  reads=[self.V_r], acc=[v_r], tile=v_r)
        NB = 4
        AT = [ar.alloc(512, BF16, f"AT{i}") for i in range(NB)]
        E = [ar.alloc(512, F32, f"E{i}") for i in range(NB)]
        SP = [ar.alloc(512, F32, f"SP{i}") for i in range(NB)]
        SS = [[ar.alloc(512, F32, f"SS{j}{i}") for i in range(2)] for j in range(2)]
        gts = [ar.alloc(512, BF16, f"gt{i}") for i in range(2)]
        ybs = [ar.alloc(512, BF16, f"yb{i}") for i in range(2)]
        sb = [self.banks[i] for i in range(NB)]
        pos = [self.banks[4], self.banks[5]]
        cf, cf_r = self.cf, self.cf_r
        seq = []
        for qt in range(NQ):
            nkb = 4 * qt + 4
            for idx, kb in enumerate(range(nkb - 1, -1, -1)):
                for j in range(2):
                    seq.append((qt, j, kb, idx, nkb))
        n = len(seq)

        def st1(i):
            qt, j, kb, idx, nkb = seq[i]
            s_t, s_r = sb[i % NB]
            e_t, e_r = E[i % NB]
            p_t, p_r = SP[i % NB]
            q_t, q_r = QT[j]
            k_t, k_r = KT[j]
            d = kb - 4 * qt
            P.op("pe", lambda e: e.matmul(s_t, lhsT=k_t[0:64, kb * 128:(kb + 1) * 128], rhs=q_t[0:64, qt * 512:(qt + 1) * 512], start=True, stop=(d < 0)),
                 reads=[k_r, q_r], writes=[s_r])
            if d >= 0:
                P.op("pe", lambda e: e.matmul(s_t, lhsT=self.idb, rhs=self.cmbs[:, d * 512:(d + 1) * 512], start=False, stop=True),
                     reads=[self.idb_r, self.cmbs_r], acc=[s_r])
            P.op("act", lambda e: e.activation(out=e_t, in_=s_t, func=AF.Exp), reads=[s_r], writes=[e_r])
            P.op("act", lambda e: e.activation(out=p_t, in_=e_t, func=AF.Ln, bias=1.0, scale=1.0), reads=[e_r], writes=[p_r])

        def st2(i):
            qt, j, kb, idx, nkb = seq[i]
            s_t, s_r = sb[i % NB]
            p_t, p_r = SP[i % NB]
            a_t, a_r = AT[i % NB]
            so_t, so_r = SS[j][(idx + 1) % 2]
            sn_t, sn_r = SS[j][idx % 2]
            P.op("pe", lambda e: e.matmul(s_t, lhsT=cf[:, C_NTRI:C_NTRI + 128], rhs=p_t, start=False, stop=(idx == 0), skip_group_check=True),
                 reads=[cf_r, p_r], acc=[s_r])
            if idx > 0:
                P.op("pe", lambda e: e.matmul(s_t, lhsT=cf[:, C_NONES:C_NONES + 128], rhs=so_t, start=False, stop=True, skip_group_check=True),
                     reads=[cf_r, so_r], acc=[s_r])
            if kb > 0:
                if idx == 0:
                    P.op("pool", lambda e: e.tensor_copy(out=sn_t, in_=p_t), reads=[p_r], writes=[sn_r])
                else:
                    P.op("pool", lambda e: e.tensor_tensor(out=sn_t, in0=so_t, in1=p_t, op=ALU.add), reads=[so_r, p_r], writes=[sn_r])
            P.op("act", lambda e: e.activation(out=a_t, in_=s_t, func=AF.Exp), reads=[s_r], writes=[a_r])

        def st3(i):
            qt, j, kb, idx, nkb = seq[i]
            a_t, a_r = AT[i % NB]
            pO, pO_r = pos[qt % 2]
            first = (idx == 0 and j == 0)
            P.op("pe", lambda e: e.matmul(pO[64 * j:64 * j + 64, :], lhsT=v3[:, kb, 64 * j:64 * j + 64], rhs=a_t, start=(idx == 0), stop=(idx == nkb - 1)),
                 reads=[a_r, v_r], writes=[pO_r] if first else (), acc=() if first else [pO_r])
            if idx == nkb - 1 and j == 1:
                par = qt % 2
                gt_t, gt_r = gts[par]
                yb, yb_r = ybs[par]
                sl = slice(qt * 512, (qt + 1) * 512)
                P.dma("sp", gt_t, self.GT[m, :, sl], reads=[self.GT_r], writes=[gt_r], tile=gt_r)
                P.op("dve", lambda e: e.tensor_tensor(out=yb, in0=pO, in1=gt_t, op=ALU.mult), reads=[pO_r, gt_r], writes=[yb_r])
                dst_ap = self.y_dst(m, qt) if getattr(self, "y_dst", None) is not None else yT_ap[m * 128:(m + 1) * 128, sl]
                o = P.dma("pool", dst_ap, yb, reads=[yb_r], acc=[self.yT_r], tile=yb_r)
                if self.yT_external:
                    P.finish(o)

        for i in range(n + 2):
            if i < n:
                st1(i)
            if 0 <= i - 1 < n:
                st2(i - 1)
            if 0 <= i - 2 < n:
                st3(i - 2)
        P.barrier()

    def make_strict_mask(self):
        P, cst = self.P, self.cst
        self.cmbs, self.cmbs_r = cst.alloc(2048, BF16, "cmbs")
        cm3 = self.cmb.rearrange("p (d q) -> p d q", d=4)
        cs3 = self.cmbs.rearrange("p (d q) -> p d q", d=4)
        P.op("dve", lambda e: e.memset(self.cmbs, NEGM), writes=[self.cmbs_r])
        P.op("dve", lambda e: e.tensor_copy(out=cs3[:, :, 1:512], in_=cm3[:, :, 0:511]), reads=[self.cmb_r], writes=[self.cmbs_r])

    def phase_A(self, yT_ap):
        P, ar, S, NT, NQ = self.P, self.ar, self.S, self.NT, self.NQ
        m = 0
        ar.reset()
        QT, KT, (v3, v_r) = self.load_qkv(m)
        bufs = self.out_bufs()
        num, num_r = ar.alloc(S, F32, "numA")
        den, den_r = ar.alloc(S, F32, "denA")
        v_t2 = v3
        AT = [ar.alloc(256, BF16, f"ATa{i}") for i in range(3)]
        sb = [self.banks[i] for i in range(3)]
        pOs = [self.banks[3], self.banks[4]]
        pDs = [self.banks[5], self.banks[6]]
        it = 0
        grp = 0
        for dil in (1, 4, 16):
            L = S // dil
            nb = L // 128
            vt3, vt_r = v3, v_r
            if dil > 1:
                src = self.V[:, m * 128:(m + 1) * 128].rearrange("(n i r) c -> r i n c", i=128, r=dil)
                for r in range(dil):
                    P.dma("sp", v3[:, r * nb:(r + 1) * nb, :], src[r], reads=[self.V_r],
                          writes=[v_r] if r == 0 else (), acc=() if r == 0 else [v_r], tile=v_r)
            for r in range(dil):
                gs = min(4, nb)
                for n0 in range(0, nb, gs):
                    pO, pO_r = pOs[grp % 2]
                    pD, pD_r = pDs[grp % 2]
                    grp += 1
                    for j in range(2):
                        q_t, q_r = QT[j]
                        k_t, k_r = KT[j]
                        for n in range(n0, n0 + gs):
                            s_t, s_r = sb[it % 3]
                            a_t, a_r = AT[it % 3]
                            it += 1
                            def tok(bi, dil=dil, r=r):
                                return slice(r + dil * 128 * bi, r + dil * 128 * bi + dil * 127 + 1, dil)
                            c0 = 0 if n > 0 else 128
                            if n > 0:
                                P.op("pe", lambda e, s_t=s_t, k_t=k_t, q_t=q_t, n=n, tok=tok: e.matmul(s_t[:, 0:128], lhsT=k_t[0:64, tok(n - 1)], rhs=q_t[0:64, tok(n)], start=True, stop=False),
                                     reads=[k_r, q_r], writes=[s_r])
                            P.op("pe", lambda e, s_t=s_t, k_t=k_t, q_t=q_t, n=n, tok=tok: e.matmul(s_t[:, 128:256], lhsT=k_t[0:64, tok(n)], rhs=q_t[0:64, tok(n)], start=(n == 0), stop=False),
                                 reads=[k_r, q_r], writes=[s_r] if n == 0 else (), acc=() if n == 0 else [s_r])
                            P.op("pe", lambda e, s_t=s_t, c0=c0: e.matmul(s_t[:, c0:256], lhsT=self.idb, rhs=self.bmb[:, c0:256], start=False, stop=True),
                                 reads=[self.idb_r, self.bmb_r], acc=[s_r])
                            P.op("act", lambda e, s_t=s_t, a_t=a_t, c0=c0: e.activation(out=a_t[:, c0:256], in_=s_t[:, c0:256], func=AF.Exp), reads=[s_r], writes=[a_r])
                            cs_ = (n - n0) * 128
                            first = (j == 0 and n == n0)
                            kbs = ([(n - 1, 0)] if n > 0 else []) + [(n, 128)]
                            for ii, (kbi, ac) in enumerate(kbs):
                                P.op("pe", lambda e, a_t=a_t, kbi=kbi, ac=ac, j=j, cs_=cs_, ii=ii, nk=len(kbs), r=r, nb=nb, pO=pO, vt3=vt3: e.matmul(
                                        pO[64 * j:64 * j + 64, cs_:cs_ + 128], lhsT=vt3[:, r * nb + kbi, 64 * j:64 * j + 64], rhs=a_t[:, ac:ac + 128], start=(ii == 0), stop=(ii == nk - 1)),
                                     reads=[a_r, vt_r], writes=[pO_r] if (first and ii == 0) else (), acc=() if (first and ii == 0) else [pO_r])
                                P.op("pe", lambda e, a_t=a_t, ac=ac, j=j, cs_=cs_, ii=ii, nk=len(kbs), pD=pD: e.matmul(
                                        pD[64 * j:64 * j + 64, cs_:cs_ + 128], lhsT=self.oneb[:, 0:64], rhs=a_t[:, ac:ac + 128], start=(ii == 0), stop=(ii == nk - 1)),
                                     reads=[a_r, self.oneb_r], writes=[pD_r] if (first and ii == 0) else (), acc=() if (first and ii == 0) else [pD_r])
                    t0 = r + dil * 128 * n0
                    tsl = slice(t0, t0 + dil * (gs * 128 - 1) + 1, dil)
                    gw = gs * 128
                    if dil == 1:
                        P.op("dve", lambda e, pO=pO, tsl=tsl, gw=gw: e.tensor_copy(out=num[:, tsl], in_=pO[:, 0:gw]), reads=[pO_r], acc=[num_r])
                        P.op("act", lambda e, pD=pD, tsl=tsl, gw=gw: e.copy(out=den[:, tsl], in_=pD[:, 0:gw]), reads=[pD_r], acc=[den_r])
                    else:
                        P.op("dve", lambda e, pO=pO, tsl=tsl, gw=gw: e.tensor_tensor(out=num[:, tsl], in0=num[:, tsl], in1=pO[:, 0:gw], op=ALU.add), reads=[pO_r, num_r], acc=[num_r])
                        P.op("dve", lambda e, pD=pD, tsl=tsl, gw=gw: e.tensor_tensor(out=den[:, tsl], in0=den[:, tsl], in1=pD[:, 0:gw], op=ALU.add), reads=[pD_r, den_r], acc=[den_r])
        for qt in range(NQ):
            sl = slice(qt * 512, (qt + 1) * 512)
            self.finish_qtile(m, qt, num[:, sl], num_r, den[:, sl], den_r, yT_ap, bufs)
        P.barrier()


def _phase_O(lb, YG_ap, YG_r, wo_ap, x_ap, x_r, out_ap, out_r, tok0, ntok, out_row0, external_out):
    P, ar = lb.P, lb.ar
    ar.reset()
    Wb, Wb_r = ar.alloc(NCH * D_MODEL, BF16, "Wob")
    Wb3 = Wb.rearrange("p (c n) -> p c n", c=NCH)
    wst = [ar.alloc(D_MODEL, F32, f"wost{i}") for i in range(2)]
    for q in range(NCH):
        r, m = q // 4, q % 4
        wrow = (m * 4 + r) * 128
        ws, ws_r = wst[q % 2]
        P.dma("sp", ws, wo_ap[wrow:wrow + 128, :], writes=[ws_r], tile=ws_r)
        eng = "dve" if q % 2 == 0 else "pool"
        P.op(eng, lambda e, ws=ws, q=q: e.tensor_copy(out=Wb3[:, q, :], in_=ws), reads=[ws_r], acc=[Wb_r])
    yt = [ar.alloc(NCH * 128, BF16, f"oyt{i}") for i in range(2)]
    xs = [ar.alloc(D_MODEL, F32, f"oxs{i}") for i in range(2)]
    xos = [ar.alloc(D_MODEL, F32, f"oxo{i}") for i in range(2)]
    bi = 0
    for ti in range(ntok // 128):
        t0 = tok0 + ti * 128
        y_t, y_r = yt[ti % 2]
        x_t, xr_ = xs[ti % 2]
        o_t, o_r = xos[ti % 2]
        y3 = y_t.rearrange("p (c s) -> p c s", c=NCH)
        P.dma("sp", y3, YG_ap(t0).rearrange("(c p) s -> p c s", p=128), reads=[YG_r], writes=[y_r], tile=y_r)
        P.dma("sp", x_t, x_ap[t0:t0 + 128, :], reads=[x_r] if x_r is not None else (), writes=[xr_], tile=xr_)
        for cg in range(4):
            bk, bk_r = lb.banks[bi % 8]
            bi += 1
            for c in range(NCH):
                P.op("pe", lambda e, bk=bk, c=c, cg=cg, y3=y3: e.matmul(bk, lhsT=y3[:, c, :], rhs=Wb3[:, c, cg * 512:(cg + 1) * 512], start=(c == 0), stop=(c == NCH - 1)),
                     reads=[y_r, Wb_r], writes=[bk_r] if c == 0 else (), acc=() if c == 0 else [bk_r])
            P.op("dve", lambda e, bk=bk, cg=cg, o_t=o_t, x_t=x_t: e.tensor_tensor(out=o_t[:, cg * 512:(cg + 1) * 512], in0=x_t[:, cg * 512:(cg + 1) * 512], in1=bk, op=ALU.add),
                 reads=[bk_r, xr_], acc=[o_r])
        orow = out_row0 + ti * 128
        o = P.dma("pool", out_ap[orow:orow + 128, :], o_t, reads=[o_r], acc=[out_r] if out_r is not None else (), tile=o_r)
        if external_out:
            P.finish(o)
    P.barrier()


def build_fused_program(S, depth, groups=((0, 1, 2, 3), (4, 5, 6, 7))):
    nc = bass.Bass("TRN2", target_bir_lowering=False)
    x = nc.dram_tensor("x", [S, D_MODEL], F32, kind="ExternalInput").ap()
    wc = nc.dram_tensor("wc", [depth, D_MODEL, WC], F32, kind="ExternalInput").ap()
    ng = nc.dram_tensor("ng", [depth, 128, NCH], F32, kind="ExternalInput").ap()
    gqk = nc.dram_tensor("gqk", [depth, 2, 512], F32, kind="ExternalInput").ap()
    fb = nc.dram_tensor("fb", [depth, 1, 2], F32, kind="ExternalInput").ap()
    cs = nc.dram_tensor("cs", [S, 64], F32, kind="ExternalInput").ap()
    consts = nc.dram_tensor("consts", [128, C_TOT], F32, kind="ExternalInput").ap()
    wo = nc.dram_tensor("wo", [depth, D_MODEL, D_MODEL], F32, kind="ExternalInput").ap()
    xo = nc.dram_tensor("xo", [S, D_MODEL], F32, kind="ExternalOutput").ap()
    Xb = [nc.dram_tensor(f"Xbuf{i}", [S, D_MODEL], F32).ap() for i in range(2)]
    PART = 1024
    NP = S // PART
    YL = [[nc.dram_tensor(f"YL{i}_{p}", [512, PART], BF16).ap() for p in range(NP)] for i in range(2)]
    YG = [[nc.dram_tensor(f"YG{i}_{p}", [4 * 512, PART], BF16).ap() for p in range(NP)] for i in range(2)]
    P = Prog(nc)
    st = contextlib.ExitStack()
    with st:
        lb = LayerBuilder(nc, P, st, S)
        lb.yT_external = False
        lb.load_consts(consts)
        lb.make_strict_mask()
        Xb_r = [P.res("Xb0"), P.res("Xb1")]
        YL_r = [P.res("YL0"), P.res("YL1")]
        YG_r = [P.res("YG0"), P.res("YG1")]
        cc_r = P.res("cc")
        xo_r = P.res("xo")
        grp = [list(g_) for g_ in groups]
        import os as _os
        for l in range(depth):
            src, src_r = (x, None) if l == 0 else (Xb[(l - 1) % 2], Xb_r[(l - 1) % 2])
            yl, yl_r = YL[l % 2], YL_r[l % 2]
            yg, yg_r = YG[l % 2], YG_r[l % 2]
            lb.yT_r = yl_r
            lb.x_src_r = src_r
            lb.y_dst = lambda m, qt, yl=yl: yl[(qt * 512) // PART][m * 128:(m + 1) * 128, (qt * 512) % PART:(qt * 512) % PART + 512]
            lb.phase1(src, wc[l], ng[l], gqk[l], fb[l], cs)
            lb.phase_A(None)
            lb.phase_B(None)
            lb.phase_dense(2, None)
            lb.phase_dense(3, None)
            if not _os.environ.get("FUSED_NOCC"):
                for p in range(NP):
                    P.collective(lambda e, a=yl[p], b_=yg[p]: e.collective_compute("AllGather", ALU.bypass, replica_groups=grp, ins=[a], outs=[b_]),
                                 [yl_r], [yg_r] if p == 0 else [], cc_r)
                    if p > 0:
                        yg_r.w.append(P.ops["pool"][-1])
            last = (l == depth - 1)
            yg_fn = lambda t0, yg=yg: yg[t0 // PART][:, t0 % PART:t0 % PART + 128]
            if last:
                _phase_O(lb, yg_fn, yg_r, wo[l], src, src_r, xo, xo_r, 0, S, 0, True)
            else:
                _phase_O(lb, yg_fn, yg_r, wo[l], src, src_r, Xb[l % 2], Xb_r[l % 2], 0, S, 0, False)
        P.emit()
    return nc, P


def build_layer_program(S, phases="1ABCD"):
    nc = bass.Bass("TRN2", target_bir_lowering=False)
    x = nc.dram_tensor("x", [S, D_MODEL], F32, kind="ExternalInput").ap()
    wc = nc.dram_tensor("wc", [D_MODEL, WC], F32, kind="ExternalInput").ap()
    ng = nc.dram_tensor("ng", [128, NCH], F32, kind="ExternalInput").ap()
    gqk = nc.dram_tensor("gqk", [2, 512], F32, kind="ExternalInput").ap()
    fb = nc.dram_tensor("fb", [1, 2], F32, kind="ExternalInput").ap()
    cs = nc.dram_tensor("cs", [S, 64], F32, kind="ExternalInput").ap()
    consts = nc.dram_tensor("consts", [128, C_TOT], F32, kind="ExternalInput").ap()
    yT = nc.dram_tensor("yT", [512, S], BF16, kind="ExternalOutput").ap()
    P = Prog(nc)
    st = contextlib.ExitStack()
    with st:
        lb = LayerBuilder(nc, P, st, S)
        lb.load_consts(consts)
        lb.make_strict_mask()
        if "1" in phases:
            lb.phase1(x, wc, ng, gqk, fb, cs)
        if "A" in phases:
            lb.phase_A(yT)
        if "B" in phases:
            lb.phase_B(yT)
        if "C" in phases:
            lb.phase_dense(2, yT)
        if "D" in phases:
            lb.phase_dense(3, yT)
        P.emit()
    return nc, P


def build_out_program(T):
    nc = bass.Bass("TRN2", target_bir_lowering=False)
    yT = nc.dram_tensor("yT", [D_MODEL, T], BF16, kind="ExternalInput").ap()
    wo = nc.dram_tensor("wo", [D_MODEL, D_MODEL], F32, kind="ExternalInput").ap()
    x = nc.dram_tensor("x", [T, D_MODEL], F32, kind="ExternalInput").ap()
    xo = nc.dram_tensor("xo", [T, D_MODEL], F32, kind="ExternalOutput").ap()
    P = Prog(nc)
    st = contextlib.ExitStack()
    with st:
        ar = Arena(nc, st, P, "arena", 160 * 1024)
        banks = []
        for i in range(8):
            t = st.enter_context(nc.psum_tensor(f"bank{i}", [128, 512], F32))
            banks.append((t[:, :], P.res(f"bank{i}")))
        Wb, Wb_r = ar.alloc(NCH * D_MODEL, BF16, "Wob")
        Wb3 = Wb.rearrange("p (c n) -> p c n", c=NCH)
        wst = [ar.alloc(D_MODEL, F32, f"wst{i}") for i in range(2)]
        for c in range(NCH):
            ws, ws_r = wst[c % 2]
            P.dma("sp", ws, wo[c * 128:(c + 1) * 128, :], writes=[ws_r], tile=ws_r)
            eng = "dve" if c % 2 == 0 else "pool"
            P.op(eng, lambda e, ws=ws, c=c: e.tensor_copy(out=Wb3[:, c, :], in_=ws), reads=[ws_r], acc=[Wb_r])
        yt = [ar.alloc(NCH * 128, BF16, f"yt{i}") for i in range(2)]
        xs = [ar.alloc(D_MODEL, F32, f"xs{i}") for i in range(2)]
        xos = [ar.alloc(D_MODEL, F32, f"xo{i}") for i in range(2)]
        NT = T // 128
        bi = 0
        for t in range(NT):
            y_t, y_r = yt[t % 2]
            x_t, x_r = xs[t % 2]
            o_t, o_r = xos[t % 2]
            y3 = y_t.rearrange("p (c s) -> p c s", c=NCH)
            P.dma("sp", y3, yT[:, t * 128:(t + 1) * 128].rearrange("(c p) s -> p c s", p=128), writes=[y_r], tile=y_r)
            P.dma("sp", x_t, x[t * 128:(t + 1) * 128, :], writes=[x_r], tile=x_r)
            for cg in range(4):
                bk, bk_r = banks[bi % 8]
                bi += 1
                for c in range(NCH):
                    P.op("pe", lambda e, bk=bk, c=c, cg=cg, y3=y3: e.matmul(bk, lhsT=y3[:, c, :], rhs=Wb3[:, c, cg * 512:(cg + 1) * 512], start=(c == 0), stop=(c == NCH - 1)),
                         reads=[y_r, Wb_r], writes=[bk_r] if c == 0 else (), acc=() if c == 0 else [bk_r])
                P.op("dve", lambda e, bk=bk, cg=cg, o_t=o_t, x_t=x_t: e.tensor_tensor(out=o_t[:, cg * 512:(cg + 1) * 512], in0=x_t[:, cg * 512:(cg + 1) * 512], in1=bk, op=ALU.add),
                     reads=[bk_r, x_r], acc=[o_r])
            o = P.dma("pool", xo[t * 128:(t + 1) * 128, :], o_t, reads=[o_r], tile=o_r)
            P.finish(o)
        P.emit()
    return nc, P


_CACHE = {}


def _layer_prog(S):
    if ("L", S) not in _CACHE:
        _CACHE[("L", S)] = build_layer_program(S)[0]
    return _CACHE[("L", S)]


def _out_prog(T):
    if ("O", T) not in _CACHE:
        _CACHE[("O", T)] = build_out_program(T)[0]
    return _CACHE[("O", T)]


def pack_layer_inputs(x_b, norm_gain_l, w_in_l, qn_l, kn_l, fb_l, g, cs, consts):
    cols = []
    for t in range(4):
        for m in range(4):
            c0 = t * 2048 + m * 512 + g * 128
            cols.append(np.arange(c0, c0 + 128))
    cols.append(np.array([8192 + 2 * g, 8192 + 2 * g + 1]))
    cols = np.concatenate(cols)
    wc = np.ascontiguousarray(w_in_l[:, cols])
    ng = np.ascontiguousarray(norm_gain_l.reshape(NCH, 128).T)
    one = np.ones(64, np.float32)
    gq = np.concatenate([qn_l[0], qn_l[0], one, one, qn_l[1], qn_l[1], qn_l[2], qn_l[2]])
    gk = np.concatenate([kn_l[0], kn_l[0], one, one, kn_l[1], kn_l[1], kn_l[2], kn_l[2]])
    gqk = np.ascontiguousarray(np.stack([gq, gk]).astype(np.float32))
    fb = np.ascontiguousarray(fb_l[2 * g:2 * g + 2].reshape(1, 2).astype(np.float32))
    return {"x": x_b, "wc": wc, "ng": ng, "gqk": gqk, "fb": fb, "cs": cs, "consts": consts}


def pack_fused_inputs(x_b, norm_gain, w_in, qn, kn, fbias, w_out, g, cs, consts):
    depth = norm_gain.shape[0]
    per = [pack_layer_inputs(None, norm_gain[l], w_in[l], qn[l], kn[l], fbias[l], g, cs, consts) for l in range(depth)]
    return {"x": x_b,
            "wc": np.ascontiguousarray(np.stack([p["wc"] for p in per])),
            "ng": np.ascontiguousarray(np.stack([p["ng"] for p in per])),
            "gqk": np.ascontiguousarray(np.stack([p["gqk"] for p in per])),
            "fb": np.ascontiguousarray(np.stack([p["fb"] for p in per])),
            "cs": cs, "consts": consts, "wo": w_out}


def _fused_prog(S, depth):
    if ("F", S, depth) not in _CACHE:
        _CACHE[("F", S, depth)] = build_fused_program(S, depth)[0]
    return _CACHE[("F", S, depth)]


def kernel(x, norm_gain, w_in, q_norm_gain, k_norm_gain, forget_bias, w_out):
    x = np.ascontiguousarray(np.asarray(x, dtype=np.float32))
    B, S, D = x.shape
    depth = norm_gain.shape[0]
    norm_gain = np.asarray(norm_gain, np.float32)
    w_in = np.asarray(w_in, np.float32)
    q_norm_gain = np.asarray(q_norm_gain, np.float32)
    k_norm_gain = np.asarray(k_norm_gain, np.float32)
    forget_bias = np.asarray(forget_bias, np.float32)
    w_out = np.ascontiguousarray(np.asarray(w_out, np.float32))
    cs = rope_table(S)
    consts = make_consts()
    assert B == 2
    nc = _fused_prog(S, depth)
    in_maps = []
    for c in range(8):
        b, g = c // 4, c % 4
        in_maps.append(pack_fused_inputs(x[b], norm_gain, w_in, q_norm_gain, k_norm_gain, forget_bias, w_out, g, cs, consts))
    res = run_bass_kernel_spmd(nc, in_maps, core_ids=list(range(8)))
    out = np.stack([np.asarray(res.results[0]["xo"]), np.asarray(res.results[4]["xo"])], axis=0)
    return out.astype(np.float32)


def kernel_unfused(x, norm_gain, w_in, q_norm_gain, k_norm_gain, forget_bias, w_out):
    x = np.ascontiguousarray(np.asarray(x, dtype=np.float32))
    B, S, D = x.shape
    depth = norm_gain.shape[0]
    norm_gain = np.asarray(norm_gain, np.float32)
    w_in = np.asarray(w_in, np.float32)
    q_norm_gain = np.asarray(q_norm_gain, np.float32)
    k_norm_gain = np.asarray(k_norm_gain, np.float32)
    forget_bias = np.asarray(forget_bias, np.float32)
    w_out = np.asarray(w_out, np.float32)
    cs = rope_table(S)
    consts = make_consts()
    ncores = 8
    T = B * S // ncores
    cur = x
    for l in range(depth):
        ncL = _layer_prog(S)
        in_maps = []
        for c in range(ncores):
            b, g = c // 4, c % 4
            in_maps.append(pack_layer_inputs(cur[b], norm_gain[l], w_in[l], q_norm_gain[l], k_norm_gain[l], forget_bias[l], g, cs, consts))
        res = run_bass_kernel_spmd(ncL, in_maps, core_ids=list(range(ncores)))
        yT_full = []
        for b in range(B):
            yb = np.zeros((4, 8, 64, S), dtype=ml_dtypes.bfloat16)
            for g in range(4):
                y = np.asarray(res.results[b * 4 + g]["yT"]).reshape(4, 2, 64, S)
                yb[:, 2 * g:2 * g + 2] = y
            yT_full.append(yb.reshape(2048, S))
        yT_cat = np.concatenate(yT_full, axis=1)
        xf = cur.reshape(B * S, D)
        ncO = _out_prog(T)
        in_maps = []
        for c in range(ncores):
            in_maps.append({"yT": np.ascontiguousarray(yT_cat[:, c * T:(c + 1) * T]), "wo": w_out[l],
                            "x": np.ascontiguousarray(xf[c * T:(c + 1) * T])})
        res = run_bass_kernel_spmd(ncO, in_maps, core_ids=list(range(ncores)))
        cur = np.concatenate([np.asarray(r["xo"]) for r in res.results], axis=0).reshape(B, S, D)
    return cur.astype(np.float32)
```

```python
import contextlib
import numpy as np
import ml_dtypes
import concourse.bass as bass
import concourse.mybir as mybir
from concourse.bass_utils import run_bass_kernel_spmd

F32 = mybir.dt.float32
BF16 = mybir.dt.bfloat16
AF = mybir.ActivationFunctionType
ALU = mybir.AluOpType
AX = mybir.AxisListType

D_MODEL = 2048
NCH = 16
WC = 2050
HD = 64
NEGM = -30000.0
SCALE = 0.125
EPS = 1e-6
NAUG = (0, 0, 32, 3)
ROWS = tuple(a + 64 for a in NAUG)


class Res:
    __slots__ = ("name", "w", "r", "uid", "pre")
    _n = [0]

    def __init__(self, name):
        Res._n[0] += 1
        self.uid = Res._n[0]
        self.name = name
        self.w = []
        self.r = []
        self.pre = []


class Op:
    __slots__ = ("eng", "fn", "deps", "has_dependents", "semkey", "value", "is_dma", "seq", "inc")

    def __init__(self, eng, fn, is_dma, semkey):
        self.seq = 0
        self.inc = 16 if is_dma else 1
        self.eng = eng
        self.fn = fn
        self.deps = []
        self.has_dependents = bool(is_dma)
        self.semkey = semkey
        self.value = None
        self.is_dma = is_dma


class Prog:
    ENGS = ("pe", "act", "dve", "pool", "sp")

    def __init__(self, nc):
        self.nc = nc
        self.ops = {e: [] for e in self.ENGS}
        self.nres = 0
        self.final_ops = []
        self.pending_barrier = {e: [] for e in self.ENGS}
        self.dma_since_barrier = []

    def res(self, name=None):
        self.nres += 1
        return Res(name or f"r{self.nres}")

    def _add(self, eng, fn, reads, writes, acc, is_dma, semkey, after=()):
        op = Op(eng, fn, is_dma, semkey)
        deps = list(self.pending_barrier[eng]) + list(after)
        self.pending_barrier[eng] = []
        for r in reads:
            deps.extend(r.w)
        for w in writes:
            deps.extend(w.r)
            deps.extend(w.w)
        for w in acc:
            deps.extend(w.r)
            deps.extend(w.pre)
        seen = set()
        for d in deps:
            if id(d) in seen:
                continue
            seen.add(id(d))
            if d.eng == eng and not d.is_dma and not is_dma and eng == "pe":
                continue
            d.has_dependents = True
            op.deps.append(d)
        for r in reads:
            r.r.append(op)
        for w in writes:
            w.pre = [d for d in (w.r + w.w) if d.eng != eng or d.is_dma]
            w.w = [op]
            w.r = []
        for w in acc:
            w.w.append(op)
        self.ops[eng].append(op)
        self.seqn = getattr(self, "seqn", 0) + 1
        op.seq = self.seqn
        if is_dma:
            self.dma_since_barrier.append(op)
            self.dma_by_key = getattr(self, "dma_by_key", {})
            self.dma_by_key.setdefault(semkey, []).append(op)
        return op

    def op(self, eng, fn, reads=(), writes=(), acc=(), after=()):
        return self._add(eng, fn, reads, writes, acc, False, ("eng", eng), after)

    def dma(self, eng, out_ap, in_ap, reads=(), writes=(), acc=(), tile=None, **kw):
        fn = lambda e: e.dma_start(out=out_ap, in_=in_ap, **kw)
        return self._add(eng, fn, reads, writes, acc, True, ("res", tile.name))

    def collective(self, fn, reads, writes, res):
        op = self._add("pool", fn, reads, writes, (), True, ("res", res.name))
        op.inc = 1
        return op

    def finish(self, op):
        op.has_dependents = True
        self.final_ops.append(op)

    def barrier(self):
        deps = []
        for e in self.ENGS:
            for op in reversed(self.ops[e]):
                if not op.is_dma:
                    deps.append(op)
                    break
        deps.extend(self.dma_since_barrier)
        self.dma_since_barrier = []
        for e in self.ENGS:
            self.pending_barrier[e] = list(self.pending_barrier[e]) + deps

    def emit(self):
        nc = self.nc
        semvals = {}
        semkeys = []
        for e in self.ENGS:
            for op in self.ops[e]:
                if op.has_dependents:
                    k = op.semkey
                    if k not in semvals:
                        semvals[k] = 0
                        semkeys.append(k)
                    semvals[k] += op.inc
                    op.value = semvals[k]
        stack = contextlib.ExitStack()
        sems = {}
        for i, k in enumerate(semkeys):
            sems[k] = stack.enter_context(nc.semaphore(f"s{i}"))
        self.nsems = len(semkeys)
        ops = self.ops
        final_ops = self.final_ops

        import bisect
        dma_by_key = getattr(self, "dma_by_key", {})
        dma_seqs = {k: [o.seq for o in v] for k, v in dma_by_key.items()}

        def need(op_seq, deps):
            req = {}
            for d in deps:
                k = d.semkey
                v = d.value
                if d.is_dma:
                    lst = dma_by_key[k]
                    i = bisect.bisect_left(dma_seqs[k], op_seq) - 1
                    if i >= 0 and lst[i].value > v:
                        v = lst[i].value
                if req.get(k, 0) < v:
                    req[k] = v
            return req

        def run(ename, e):
            seen = {}
            for op in ops[ename]:
                for k, v in need(op.seq, op.deps).items():
                    if seen.get(k, 0) >= v:
                        continue
                    e.wait_ge(sems[k], v)
                    seen[k] = v
                ins = op.fn(e)
                if op.has_dependents:
                    ins.then_inc(sems[op.semkey], op.inc)
            if ename == "sp":
                for k, v in need(10 ** 12, final_ops).items():
                    if seen.get(k, 0) >= v:
                        continue
                    e.wait_ge(sems[k], v)
                    seen[k] = v

        with stack:
            with nc.Block() as block:
                @block.tensor
                def _(e):
                    run("pe", e)

                @block.scalar
                def _(e):
                    run("act", e)

                @block.vector
                def _(e):
                    run("dve", e)

                @block.gpsimd
                def _(e):
                    run("pool", e)

                @block.sync
                def _(e):
                    run("sp", e)


class _PH:
    __slots__ = ("real",)

    def __init__(self):
        self.real = None


class Deferred:
    def __init__(self):
        self.calls = []

    def op(self, *a, **kw):
        ph = _PH()
        self.calls.append(("op", a, kw, ph))
        return ph

    def dma(self, *a, **kw):
        ph = _PH()
        self.calls.append(("dma", a, kw, ph))
        return ph


def replay_interleaved(P, streams):
    idx = [0] * len(streams)
    live = True
    while live:
        live = False
        for si, st_ in enumerate(streams):
            if idx[si] < len(st_.calls):
                kind, a, kw, ph = st_.calls[idx[si]]
                idx[si] += 1
                if "after" in kw:
                    kw = dict(kw)
                    kw["after"] = [x.real if isinstance(x, _PH) else x for x in kw["after"]]
                ph.real = getattr(P, kind)(*a, **kw)
                live = True


class Arena:
    def __init__(self, nc, st, P, name, nbytes):
        self.t = st.enter_context(nc.sbuf_tensor(name, [128, nbytes // 2], BF16))
        self.cap = nbytes
        self.off = 0
        self.P = P
        self.name = name
        self.n = 0

    def alloc(self, cols, dt, name=None):
        esz = 4 if dt == F32 else 2
        off = (self.off + 63) // 64 * 64
        size = cols * esz
        assert off + size <= self.cap, (self.name, name, off, size, self.cap)
        ap = self.t[:, off // 2:(off + size) // 2]
        if dt == F32:
            ap = ap.bitcast(F32)
        self.off = off + size
        self.n += 1
        return ap, self.P.res(f"{self.name}.{name or self.n}")

    def mark(self):
        return self.off

    def reset(self, mark=0):
        self.off = mark


C_ID = 0
C_TRI = 128
C_ONES = 256
C_NTRI = 384
C_NONES = 512
C_CM = 640
C_BM = 640 + 2048
C_TOT = C_BM + 256


def make_consts():
    c = np.zeros((128, C_TOT), np.float32)
    i = np.arange(128)
    c[:, C_ID:C_ID + 128] = np.eye(128)
    c[:, C_TRI:C_TRI + 128] = (i[:, None] <= i[None, :])
    c[:, C_ONES:C_ONES + 128] = 1.0
    c[:, C_NTRI:C_NTRI + 128] = -(i[:, None] >= i[None, :]).astype(np.float32)
    c[:, C_NONES:C_NONES + 128] = -1.0
    q = np.arange(512)
    for d in range(4):
        c[:, C_CM + d * 512:C_CM + (d + 1) * 512] = np.where(q[None, :] >= 128 * d + i[:, None], 0.0, NEGM)
    c[:, C_BM:C_BM + 128] = np.where(i[:, None] >= i[None, :], 0.0, NEGM)
    c[:, C_BM + 128:C_BM + 256] = np.where(i[:, None] <= i[None, :], 0.0, NEGM)
    return c


def rope_table(S):
    inv = (1.0 / (np.float32(10000.0) ** (np.arange(0, HD, 2, dtype=np.float32) / np.float32(HD)))).astype(np.float32)
    ang = (np.arange(S, dtype=np.float32)[:, None] * inv[None, :]).astype(np.float32)
    return np.concatenate([np.cos(ang), np.sin(ang)], axis=1).astype(np.float32)


class LayerBuilder:
    def __init__(self, nc, P, st, S):
        self.nc, self.P, self.st, self.S = nc, P, st, S
        self.NT = S // 128
        self.NQ = S // 512
        assert S % 2048 == 0
        self.ar = Arena(nc, st, P, "arena", 180 * 1024)
        self.cst = Arena(nc, st, P, "cst", 24 * 1024)
        self.banks = []
        for i in range(8):
            t = st.enter_context(nc.psum_tensor(f"bank{i}", [128, 512], F32))
            self.banks.append((t[:, :], P.res(f"bank{i}")))
        self.QA = nc.dram_tensor("QA", [8, 96, S], BF16).ap()
        self.KA = nc.dram_tensor("KA", [8, 96, S], BF16).ap()
        self.V = nc.dram_tensor("Vs", [S, 512], BF16).ap()
        self.GT = nc.dram_tensor("GT", [4, 128, S], BF16).ap()
        self.QA_r = [P.res(f"QA{h}") for h in range(8)]
        self.KA_r = [P.res(f"KA{h}") for h in range(8)]
        self.V_r = P.res("Vd")
        self.GT_r = P.res("GTd")
        self.yT_r = P.res("yTd")
        self.yT_external = True

    def load_consts(self, consts_ap):
        P, cst = self.P, self.cst
        cf, cf_r = cst.alloc(C_TOT, F32, "cf")
        self.cf, self.cf_r = cf, cf_r
        P.dma("sp", cf, consts_ap, writes=[cf_r], tile=cf_r)
        self.idb, self.idb_r = cst.alloc(128, BF16, "idb")
        self.cmb, self.cmb_r = cst.alloc(2048, BF16, "cmb")
        self.bmb, self.bmb_r = cst.alloc(256, BF16, "bmb")
        self.oneb, self.oneb_r = cst.alloc(128, BF16, "oneb")
        P.op("dve", lambda e: e.tensor_copy(out=self.idb, in_=cf[:, C_ID:C_ID + 128]), reads=[cf_r], writes=[self.idb_r])
        P.op("dve", lambda e: e.tensor_copy(out=self.cmb, in_=cf[:, C_CM:C_CM + 2048]), reads=[cf_r], writes=[self.cmb_r])
        P.op("dve", lambda e: e.tensor_copy(out=self.bmb, in_=cf[:, C_BM:C_BM + 256]), reads=[cf_r], writes=[self.bmb_r])
        P.op("dve", lambda e: e.tensor_copy(out=self.oneb, in_=cf[:, C_ONES:C_ONES + 128]), reads=[cf_r], writes=[self.oneb_r])
        self.negF, self.negF_r = cst.alloc(self.NT * 2, F32, "negF")

    def phase1(self, x_ap, wc_ap, ng_ap, gqk_ap, fb_ap, cs_ap):
        P, ar, S, NT = self.P, self.ar, self.S, self.NT
        cf, cf_r = self.cf, self.cf_r
        idb, idb_r = self.idb, self.idb_r
        ar.reset()
        Wb, Wb_r = ar.alloc(NCH * WC, BF16, "Wb")
        Wb3 = Wb.rearrange("p (c n) -> p c n", c=NCH)
        ng, ng_r = ar.alloc(NCH, F32, "ng")
        Gq, Gq_r = ar.alloc(512, F32, "Gq")
        Gk, Gk_r = ar.alloc(512, F32, "Gk")
        nfb, nfb_r = ar.alloc(2, F32, "nfb")
        kmT = [ar.alloc(32, F32, f"kmT{j}") for j in range(2)]
        carry, carry_r = ar.alloc(2, F32, "carry")
        P.dma("sp", ng, ng_ap, writes=[ng_r], tile=ng_r)
        P.dma("sp", Gq, gqk_ap[0].partition_broadcast(128), writes=[Gq_r], tile=Gq_r)
        P.dma("sp", Gk, gqk_ap[1].partition_broadcast(128), writes=[Gk_r], tile=Gk_r)
        P.dma("sp", nfb, fb_ap[0].partition_broadcast(128), writes=[nfb_r], tile=nfb_r)
        P.op("dve", lambda e: e.tensor_scalar_mul(out=Gq, in0=Gq, scalar1=SCALE), reads=[Gq_r], writes=[Gq_r])
        P.op("dve", lambda e: e.tensor_scalar_mul(out=nfb, in0=nfb, scalar1=-1.0), reads=[nfb_r], writes=[nfb_r])
        P.op("dve", lambda e: e.memset(carry, 0.0), writes=[carry_r])
        for j in range(2):
            P.op("dve", lambda e, j=j: e.memset(kmT[j][0], 0.0), writes=[kmT[j][1]])
        wst = [ar.alloc(WC, F32, f"wst{i}") for i in range(2)]
        for c in range(NCH):
            ws, ws_r = wst[c % 2]
            P.dma("sp", ws, wc_ap[c * 128:(c + 1) * 128, :], writes=[ws_r], tile=ws_r)
            eng = "dve" if c % 2 == 0 else "pool"
            P.op(eng, lambda e, ws=ws, c=c: e.tensor_scalar(out=Wb3[:, c, :], in0=ws, scalar1=ng[:, c:c + 1], scalar2=None, op0=ALU.mult),
                 reads=[ws_r, ng_r], acc=[Wb_r])

        xs = [ar.alloc(D_MODEL, F32, f"xs{i}") for i in range(2)]
        xn = [ar.alloc(D_MODEL, BF16, f"xn{i}") for i in range(2)]
        xT = [ar.alloc(D_MODEL, BF16, f"xT{i}") for i in range(2)]
        junk, junk_r = ar.alloc(D_MODEL, BF16, "junk")
        ss = [ar.alloc(1, F32, f"ss{i}") for i in range(2)]
        rr = [ar.alloc(1, F32, f"rr{i}") for i in range(2)]
        cs = [ar.alloc(64, F32, f"cs{i}") for i in range(3)]
        qsb = [ar.alloc(512, F32, f"qsb{i}") for i in range(2)]
        ksb = [ar.alloc(512, F32, f"ksb{i}") for i in range(2)]
        pfs = [ar.alloc(2, F32, f"pfs{i}") for i in range(2)]
        tmps = {w: ar.alloc(512, F32, f"tmp{w}") for w in "qk"}
        sshs = {w: ar.alloc(8, F32, f"ssh{w}") for w in "qk"}
        rtsd = {w: [ar.alloc(128, F32, f"rt{w}{i}") for i in range(4)] for w in "qk"}
        QAtm = [ar.alloc(8 * 96, BF16, f"QAtm{i}") for i in range(2)]
        KAtm = [ar.alloc(8 * 96, BF16, f"KAtm{i}") for i in range(2)]
        QTst = [ar.alloc(8 * 128, BF16, f"QTst{i}") for i in range(2)]
        KTst = [ar.alloc(8 * 128, BF16, f"KTst{i}") for i in range(2)]
        vst = [ar.alloc(512, BF16, f"vst{i}") for i in range(2)]
        gst = [ar.alloc(512, BF16, f"gst{i}") for i in range(2)]
        qTf = [ar.alloc(128, F32, f"qTf{j}") for j in range(2)]
        gm = [ar.alloc(32, F32, f"gm{j}") for j in range(2)]
        top8 = [ar.alloc(8, F32, f"top8{j}") for j in range(2)]
        sel = [ar.alloc(32, F32, f"sel{j}") for j in range(2)]
        ef, ef_r = ar.alloc(2, F32, "ef")
        lf, lf_r = ar.alloc(2, F32, "lf")
        Ft, Ft_r = ar.alloc(2, F32, "Ft")
        r1, r1_r = ar.alloc(2, F32, "r1")
        r2, r2_r = ar.alloc(2, F32, "r2")
        hif, hif_r = ar.alloc(2, F32, "hif")
        hb = [ar.alloc(2, BF16, f"hb{i}") for i in range(3)]

        (pT, pT_r) = self.banks[0][0], self.banks[0][1]
        pTb0 = self.banks[0][0][:, :].bitcast(BF16)
        pTb1 = self.banks[1][0][:, :].bitcast(BF16)
        pT1_r = self.banks[1][1]
        pq, pq_r = self.banks[2]
        pk, pk_r = self.banks[3]
        pv, pv_r = self.banks[4]
        pg, pg_r = self.banks[5]
        pm, pm_r = self.banks[6]
        pm2, pm2_r = self.banks[7]
        import os as _os2
        OLDB = bool(_os2.environ.get("OLD_BANKS"))
        pT7 = pTb0 if OLDB else pm2.bitcast(BF16)
        pT7_r = pT_r if OLDB else pm2_r
        qtb, qtb_r, qtoff = (pm2, pm2_r, 0) if OLDB else (pm, pm_r, 128)
        pf_r = pF_r = pkm_r = pm_r
        pgt_r = [pm_r, pm_r]
        pqT_r = [pm2_r, pm2_r]

        def load_x(t):
            x_t, x_r = xs[t % 2]
            xsr = getattr(self, "x_src_r", None)
            P.dma("sp", x_t, x_ap[t * 128:(t + 1) * 128, :], reads=[xsr] if xsr is not None else (), writes=[x_r], tile=x_r)
            c_t, c_r = cs[t % 3]
            P.dma("sp", c_t, cs_ap[t * 128:(t + 1) * 128, :], writes=[c_r], tile=c_r)

        def post_qk(P, t, which, src_bank, src_r, G, G_r, ATM, TST, DR, DR_r):
            tmp, tmp_r = tmps[which]
            ssh, ssh_r = sshs[which]
            rt = rtsd[which]
            w_t, w_r = (qsb if which == "q" else ksb)[t % 2]
            c_t, c_r = cs[t % 3]
            atm, atm_r = ATM[t % 2]
            atm3 = atm.rearrange("p (h r) -> p h r", h=8)
            tst, tst_r = TST[t % 2]
            b = t // 2
            pf_t, pfs_r = pfs[t % 2]
            P.op("dve", lambda e: e.tensor_tensor(out=tmp, in0=w_t, in1=w_t, op=ALU.mult), reads=[w_r], writes=[tmp_r])
            P.op("dve", lambda e: e.reduce_sum(out=ssh, in_=tmp.rearrange("p (h d) -> p h d", h=8), axis=AX.X), reads=[tmp_r], writes=[ssh_r])
            P.op("act", lambda e: e.activation(out=ssh, in_=ssh, func=AF.Sqrt, bias=EPS, scale=1.0 / HD), reads=[ssh_r], writes=[ssh_r])
            P.op("dve", lambda e: e.reciprocal(out=ssh, in_=ssh), reads=[ssh_r], writes=[ssh_r])
            P.op("dve", lambda e: e.memset(ssh[:, 2:4], 1.0), reads=[ssh_r], writes=[ssh_r])
            w3 = w_t.rearrange("p (h d) -> p h d", h=8)
            P.op("dve", lambda e: e.tensor_tensor(out=w3, in0=w3, in1=ssh.unsqueeze(2).to_broadcast([128, 8, HD]), op=ALU.mult), reads=[w_r, ssh_r], writes=[w_r])
            P.op("dve", lambda e: e.tensor_tensor(out=w_t, in0=w_t, in1=G, op=ALU.mult), reads=[w_r, G_r], writes=[w_r])
            w4 = w_t.rearrange("p (a b d) -> p a b d", a=2, b=4)
            q1 = w4[:, :, 0:2, 0:32]
            q2 = w4[:, :, 0:2, 32:64]
            cosb = c_t[:, 0:32].unsqueeze(1).unsqueeze(1).to_broadcast([128, 2, 2, 32])
            sinb = c_t[:, 32:64].unsqueeze(1).unsqueeze(1).to_broadcast([128, 2, 2, 32])
            rts = [(a.rearrange("p (a b d) -> p a b d", a=2, b=2), r) for a, r in rt]
            P.op("dve", lambda e: e.tensor_tensor(out=rts[0][0], in0=q1, in1=cosb, op=ALU.mult), reads=[w_r, c_r], writes=[rts[0][1]])
            P.op("dve", lambda e: e.tensor_tensor(out=rts[1][0], in0=q2, in1=sinb, op=ALU.mult), reads=[w_r, c_r], writes=[rts[1][1]])
            P.op("pool", lambda e: e.tensor_tensor(out=rts[2][0], in0=q2, in1=cosb, op=ALU.mult), reads=[w_r, c_r], writes=[rts[2][1]])
            P.op("pool", lambda e: e.tensor_tensor(out=rts[3][0], in0=q1, in1=sinb, op=ALU.mult), reads=[w_r, c_r], writes=[rts[3][1]])
            P.op("dve", lambda e: e.tensor_tensor(out=q1, in0=rts[0][0], in1=rts[1][0], op=ALU.subtract), reads=[rts[0][1], rts[1][1]], writes=[w_r])
            P.op("dve", lambda e: e.tensor_tensor(out=q2, in0=rts[2][0], in1=rts[3][0], op=ALU.add), reads=[rts[2][1], rts[3][1], w_r], writes=[w_r])
            P.op("dve", lambda e: e.tensor_copy(out=atm3[:, 0:4, 0:64], in_=w3[:, 0:4, :]), reads=[w_r], acc=[atm_r])
            P.op("pool", lambda e: e.tensor_copy(out=atm3[:, 4:6, 32:96], in_=w3[:, 4:6, :]), reads=[w_r], acc=[atm_r])
            P.op("pool", lambda e: e.tensor_copy(out=atm3[:, 6:8, 3:67], in_=w3[:, 6:8, :]), reads=[w_r], acc=[atm_r])
            if which == "q":
                for j in range(2):
                    h = 4 + j
                    qT_t, qT_r = qTf[j]
                    g_t, g_r = gm[j]
                    t8, t8_r = top8[j]
                    s_t, s_r = sel[j]
                    if b > 0:
                        P.op("pe", lambda e, h=h, j=j: e.transpose(out=qtb[0:64, qtoff + j * 128:qtoff + (j + 1) * 128], in_=w_t[:, h * 64:(h + 1) * 64], identity=cf[:, C_ID:C_ID + 128]),
                             reads=[w_r, cf_r], acc=[qtb_r])
                        P.op("act", lambda e, j=j, qT_t=qT_t: e.copy(out=qT_t[0:64, :], in_=qtb[0:64, qtoff + j * 128:qtoff + (j + 1) * 128]), reads=[qtb_r], writes=[qT_r])
                        P.op("pe", lambda e, j=j, qT_t=qT_t: e.matmul(pm[:, 16 + 32 * j:48 + 32 * j], lhsT=qT_t[0:64, :], rhs=kmT[j][0][0:64, :], start=True, stop=True),
                             reads=[qT_r, kmT[j][1]], acc=[pgt_r[j]])
                        P.op("dve", lambda e, g_t=g_t: e.memset(g_t, -1e30), writes=[g_r])
                        P.op("dve", lambda e, j=j, g_t=g_t: e.tensor_copy(out=g_t[:, 0:b], in_=pm[:, 16 + 32 * j:16 + 32 * j + b]), reads=[pgt_r[j]], writes=[g_r])
                        P.op("dve", lambda e, g_t=g_t, t8=t8: e.max(out=t8, in_=g_t), reads=[g_r], writes=[t8_r])
                        P.op("dve", lambda e, g_t=g_t, t8=t8, s_t=s_t: e.tensor_scalar(out=s_t, in0=g_t, scalar1=t8[:, 2:3], scalar2=-NEGM, op0=ALU.is_ge, op1=ALU.mult),
                             reads=[g_r, t8_r], writes=[s_r])
                        o1 = P.op("dve", lambda e, h=h, s_t=s_t: e.tensor_scalar_add(out=atm3[:, h, 0:32], in0=s_t, scalar1=NEGM), reads=[s_r], acc=[atm_r])
                    else:
                        o1 = P.op("dve", lambda e, h=h: e.memset(atm3[:, h, 0:32], NEGM), acc=[atm_r])
                    P.op("dve", lambda e, h=h: e.memset(atm3[:, h, b:b + 1], 0.0), acc=[atm_r], after=[o1])
                for j in range(2):
                    P.op("act", lambda e, j=j: e.activation(out=ef[:, j:j + 1], in_=pf_t[:, j:j + 1], func=AF.Exp, bias=nfb[:, j:j + 1], scale=-1.0),
                         reads=[pfs_r, nfb_r], acc=[ef_r])
                P.op("act", lambda e: e.activation(out=lf, in_=ef, func=AF.Ln, bias=1.0, scale=1.0), reads=[ef_r], writes=[lf_r])
                P.op("dve", lambda e: e.tensor_scalar_mul(out=lf, in0=lf, scalar1=-1.0), reads=[lf_r], writes=[lf_r])
                P.op("pe", lambda e: e.matmul(pm[:, 8:10], lhsT=cf[:, C_TRI:C_TRI + 128], rhs=lf, start=True, stop=True), reads=[cf_r, lf_r], acc=[pF_r])
                P.op("pe", lambda e: e.matmul(pm[:, 10:12], lhsT=cf[:, C_ONES:C_ONES + 128], rhs=lf, start=True, stop=True), reads=[cf_r, lf_r], acc=[pF_r])
                P.op("dve", lambda e: e.tensor_tensor(out=Ft, in0=pm[:, 8:10], in1=carry, op=ALU.add), reads=[pF_r, carry_r], writes=[Ft_r])
                P.op("dve", lambda e: e.tensor_tensor(out=carry, in0=pm[:, 10:12], in1=carry, op=ALU.add), reads=[pF_r, carry_r], writes=[carry_r])
                P.op("dve", lambda e: e.tensor_scalar_mul(out=self.negF[:, 2 * t:2 * t + 2], in0=Ft, scalar1=-1.0), reads=[Ft_r], acc=[self.negF_r])
                P.op("dve", lambda e: e.tensor_copy(out=hb[0][0], in_=Ft), reads=[Ft_r], writes=[hb[0][1]])
                P.op("dve", lambda e: e.tensor_copy(out=hif, in_=hb[0][0]), reads=[hb[0][1]], writes=[hif_r])
                P.op("dve", lambda e: e.tensor_tensor(out=r1, in0=Ft, in1=hif, op=ALU.subtract), reads=[Ft_r, hif_r], writes=[r1_r])
                P.op("dve", lambda e: e.tensor_copy(out=hb[1][0], in_=r1), reads=[r1_r], writes=[hb[1][1]])
                P.op("dve", lambda e: e.tensor_copy(out=hif, in_=hb[1][0]), reads=[hb[1][1]], writes=[hif_r])
                P.op("dve", lambda e: e.tensor_tensor(out=r2, in0=r1, in1=hif, op=ALU.subtract), reads=[r1_r, hif_r], writes=[r2_r])
                for i3 in range(2):
                    P.op("dve", lambda e, i3=i3: e.tensor_copy(out=atm3[:, 6:8, i3:i3 + 1], in_=hb[i3][0].unsqueeze(2)), reads=[hb[i3][1]], acc=[atm_r])
                P.op("dve", lambda e: e.tensor_copy(out=atm3[:, 6:8, 2:3], in_=r2.unsqueeze(2)), reads=[r2_r], acc=[atm_r])
            else:
                o1 = P.op("pool", lambda e: e.memset(atm3[:, 4:6, 0:32], 0.0), acc=[atm_r])
                P.op("pool", lambda e: e.memset(atm3[:, 4:6, b:b + 1], 1.0), acc=[atm_r], after=[o1])
                P.op("pool", lambda e: e.memset(atm3[:, 6:8, 0:3], 1.0), acc=[atm_r])
                for j in range(2):
                    h = 4 + j
                    P.op("pe", lambda e, h=h, j=j: e.matmul(pm[0:64, 96 + j:97 + j], lhsT=w_t[:, h * 64:(h + 1) * 64], rhs=cf[:, C_ONES:C_ONES + 1],
                                                         start=True, stop=True),
                         reads=[w_r, cf_r], acc=[pkm_r])
                for j in range(2):
                    P.op("dve", lambda e, j=j: e.scalar_tensor_tensor(out=kmT[j][0][0:64, b:b + 1], in0=pm[0:64, 96 + j:97 + j], scalar=1.0 / 256.0, in1=kmT[j][0][0:64, b:b + 1],
                                                                   op0=ALU.mult, op1=ALU.add),
                         reads=[pkm_r, kmT[j][1]], writes=[kmT[j][1]])
            for h in range(8):
                R = ROWS[h // 2]
                dst = (pTb0 if h < 8 else None)
                P.op("pe", lambda e, h=h, R=R: e.transpose(out=pT7[0:R, h * 128:(h + 1) * 128], in_=atm3[:, h, 0:R], identity=idb),
                     reads=[atm_r, idb_r], writes=[pT7_r] if h == 0 else (), acc=() if h == 0 else [pT7_r])
            tst3 = tst.rearrange("p (h s) -> p h s", h=8)
            pT3 = pT7.rearrange("p (h s) -> p h s", h=8)
            P.op("act", lambda e: e.copy(out=tst3[0:64, 0:4, :], in_=pT3[0:64, 0:4, :]), reads=[pT7_r], writes=[tst_r])
            if True:
                P.op("act", lambda e: e.copy(out=tst3[0:96, 4:6, :], in_=pT3[0:96, 4:6, :]), reads=[pT7_r], acc=[tst_r])
            else:
                P.op("dve", lambda e: e.tensor_copy(out=tst3[0:96, 4:6, :], in_=pT3[0:96, 4:6, :]), reads=[pT7_r], acc=[tst_r])
            P.op("act", lambda e: e.copy(out=tst3[0:67, 6:8, :], in_=pT3[0:67, 6:8, :]), reads=[pT7_r], acc=[tst_r])
            sl = slice(t * 128, (t + 1) * 128)
            P.dma("pool", DR[0:4, 0:64, sl].rearrange("h r s -> r h s"), tst3[0:64, 0:4, :], reads=[tst_r], acc=DR_r[0:4], tile=tst_r)
            P.dma("pool", DR[4:6, 0:96, sl].rearrange("h r s -> r h s"), tst3[0:96, 4:6, :], reads=[tst_r], acc=DR_r[4:6], tile=tst_r)
            P.dma("pool", DR[6:8, 0:67, sl].rearrange("h r s -> r h s"), tst3[0:67, 6:8, :], reads=[tst_r], acc=DR_r[6:8], tile=tst_r)

        def main(t):
            x_t, x_r = xs[t % 2]
            n_t, n_r = xn[t % 2]
            T_t, T_r = xT[t % 2]
            s_t, s_r = ss[t % 2]
            r_t, r_r = rr[t % 2]
            P.op("act", lambda e: e.activation(out=junk, in_=x_t, func=AF.Square, accum_out=s_t), reads=[x_r], writes=[junk_r, s_r])
            P.op("act", lambda e: e.activation(out=r_t, in_=s_t, func=AF.Sqrt, bias=EPS, scale=1.0 / D_MODEL), reads=[s_r], writes=[r_r])
            P.op("dve", lambda e: e.reciprocal(out=r_t, in_=r_t), reads=[r_r], writes=[r_r])
            P.op("dve", lambda e: e.tensor_scalar(out=n_t, in0=x_t, scalar1=r_t, scalar2=None, op0=ALU.mult), reads=[x_r, r_r], writes=[n_r])
            for c in range(NCH):
                dst = pTb0 if c < 8 else pTb1
                dr = pT_r if c < 8 else pT1_r
                cc = c % 8
                P.op("pe", lambda e, dst=dst, cc=cc, c=c: e.transpose(out=dst[:, cc * 128:(cc + 1) * 128], in_=n_t[:, c * 128:(c + 1) * 128], identity=idb),
                     reads=[n_r, idb_r], writes=[dr] if cc == 0 else (), acc=() if cc == 0 else [dr])
            P.op("act", lambda e: e.copy(out=T_t[:, 0:1024], in_=pTb0), reads=[pT_r], writes=[T_r])
            P.op("dve", lambda e: e.tensor_copy(out=T_t[:, 1024:2048], in_=pTb1), reads=[pT1_r], acc=[T_r])
            for (bank, b_r, c0) in ((pk, pk_r, 512), (pq, pq_r, 0), (pv, pv_r, 1024)):
                for c in range(NCH):
                    P.op("pe", lambda e, bank=bank, c=c, c0=c0: e.matmul(bank, lhsT=T_t[:, c * 128:(c + 1) * 128], rhs=Wb3[:, c, c0:c0 + 512], start=(c == 0), stop=(c == NCH - 1)),
                         reads=[T_r, Wb_r], writes=[b_r] if c == 0 else (), acc=() if c == 0 else [b_r])
            for m in range(4):
                for c in range(NCH):
                    first = (m == 0 and c == 0)
                    P.op("pe", lambda e, m=m, c=c: e.matmul(pg[:, m * 128:(m + 1) * 128], lhsT=Wb3[:, c, 1536 + m * 128:1536 + (m + 1) * 128], rhs=T_t[:, c * 128:(c + 1) * 128], start=(c == 0), stop=(c == NCH - 1)),
                         reads=[T_r, Wb_r], writes=[pg_r] if first else (), acc=() if first else [pg_r])
            for c in range(NCH):
                P.op("pe", lambda e, c=c: e.matmul(pm[:, 0:2], lhsT=T_t[:, c * 128:(c + 1) * 128], rhs=Wb3[:, c, 2048:2050], start=(c == 0), stop=(c == NCH - 1)),
                     reads=[T_r, Wb_r], writes=[pf_r] if c == 0 else (), acc=() if c == 0 else [pf_r])

        def post_a(t):
            v_t, v_r = vst[t % 2]
            g_t, g_r = gst[t % 2]
            k_t, k_r = ksb[t % 2]
            q_t, q_r = qsb[t % 2]
            pf_t, pfs_r = pfs[t % 2]
            P.op("act", lambda e: e.copy(out=k_t, in_=pk), reads=[pk_r], writes=[k_r])
            if True:
                P.op("act", lambda e: e.copy(out=q_t, in_=pq), reads=[pq_r], writes=[q_r])
            else:
                P.op("dve", lambda e: e.tensor_copy(out=q_t, in_=pq), reads=[pq_r], writes=[q_r])
            P.op("act", lambda e: e.copy(out=v_t, in_=pv), reads=[pv_r], writes=[v_r])
            P.dma("pool", self.V[t * 128:(t + 1) * 128, :], v_t, reads=[v_r], acc=[self.V_r], tile=v_r)
            if True:
                P.op("act", lambda e: e.copy(out=pf_t, in_=pm[:, 0:2]), reads=[pf_r], writes=[pfs_r])
            else:
                P.op("dve", lambda e: e.tensor_copy(out=pf_t, in_=pm[:, 0:2]), reads=[pf_r], writes=[pfs_r])
            P.op("act", lambda e: e.activation(out=g_t, in_=pg, func=AF.Silu), reads=[pg_r], writes=[g_r])
            P.dma("pool", self.GT[:, :, t * 128:(t + 1) * 128].rearrange("m c s -> c m s"), g_t.rearrange("p (m s) -> p m s", m=4), reads=[g_r], acc=[self.GT_r], tile=g_r)

        def post_b(t):
            dk, dq = Deferred(), Deferred()
            post_qk(dk, t, "k", pk, pk_r, Gk, Gk_r, KAtm, KTst, self.KA, self.KA_r)
            post_qk(dq, t, "q", pq, pq_r, Gq, Gq_r, QAtm, QTst, self.QA, self.QA_r)
            replay_interleaved(P, [dk, dq])

        import os as _os
        if _os.environ.get("PH1_SEQ"):
            load_x(0)
            if NT > 1:
                load_x(1)
            for t in range(NT):
                if t + 2 < NT:
                    load_x(t + 2)
                main(t)
                post_a(t)
                post_b(t)
        else:
            load_x(0)
            if NT > 1:
                load_x(1)
            main(0)
            post_a(0)
            for t in range(NT):
                if t + 2 < NT:
                    load_x(t + 2)
                if t + 1 < NT:
                    main(t + 1)
                post_b(t)
                if t + 1 < NT:
                    post_a(t + 1)
        P.barrier()

    def load_qkv(self, m, need_q=True):
        P, ar, S, NT = self.P, self.ar, self.S, self.NT
        R = ROWS[m]
        QT, KT = [], []
        for j in range(2):
            h = 2 * m + j
            q_t, q_r = ar.alloc(S, BF16, f"QT{m}{j}")
            k_t, k_r = ar.alloc(S, BF16, f"KT{m}{j}")
            nsp = 4
            for i in range(nsp):
                sl = slice(i * S // nsp, (i + 1) * S // nsp)
                P.dma("sp", q_t[0:R, sl], self.QA[h, 0:R, sl], reads=[self.QA_r[h]], acc=[q_r], tile=q_r)
                P.dma("sp", k_t[0:R, sl], self.KA[h, 0:R, sl], reads=[self.KA_r[h]], acc=[k_r], tile=k_r)
            QT.append((q_t, q_r))
            KT.append((k_t, k_r))
        v_t, v_r = ar.alloc(NT * 128, BF16, f"V{m}")
        v3 = v_t.rearrange("p (n c) -> p n c", n=NT)
        nsp = 4
        for i in range(nsp):
            n0, n1 = i * NT // nsp, (i + 1) * NT // nsp
            P.dma("sp", v3[:, n0:n1, :], self.V[n0 * 128:n1 * 128, m * 128:(m + 1) * 128].rearrange("(n p) c -> p n c", p=128),
                  reads=[self.V_r], acc=[v_r], tile=v_r)
        return QT, KT, (v3, v_r)

    def finish_qtile(self, m, qt, num_ap, num_r, den_ap, den_r, yT_ap, bufs, has_den=True):
        P = self.P
        i = bufs["i"]
        bufs["i"] += 1
        gt_t, gt_r = bufs["gt"][i % 2]
        y32, y32_r = bufs["y32"][i % 2]
        yb, yb_r = bufs["yb"][i % 2]
        sl = slice(qt * 512, (qt + 1) * 512)
        P.dma("sp", gt_t, self.GT[m, :, sl], reads=[self.GT_r], writes=[gt_r], tile=gt_r)
        if has_den:
            P.op("dve", lambda e: e.reciprocal(out=y32, in_=den_ap), reads=[den_r], writes=[y32_r])
            P.op("dve", lambda e: e.tensor_tensor(out=y32, in0=num_ap, in1=y32, op=ALU.mult), reads=[num_r, y32_r], writes=[y32_r])
            P.op("pool", lambda e: e.tensor_tensor(out=yb, in0=y32, in1=gt_t, op=ALU.mult), reads=[y32_r, gt_r], writes=[yb_r])
        else:
            P.op("dve", lambda e: e.tensor_tensor(out=yb, in0=num_ap, in1=gt_t, op=ALU.mult), reads=[num_r, gt_r], writes=[yb_r])
        dst_ap = self.y_dst(m, qt) if getattr(self, "y_dst", None) is not None else yT_ap[m * 128:(m + 1) * 128, sl]
        o = P.dma("pool", dst_ap, yb, reads=[yb_r], acc=[self.yT_r], tile=yb_r)
        if self.yT_external:
            P.finish(o)

    def out_bufs(self):
        ar = self.ar
        return {"i": 0,
                "gt": [ar.alloc(512, BF16, f"gt{i}") for i in range(2)],
                "y32": [ar.alloc(512, F32, f"y32{i}") for i in range(2)],
                "yb": [ar.alloc(512, BF16, f"yb{i}") for i in range(2)]}

    def load_vaug(self, m):
        P, ar, S, NT = self.P, self.ar, self.S, self.NT
        va, va_r = ar.alloc(NT * 256, BF16, f"VA{m}")
        va4 = va.rearrange("p (n j c) -> p n j c", n=NT, j=2)
        o1 = P.op("pool", lambda e: e.memset(va4[:, :, :, 64:128], 1.0), writes=[va_r])
        nsp = 4
        for i in range(nsp):
            n0, n1 = i * NT // nsp, (i + 1) * NT // nsp
            for j in range(2):
                P.dma("sp", va4[:, n0:n1, j, 0:64], self.V[n0 * 128:n1 * 128, m * 128 + j * 64:m * 128 + (j + 1) * 64].rearrange("(n p) c -> p n c", p=128),
                      reads=[self.V_r], acc=[va_r], tile=va_r)
        return va4, va_r

    def load_qk(self, m):
        P, ar, S = self.P, self.ar, self.S
        R = ROWS[m]
        QT, KT = [], []
        for j in range(2):
            h = 2 * m + j
            q_t, q_r = ar.alloc(S, BF16, f"QT{m}{j}")
            k_t, k_r = ar.alloc(S, BF16, f"KT{m}{j}")
            nsp = 4
            for i in range(nsp):
                sl = slice(i * S // nsp, (i + 1) * S // nsp)
                P.dma("sp", k_t[0:R, sl], self.KA[h, 0:R, sl], reads=[self.KA_r[h]], acc=[k_r], tile=k_r)
                P.dma("sp", q_t[0:R, sl], self.QA[h, 0:R, sl], reads=[self.QA_r[h]], acc=[q_r], tile=q_r)
            QT.append((q_t, q_r))
            KT.append((k_t, k_r))
        return QT, KT

    def phase_dense(self, m, yT_ap):
        P, ar, S, NT, NQ = self.P, self.ar, self.S, self.NT, self.NQ
        ar.reset()
        R = ROWS[m]
        QT, KT = self.load_qk(m)
        va4, va_r = self.load_vaug(m)
        NA = 4
        AT = [ar.alloc(512, BF16, f"AT{i}") for i in range(NA)]
        gts = [ar.alloc(512, BF16, f"gt{i}") for i in range(2)]
        ybs = [ar.alloc(512, BF16, f"yb{i}") for i in range(2)]
        y32s = [ar.alloc(512, F32, f"y32{i}") for i in range(2)]
        rdn = [ar.alloc(512, F32, f"rdn{i}") for i in range(2)]
        sb = [self.banks[i] for i in range(3)]
        pod = [[self.banks[3], self.banks[4]], [self.banks[5], self.banks[6]]]
        seq = []
        for qt in range(NQ):
            nkb = 4 * qt + 4
            for j in range(2):
                for kb in range(nkb):
                    seq.append((qt, j, kb, nkb))
        LOOK = 2
        n = len(seq)

        def stage_S(i):
            qt, j, kb, nkb = seq[i]
            s_t, s_r = sb[i % 3]
            a_t, a_r = AT[i % NA]
            q_t, q_r = QT[j]
            k_t, k_r = KT[j]
            d = kb - 4 * qt
            P.op("pe", lambda e: e.matmul(s_t, lhsT=k_t[0:R, kb * 128:(kb + 1) * 128], rhs=q_t[0:R, qt * 512:(qt + 1) * 512], start=True, stop=(d < 0)),
                 reads=[k_r, q_r], writes=[s_r])
            if d >= 0:
                P.op("pe", lambda e: e.matmul(s_t, lhsT=self.idb, rhs=self.cmb[:, d * 512:(d + 1) * 512], start=False, stop=True),
                     reads=[self.idb_r, self.cmb_r], acc=[s_r])
            if m == 3:
                P.op("act", lambda e: e.activation(out=a_t, in_=s_t, func=AF.Exp, bias=self.negF[:, 2 * kb + j:2 * kb + j + 1], scale=1.0),
                     reads=[s_r, self.negF_r], writes=[a_r])
            else:
                P.op("act", lambda e: e.activation(out=a_t, in_=s_t, func=AF.Exp), reads=[s_r], writes=[a_r])

        def stage_PV(i):
            qt, j, kb, nkb = seq[i]
            a_t, a_r = AT[i % NA]
            pb, pb_r = pod[qt % 2][j]
            P.op("pe", lambda e: e.matmul(pb, lhsT=va4[:, kb, j, :], rhs=a_t, start=(kb == 0), stop=(kb == nkb - 1)),
                 reads=[a_r, va_r], writes=[pb_r] if kb == 0 else (), acc=() if kb == 0 else [pb_r])
            if kb == nkb - 1:
                par = qt % 2
                gt_t, gt_r = gts[par]
                yb, yb_r = ybs[par]
                y32, y32_r = y32s[par]
                rd, rd_r = rdn[j]
                sl = slice(qt * 512, (qt + 1) * 512)
                if j == 0:
                    P.dma("sp", gt_t, self.GT[m, :, sl], reads=[self.GT_r], writes=[gt_r], tile=gt_r)
                P.op("dve", lambda e: e.reciprocal(out=rd[0:64, :], in_=pb[64:128, :]), reads=[pb_r], writes=[rd_r])
                P.op("dve", lambda e: e.tensor_tensor(out=y32[64 * j:64 * j + 64, :], in0=pb[0:64, :], in1=rd[0:64, :], op=ALU.mult),
                     reads=[pb_r, rd_r], writes=[y32_r] if j == 0 else (), acc=() if j == 0 else [y32_r])
                P.op("pool", lambda e: e.tensor_tensor(out=yb[64 * j:64 * j + 64, :], in0=y32[64 * j:64 * j + 64, :], in1=gt_t[64 * j:64 * j + 64, :], op=ALU.mult),
                     reads=[y32_r, gt_r], writes=[yb_r] if j == 0 else (), acc=() if j == 0 else [yb_r])
                if j == 1:
                    dst_ap = self.y_dst(m, qt) if getattr(self, "y_dst", None) is not None else yT_ap[m * 128:(m + 1) * 128, sl]
                    o = P.dma("pool", dst_ap, yb, reads=[yb_r], acc=[self.yT_r], tile=yb_r)
                    if self.yT_external:
                        P.finish(o)

        for i in range(n + LOOK):
            if i < n:
                stage_S(i)
            if i - LOOK >= 0:
                stage_PV(i - LOOK)
        P.barrier()

    def phase_B(self, yT_ap):
        P, ar, S, NT, NQ = self.P, self.ar, self.S, self.NT, self.NQ
        m = 1
        ar.reset()
        QT, KT = self.load_qk(m)
        v_t, v_r = ar.alloc(NT * 128, BF16, f"V{m}")
        v3 = v_t.rearrange("p (n c) -> p n c", n=NT)
        for i in range(4):
            n0, n1 = i * NT // 4, (i + 1) * NT // 4
            P.dma("sp", v3[:, n0:n1, :], self.V[n0 * 128:n1 * 128, m * 128:(m + 1) * 128].rearrange("(n p) c -> p n c", p=128),
                  reads=[self.V_r], acc=[v_r], tile=v_r)
        NB = 4
        AT = [ar.alloc(512, BF16, f"AT{i}") for i in range(NB)]
        E = [ar.alloc(512, F32, f"E{i}") for i in range(NB)]
        SP = [ar.alloc(512, F32, f"SP{i}") for i in range(NB)]
        SS = [[ar.alloc(512, F32, f"SS{j}{i}") for i in range(2)] for j in range(2)]
        gts = [ar.alloc(512, BF16, f"gt{i}") for i in range(2)]
        ybs = [ar.alloc(512, BF16, f"yb{i}") for i in range(2)]
        sb = [self.banks[i] for i in range(NB)]
        pos = [self.banks[4], self.banks[5]]
        cf, cf_r = self.cf, self.cf_r
        seq = []
        for qt in range(NQ):
            nkb = 4 * qt + 4
            for idx, kb in enumerate(range(nkb - 1, -1, -1)):
                for j in range(2):
                    seq.append((qt, j, kb, idx, nkb))
        n = len(seq)

        def st1(i):
            qt, j, kb, idx, nkb = seq[i]
            s_t, s_r = sb[i % NB]
            e_t, e_r = E[i % NB]
            p_t, p_r = SP[i % NB]
            q_t, q_r = QT[j]
            k_t, k_r = KT[j]
            d = kb - 4 * qt
            P.op("pe", lambda e: e.matmul(s_t, lhsT=k_t[0:64, kb * 128:(kb + 1) * 128], rhs=q_t[0:64, qt * 512:(qt + 1) * 512], start=True, stop=(d < 0)),
                 reads=[k_r, q_r], writes=[s_r])
            if d >= 0:
                P.op("pe", lambda e: e.matmul(s_t, lhsT=self.idb, rhs=self.cmbs[:, d * 512:(d + 1) * 512], start=False, stop=True),
                     reads=[self.idb_r, self.cmbs_r], acc=[s_r])
            P.op("act", lambda e: e.activation(out=e_t, in_=s_t, func=AF.Exp), reads=[s_r], writes=[e_r])
            P.op("act", lambda e: e.activation(out=p_t, in_=e_t, func=AF.Ln, bias=1.0, scale=1.0), reads=[e_r], writes=[p_r])

        def st2(i):
            qt, j, kb, idx, nkb = seq[i]
            s_t, s_r = sb[i % NB]
            p_t, p_r = SP[i % NB]
            a_t, a_r = AT[i % NB]
            so_t, so_r = SS[j][(idx + 1) % 2]
            sn_t, sn_r = SS[j][idx % 2]
            P.op("pe", lambda e: e.matmul(s_t, lhsT=cf[:, C_NTRI:C_NTRI + 128], rhs=p_t, start=False, stop=(idx == 0), skip_group_check=True),
                 reads=[cf_r, p_r], acc=[s_r])
            if idx > 0:
                P.op("pe", lambda e: e.matmul(s_t, lhsT=cf[:, C_NONES:C_NONES + 128], rhs=so_t, start=False, stop=True, skip_group_check=True),
                     reads=[cf_r, so_r], acc=[s_r])
            if kb > 0:
                if idx == 0:
                    P.op("pool", lambda e: e.tensor_copy(out=sn_t, in_=p_t), reads=[p_r], writes=[sn_r])
                else:
                    P.op("pool", lambda e: e.tensor_tensor(out=sn_t, in0=so_t, in1=p_t, op=ALU.add), reads=[so_r, p_r], writes=[sn_r])
            P.op("act", lambda e: e.activation(out=a_t, in_=s_t, func=AF.Exp), reads=[s_r], writes=[a_r])

        def st3(i):
            qt, j, kb, idx, nkb = seq[i]
            a_t, a_r = AT[i % NB]
            pO, pO_r = pos[qt % 2]
            first = (idx == 0 and j == 0)
            P.op("pe", lambda e: e.matmul(pO[64 * j:64 * j + 64, :], lhsT=v3[:, kb, 64 * j:64 * j + 64], rhs=a_t, start=(idx == 0), stop=(idx == nkb - 1)),
                 reads=[a_r, v_r], writes=[pO_r] if first else (), acc=() if first else [pO_r])
            if idx == nkb - 1 and j == 1:
                par = qt % 2
                gt_t, gt_r = gts[par]
                yb, yb_r = ybs[par]
                sl = slice(qt * 512, (qt + 1) * 512)
                P.dma("sp", gt_t, self.GT[m, :, sl], reads=[self.GT_r], writes=[gt_r], tile=gt_r)
                P.op("dve", lambda e: e.tensor_tensor(out=yb, in0=pO, in1=gt_t, op=ALU.mult), reads=[pO_r, gt_r], writes=[yb_r])
                dst_ap = self.y_dst(m, qt) if getattr(self, "y_dst", None) is not None else yT_ap[m * 128:(m + 1) * 128, sl]
                o = P.dma("pool", dst_ap, yb, reads=[yb_r], acc=[self.yT_r], tile=yb_r)
                if self.yT_external:
                    P.finish(o)

        for i in range(n + 2):
            if i < n:
                st1(i)
            if 0 <= i - 1 < n:
                st2(i - 1)
            if 0 <= i - 2 < n:
                st3(i - 2)
        P.barrier()

    def make_strict_mask(self):
        P, cst = self.P, self.cst
        self.cmbs, self.cmbs_r = cst.alloc(2048, BF16, "cmbs")
        cm3 = self.cmb.rearrange("p (d q) -> p d q", d=4)
        cs3 = self.cmbs.rearrange("p (d q) -> p d q", d=4)
        P.op("dve", lambda e: e.memset(self.cmbs, NEGM), writes=[self.cmbs_r])
        P.op("dve", lambda e: e.tensor_copy(out=cs3[:, :, 1:512], in_=cm3[:, :, 0:511]), reads=[self.cmb_r], writes=[self.cmbs_r])

    def phase_A(self, yT_ap):
        P, ar, S, NT, NQ = self.P, self.ar, self.S, self.NT, self.NQ
        m = 0
        ar.reset()
        QT, KT, (v3, v_r) = self.load_qkv(m)
        bufs = self.out_bufs()
        num, num_r = ar.alloc(S, F32, "numA")
        den, den_r = ar.alloc(S, F32, "denA")
        v_t2 = v3
        AT = [ar.alloc(256, BF16, f"ATa{i}") for i in range(3)]
        sb = [self.banks[i] for i in range(3)]
        pOs = [self.banks[3], self.banks[4]]
        pDs = [self.banks[5], self.banks[6]]
        it = 0
        grp = 0
        for dil in (1, 4, 16):
            L = S // dil
            nb = L // 128
            vt3, vt_r = v3, v_r
            if dil > 1:
                src = self.V[:, m * 128:(m + 1) * 128].rearrange("(n i r) c -> r i n c", i=128, r=dil)
                for r in range(dil):
                    P.dma("sp", v3[:, r * nb:(r + 1) * nb, :], src[r], reads=[self.V_r],
                          writes=[v_r] if r == 0 else (), acc=() if r == 0 else [v_r], tile=v_r)
            for r in range(dil):
                gs = min(4, nb)
                for n0 in range(0, nb, gs):
                    pO, pO_r = pOs[grp % 2]
                    pD, pD_r = pDs[grp % 2]
                    grp += 1
                    for j in range(2):
                        q_t, q_r = QT[j]
                        k_t, k_r = KT[j]
                        for n in range(n0, n0 + gs):
                            s_t, s_r = sb[it % 3]
                            a_t, a_r = AT[it % 3]
                            it += 1
                            def tok(bi, dil=dil, r=r):
                                return slice(r + dil * 128 * bi, r + dil * 128 * bi + dil * 127 + 1, dil)
                            c0 = 0 if n > 0 else 128
                            if n > 0:
                                P.op("pe", lambda e, s_t=s_t, k_t=k_t, q_t=q_t, n=n, tok=tok: e.matmul(s_t[:, 0:128], lhsT=k_t[0:64, tok(n - 1)], rhs=q_t[0:64, tok(n)], start=True, stop=False),
                                     reads=[k_r, q_r], writes=[s_r])
                            P.op("pe", lambda e, s_t=s_t, k_t=k_t, q_t=q_t, n=n, tok=tok: e.matmul(s_t[:, 128:256], lhsT=k_t[0:64, tok(n)], rhs=q_t[0:64, tok(n)], start=(n == 0), stop=False),
                                 reads=[k_r, q_r], writes=[s_r] if n == 0 else (), acc=() if n == 0 else [s_r])
                            P.op("pe", lambda e, s_t=s_t, c0=c0: e.matmul(s_t[:, c0:256], lhsT=self.idb, rhs=self.bmb[:, c0:256], start=False, stop=True),
                                 reads=[self.idb_r, self.bmb_r], acc=[s_r])
                            P.op("act", lambda e, s_t=s_t, a_t=a_t, c0=c0: e.activation(out=a_t[:, c0:256], in_=s_t[:, c0:256], func=AF.Exp), reads=[s_r], writes=[a_r])
                            cs_ = (n - n0) * 128
                            first = (j == 0 and n == n0)
                            kbs = ([(n - 1, 0)] if n > 0 else []) + [(n, 128)]
                            for ii, (kbi, ac) in enumerate(kbs):
                                P.op("pe", lambda e, a_t=a_t, kbi=kbi, ac=ac, j=j, cs_=cs_, ii=ii, nk=len(kbs), r=r, nb=nb, pO=pO, vt3=vt3: e.matmul(
                                        pO[64 * j:64 * j + 64, cs_:cs_ + 128], lhsT=vt3[:, r * nb + kbi, 64 * j:64 * j + 64], rhs=a_t[:, ac:ac + 128], start=(ii == 0), stop=(ii == nk - 1)),
                                     reads=[a_r, vt_r], writes=[pO_r] if (first and ii == 0) else (), acc=() if (first and ii == 0) else [pO_r])
                                P.op("pe", lambda e, a_t=a_t, ac=ac, j=j, cs_=cs_, ii=ii, nk=len(kbs), pD=pD: e.matmul(
                                        pD[64 * j:64 * j + 64, cs_:cs_ + 128], lhsT=self.oneb[:, 0:64], rhs=a_t[:, ac:ac + 128], start=(ii == 0), stop=(ii == nk - 1)),
                                     reads=[a_r, self.oneb_r], writes=[pD_r] if (first and ii == 0) else (), acc=() if (first and ii == 0) else [pD_r])
                    t0 = r + dil * 128 * n0
                    tsl = slice(t0, t0 + dil * (gs * 128 - 1) + 1, dil)
                    gw = gs * 128
                    if dil == 1:
                        P.op("dve", lambda e, pO=pO, tsl=tsl, gw=gw: e.tensor_copy(out=num[:, tsl], in_=pO[:, 0:gw]), reads=[pO_r], acc=[num_r])
                        P.op("act", lambda e, pD=pD, tsl=tsl, gw=gw: e.copy(out=den[:, tsl], in_=pD[:, 0:gw]), reads=[pD_r], acc=[den_r])
                    else:
                        P.op("dve", lambda e, pO=pO, tsl=tsl, gw=gw: e.tensor_tensor(out=num[:, tsl], in0=num[:, tsl], in1=pO[:, 0:gw], op=ALU.add), reads=[pO_r, num_r], acc=[num_r])
                        P.op("dve", lambda e, pD=pD, tsl=tsl, gw=gw: e.tensor_tensor(out=den[:, tsl], in0=den[:, tsl], in1=pD[:, 0:gw], op=ALU.add), reads=[pD_r, den_r], acc=[den_r])
        for qt in range(NQ):
            sl = slice(qt * 512, (qt + 1) * 512)
            self.finish_qtile(m, qt, num[:, sl], num_r, den[:, sl], den_r, yT_ap, bufs)
        P.barrier()


def _phase_O(lb, YG_ap, YG_r, wo_ap, x_ap, x_r, out_ap, out_r, tok0, ntok, out_row0, external_out):
    P, ar = lb.P, lb.ar
    ar.reset()
    Wb, Wb_r = ar.alloc(NCH * D_MODEL, BF16, "Wob")
    Wb3 = Wb.rearrange("p (c n) -> p c n", c=NCH)
    wst = [ar.alloc(D_MODEL, F32, f"wost{i}") for i in range(2)]
    for q in range(NCH):
        r, m = q // 4, q % 4
        wrow = (m * 4 + r) * 128
        ws, ws_r = wst[q % 2]
        P.dma("sp", ws, wo_ap[wrow:wrow + 128, :], writes=[ws_r], tile=ws_r)
        eng = "dve" if q % 2 == 0 else "pool"
        P.op(eng, lambda e, ws=ws, q=q: e.tensor_copy(out=Wb3[:, q, :], in_=ws), reads=[ws_r], acc=[Wb_r])
    yt = [ar.alloc(NCH * 128, BF16, f"oyt{i}") for i in range(2)]
    xs = [ar.alloc(D_MODEL, F32, f"oxs{i}") for i in range(2)]
    xos = [ar.alloc(D_MODEL, F32, f"oxo{i}") for i in range(2)]
    bi = 0
    for ti in range(ntok // 128):
        t0 = tok0 + ti * 128
        y_t, y_r = yt[ti % 2]
        x_t, xr_ = xs[ti % 2]
        o_t, o_r = xos[ti % 2]
        y3 = y_t.rearrange("p (c s) -> p c s", c=NCH)
        P.dma("sp", y3, YG_ap(t0).rearrange("(c p) s -> p c s", p=128), reads=[YG_r], writes=[y_r], tile=y_r)
        P.dma("sp", x_t, x_ap[t0:t0 + 128, :], reads=[x_r] if x_r is not None else (), writes=[xr_], tile=xr_)
        for cg in range(4):
            bk, bk_r = lb.banks[bi % 8]
            bi += 1
            for c in range(NCH):
                P.op("pe", lambda e, bk=bk, c=c, cg=cg, y3=y3: e.matmul(bk, lhsT=y3[:, c, :], rhs=Wb3[:, c, cg * 512:(cg + 1) * 512], start=(c == 0), stop=(c == NCH - 1)),
                     reads=[y_r, Wb_r], writes=[bk_r] if c == 0 else (), acc=() if c == 0 else [bk_r])
            P.op("dve", lambda e, bk=bk, cg=cg, o_t=o_t, x_t=x_t: e.tensor_tensor(out=o_t[:, cg * 512:(cg + 1) * 512], in0=x_t[:, cg * 512:(cg + 1) * 512], in1=bk, op=ALU.add),
                 reads=[bk_r, xr_], acc=[o_r])
        orow = out_row0 + ti * 128
        o = P.dma("pool", out_ap[orow:orow + 128, :], o_t, reads=[o_r], acc=[out_r] if out_r is not None else (), tile=o_r)
        if external_out:
            P.finish(o)
    P.barrier()


def build_fused_program(S, depth, groups=((0, 1, 2, 3), (4, 5, 6, 7))):
    nc = bass.Bass("TRN2", target_bir_lowering=False)
    x = nc.dram_tensor("x", [S, D_MODEL], F32, kind="ExternalInput").ap()
    wc = nc.dram_tensor("wc", [depth, D_MODEL, WC], F32, kind="ExternalInput").ap()
    ng = nc.dram_tensor("ng", [depth, 128, NCH], F32, kind="ExternalInput").ap()
    gqk = nc.dram_tensor("gqk", [depth, 2, 512], F32, kind="ExternalInput").ap()
    fb = nc.dram_tensor("fb", [depth, 1, 2], F32, kind="ExternalInput").ap()
    cs = nc.dram_tensor("cs", [S, 64], F32, kind="ExternalInput").ap()
    consts = nc.dram_tensor("consts", [128, C_TOT], F32, kind="ExternalInput").ap()
    wo = nc.dram_tensor("wo", [depth, D_MODEL, D_MODEL], F32, kind="ExternalInput").ap()
    xo = nc.dram_tensor("xo", [S, D_MODEL], F32, kind="ExternalOutput").ap()
    Xb = [nc.dram_tensor(f"Xbuf{i}", [S, D_MODEL], F32).ap() for i in range(2)]
    PART = 1024
    NP = S // PART
    YL = [[nc.dram_tensor(f"YL{i}_{p}", [512, PART], BF16).ap() for p in range(NP)] for i in range(2)]
    YG = [[nc.dram_tensor(f"YG{i}_{p}", [4 * 512, PART], BF16).ap() for p in range(NP)] for i in range(2)]
    P = Prog(nc)
    st = contextlib.ExitStack()
    with st:
        lb = LayerBuilder(nc, P, st, S)
        lb.yT_external = False
        lb.load_consts(consts)
        lb.make_strict_mask()
        Xb_r = [P.res("Xb0"), P.res("Xb1")]
        YL_r = [P.res("YL0"), P.res("YL1")]
        YG_r = [P.res("YG0"), P.res("YG1")]
        cc_r = P.res("cc")
        xo_r = P.res("xo")
        grp = [list(g_) for g_ in groups]
        import os as _os
        for l in range(depth):
            src, src_r = (x, None) if l == 0 else (Xb[(l - 1) % 2], Xb_r[(l - 1) % 2])
            yl, yl_r = YL[l % 2], YL_r[l % 2]
            yg, yg_r = YG[l % 2], YG_r[l % 2]
            lb.yT_r = yl_r
            lb.x_src_r = src_r
            lb.y_dst = lambda m, qt, yl=yl: yl[(qt * 512) // PART][m * 128:(m + 1) * 128, (qt * 512) % PART:(qt * 512) % PART + 512]
            lb.phase1(src, wc[l], ng[l], gqk[l], fb[l], cs)
            lb.phase_A(None)
            lb.phase_B(None)
            lb.phase_dense(2, None)
            lb.phase_dense(3, None)
            if not _os.environ.get("FUSED_NOCC"):
                for p in range(NP):
                    P.collective(lambda e, a=yl[p], b_=yg[p]: e.collective_compute("AllGather", ALU.bypass, replica_groups=grp, ins=[a], outs=[b_]),
                                 [yl_r], [yg_r] if p == 0 else [], cc_r)
                    if p > 0:
                        yg_r.w.append(P.ops["pool"][-1])
            last = (l == depth - 1)
            yg_fn = lambda t0, yg=yg: yg[t0 // PART][:, t0 % PART:t0 % PART + 128]
            if last:
                _phase_O(lb, yg_fn, yg_r, wo[l], src, src_r, xo, xo_r, 0, S, 0, True)
            else:
                _phase_O(lb, yg_fn, yg_r, wo[l], src, src_r, Xb[l % 2], Xb_r[l % 2], 0, S, 0, False)
        P.emit()
    return nc, P


def build_layer_program(S, phases="1ABCD"):
    nc = bass.Bass("TRN2", target_bir_lowering=False)
    x = nc.dram_tensor("x", [S, D_MODEL], F32, kind="ExternalInput").ap()
    wc = nc.dram_tensor("wc", [D_MODEL, WC], F32, kind="ExternalInput").ap()
    ng = nc.dram_tensor("ng", [128, NCH], F32, kind="ExternalInput").ap()
    gqk = nc.dram_tensor("gqk", [2, 512], F32, kind="ExternalInput").ap()
    fb = nc.dram_tensor("fb", [1, 2], F32, kind="ExternalInput").ap()
    cs = nc.dram_tensor("cs", [S, 64], F32, kind="ExternalInput").ap()
    consts = nc.dram_tensor("consts", [128, C_TOT], F32, kind="ExternalInput").ap()
    yT = nc.dram_tensor("yT", [512, S], BF16, kind="ExternalOutput").ap()
    P = Prog(nc)
    st = contextlib.ExitStack()
    with st:
        lb = LayerBuilder(nc, P, st, S)
        lb.load_consts(consts)
        lb.make_strict_mask()
        if "1" in phases:
            lb.phase1(x, wc, ng, gqk, fb, cs)
        if "A" in phases:
            lb.phase_A(yT)
        if "B" in phases:
            lb.phase_B(yT)
        if "C" in phases:
            lb.phase_dense(2, yT)
        if "D" in phases:
            lb.phase_dense(3, yT)
        P.emit()
    return nc, P


def build_out_program(T):
    nc = bass.Bass("TRN2", target_bir_lowering=False)
    yT = nc.dram_tensor("yT", [D_MODEL, T], BF16, kind="ExternalInput").ap()
    wo = nc.dram_tensor("wo", [D_MODEL, D_MODEL], F32, kind="ExternalInput").ap()
    x = nc.dram_tensor("x", [T, D_MODEL], F32, kind="ExternalInput").ap()
    xo = nc.dram_tensor("xo", [T, D_MODEL], F32, kind="ExternalOutput").ap()
    P = Prog(nc)
    st = contextlib.ExitStack()
    with st:
        ar = Arena(nc, st, P, "arena", 160 * 1024)
        banks = []
        for i in range(8):
            t = st.enter_context(nc.psum_tensor(f"bank{i}", [128, 512], F32))
            banks.append((t[:, :], P.res(f"bank{i}")))
        Wb, Wb_r = ar.alloc(NCH * D_MODEL, BF16, "Wob")
        Wb3 = Wb.rearrange("p (c n) -> p c n", c=NCH)
        wst = [ar.alloc(D_MODEL, F32, f"wst{i}") for i in range(2)]
        for c in range(NCH):
            ws, ws_r = wst[c % 2]
            P.dma("sp", ws, wo[c * 128:(c + 1) * 128, :], writes=[ws_r], tile=ws_r)
            eng = "dve" if c % 2 == 0 else "pool"
            P.op(eng, lambda e, ws=ws, c=c: e.tensor_copy(out=Wb3[:, c, :], in_=ws), reads=[ws_r], acc=[Wb_r])
        yt = [ar.alloc(NCH * 128, BF16, f"yt{i}") for i in range(2)]
        xs = [ar.alloc(D_MODEL, F32, f"xs{i}") for i in range(2)]
        xos = [ar.alloc(D_MODEL, F32, f"xo{i}") for i in range(2)]
        NT = T // 128
        bi = 0
        for t in range(NT):
            y_t, y_r = yt[t % 2]
            x_t, x_r = xs[t % 2]
            o_t, o_r = xos[t % 2]
            y3 = y_t.rearrange("p (c s) -> p c s", c=NCH)
            P.dma("sp", y3, yT[:, t * 128:(t + 1) * 128].rearrange("(c p) s -> p c s", p=128), writes=[y_r], tile=y_r)
            P.dma("sp", x_t, x[t * 128:(t + 1) * 128, :], writes=[x_r], tile=x_r)
            for cg in range(4):
                bk, bk_r = banks[bi % 8]
                bi += 1
                for c in range(NCH):
                    P.op("pe", lambda e, bk=bk, c=c, cg=cg, y3=y3: e.matmul(bk, lhsT=y3[:, c, :], rhs=Wb3[:, c, cg * 512:(cg + 1) * 512], start=(c == 0), stop=(c == NCH - 1)),
                         reads=[y_r, Wb_r], writes=[bk_r] if c == 0 else (), acc=() if c == 0 else [bk_r])
                P.op("dve", lambda e, bk=bk, cg=cg, o_t=o_t, x_t=x_t: e.tensor_tensor(out=o_t[:, cg * 512:(cg + 1) * 512], in0=x_t[:, cg * 512:(cg + 1) * 512], in1=bk, op=ALU.add),
                     reads=[bk_r, x_r], acc=[o_r])
            o = P.dma("pool", xo[t * 128:(t + 1) * 128, :], o_t, reads=[o_r], tile=o_r)
            P.finish(o)
        P.emit()
    return nc, P


_CACHE = {}


def _layer_prog(S):
    if ("L", S) not in _CACHE:
        _CACHE[("L", S)] = build_layer_program(S)[0]
    return _CACHE[("L", S)]


def _out_prog(T):
    if ("O", T) not in _CACHE:
        _CACHE[("O", T)] = build_out_program(T)[0]
    return _CACHE[("O", T)]


def pack_layer_inputs(x_b, norm_gain_l, w_in_l, qn_l, kn_l, fb_l, g, cs, consts):
    cols = []
    for t in range(4):
        for m in range(4):
            c0 = t * 2048 + m * 512 + g * 128
            cols.append(np.arange(c0, c0 + 128))
    cols.append(np.array([8192 + 2 * g, 8192 + 2 * g + 1]))
    cols = np.concatenate(cols)
    wc = np.ascontiguousarray(w_in_l[:, cols])
    ng = np.ascontiguousarray(norm_gain_l.reshape(NCH, 128).T)
    one = np.ones(64, np.float32)
    gq = np.concatenate([qn_l[0], qn_l[0], one, one, qn_l[1], qn_l[1], qn_l[2], qn_l[2]])
    gk = np.concatenate([kn_l[0], kn_l[0], one, one, kn_l[1], kn_l[1], kn_l[2], kn_l[2]])
    gqk = np.ascontiguousarray(np.stack([gq, gk]).astype(np.float32))
    fb = np.ascontiguousarray(fb_l[2 * g:2 * g + 2].reshape(1, 2).astype(np.float32))
    return {"x": x_b, "wc": wc, "ng": ng, "gqk": gqk, "fb": fb, "cs": cs, "consts": consts}


def pack_fused_inputs(x_b, norm_gain, w_in, qn, kn, fbias, w_out, g, cs, consts):
    depth = norm_gain.shape[0]
    per = [pack_layer_inputs(None, norm_gain[l], w_in[l], qn[l], kn[l], fbias[l], g, cs, consts) for l in range(depth)]
    return {"x": x_b,
            "wc": np.ascontiguousarray(np.stack([p["wc"] for p in per])),
            "ng": np.ascontiguousarray(np.stack([p["ng"] for p in per])),
            "gqk": np.ascontiguousarray(np.stack([p["gqk"] for p in per])),
            "fb": np.ascontiguousarray(np.stack([p["fb"] for p in per])),
            "cs": cs, "consts": consts, "wo": w_out}


def _fused_prog(S, depth):
    if ("F", S, depth) not in _CACHE:
        _CACHE[("F", S, depth)] = build_fused_program(S, depth)[0]
    return _CACHE[("F", S, depth)]


def kernel(x, norm_gain, w_in, q_norm_gain, k_norm_gain, forget_bias, w_out):
    x = np.ascontiguousarray(np.asarray(x, dtype=np.float32))
    B, S, D = x.shape
    depth = norm_gain.shape[0]
    norm_gain = np.asarray(norm_gain, np.float32)
    w_in = np.asarray(w_in, np.float32)
    q_norm_gain = np.asarray(q_norm_gain, np.float32)
    k_norm_gain = np.asarray(k_norm_gain, np.float32)
    forget_bias = np.asarray(forget_bias, np.float32)
    w_out = np.ascontiguousarray(np.asarray(w_out, np.float32))
    cs = rope_table(S)
    consts = make_consts()
    assert B == 2
    nc = _fused_prog(S, depth)
    in_maps = []
    for c in range(8):
        b, g = c // 4, c % 4
        in_maps.append(pack_fused_inputs(x[b], norm_gain, w_in, q_norm_gain, k_norm_gain, forget_bias, w_out, g, cs, consts))
    res = run_bass_kernel_spmd(nc, in_maps, core_ids=list(range(8)))
    out = np.stack([np.asarray(res.results[0]["xo"]), np.asarray(res.results[4]["xo"])], axis=0)
    return out.astype(np.float32)


def kernel_unfused(x, norm_gain, w_in, q_norm_gain, k_norm_gain, forget_bias, w_out):
    x = np.ascontiguousarray(np.asarray(x, dtype=np.float32))
    B, S, D = x.shape
    depth = norm_gain.shape[0]
    norm_gain = np.asarray(norm_gain, np.float32)
    w_in = np.asarray(w_in, np.float32)
    q_norm_gain = np.asarray(q_norm_gain, np.float32)
    k_norm_gain = np.asarray(k_norm_gain, np.float32)
    forget_bias = np.asarray(forget_bias, np.float32)
    w_out = np.asarray(w_out, np.float32)
    cs = rope_table(S)
    consts = make_consts()
    ncores = 8
    T = B * S // ncores
    cur = x
    for l in range(depth):
        ncL = _layer_prog(S)
        in_maps = []
        for c in range(ncores):
            b, g = c // 4, c % 4
            in_maps.append(pack_layer_inputs(cur[b], norm_gain[l], w_in[l], q_norm_gain[l], k_norm_gain[l], forget_bias[l], g, cs, consts))
        res = run_bass_kernel_spmd(ncL, in_maps, core_ids=list(range(ncores)))
        yT_full = []
        for b in range(B):
            yb = np.zeros((4, 8, 64, S), dtype=ml_dtypes.bfloat16)
            for g in range(4):
                y = np.asarray(res.results[b * 4 + g]["yT"]).reshape(4, 2, 64, S)
                yb[:, 2 * g:2 * g + 2] = y
            yT_full.append(yb.reshape(2048, S))
        yT_cat = np.concatenate(yT_full, axis=1)
        xf = cur.reshape(B * S, D)
        ncO = _out_prog(T)
        in_maps = []
        for c in range(ncores):
            in_maps.append({"yT": np.ascontiguousarray(yT_cat[:, c * T:(c + 1) * T]), "wo": w_out[l],
                            "x": np.ascontiguousarray(xf[c * T:(c + 1) * T])})
        res = run_bass_kernel_spmd(ncO, in_maps, core_ids=list(range(ncores)))
        cur = np.concatenate([np.asarray(r["xo"]) for r in res.results], axis=0).reshape(B, S, D)
    return cur.astype(np.float32)
```
